# Optimizing a Trainium2 kernel written in Bass

```python
import math
import jax, jax.numpy as jnp
from jax import lax
import numpy as np

D_MODEL = 2048
BATCH = 4
SEQ = 8192
DEPTH = 1
DEC_BATCH = 2
DEC_SEQ = 4096
PAST_LEN = 128

N_MEM = 256
DIL_PAIRS = ((128, 1), (512, 4), (2048, 16))
N_DIL = 3
A_HEADS = 4
A_HD = 128
A_WIDTH = A_HEADS * A_HD
B_HEADS = 8
Q_LORA = 512
KV_LORA = 512
D_NOPE = 128
D_ROPE = 64
D_V = 128
B_WIDTH = B_HEADS * D_V
ROPE_BASE = 10000.0
Q_BLOCK = 128
C_HEADS = 4
C_HD = 128
C_WIDTH = C_HEADS * C_HD
D_MIX = A_WIDTH + B_WIDTH + C_WIDTH
N_BUCKETS = 32
MAX_DISTANCE = 1024
EPS = 1e-6
NEG = -1e30
IN_SIZES = (N_DIL * A_WIDTH, A_WIDTH, A_WIDTH, A_WIDTH,
            Q_LORA, KV_LORA, D_ROPE, B_WIDTH,
            C_WIDTH, C_WIDTH)
D_IN = sum(IN_SIZES)

kernel_name = 'hybrid_dilated_mla_memory_encoder'


def rmsnorm(x, g):
    xf = x.astype(jnp.float32)
    y = xf * lax.rsqrt(jnp.mean(xf * xf, axis=-1, keepdims=True) + EPS)
    return (y * g.astype(jnp.float32)).astype(x.dtype)


def t5_bucket(rel):
    nb = N_BUCKETS // 2
    max_exact = nb // 2
    bucket = jnp.where(rel > 0, nb, 0)
    n = jnp.abs(rel)
    nf = jnp.maximum(n, 1).astype(jnp.float32)
    large = max_exact + (jnp.log(nf / max_exact) / math.log(MAX_DISTANCE / max_exact)
                         * (nb - max_exact)).astype(jnp.int32)
    large = jnp.minimum(large, nb - 1)
    return bucket + jnp.where(n < max_exact, n, large)


def dilated_attention(q, k, v, bias_table, window, dil):
    Bn, S, H, E = q.shape
    blk = window // (2 * dil)
    L = S // dil
    nb = -(-L // blk)
    Lp = nb * blk

    def to_res(t):
        return t.reshape(Bn, L, dil, H, E).transpose(0, 2, 1, 3, 4)

    qr = jnp.pad(to_res(q), ((0, 0), (0, 0), (0, Lp - L), (0, 0), (0, 0))).reshape(Bn, dil, nb, blk, H, E)

    def windows(t):
        tp = jnp.pad(to_res(t), ((0, 0), (0, 0), (blk, Lp - L + blk), (0, 0), (0, 0)))
        tp = tp.reshape(Bn, dil, nb + 2, blk, H, E)
        return jnp.concatenate([tp[:, :, 0:nb], tp[:, :, 1:nb + 1], tp[:, :, 2:nb + 2]], axis=3)

    kw, vw = windows(k), windows(v)
    qi = jnp.arange(blk)[:, None]
    kj = jnp.arange(3 * blk)[None, :]
    rel = kj - blk - qi
    bias = bias_table.astype(jnp.float32)[t5_bucket(rel * dil)].transpose(2, 0, 1)
    key_l = jnp.arange(nb)[:, None] * blk - blk + kj
    valid = (jnp.abs(rel)[None] <= blk) & ((key_l >= 0) & (key_l < L))[:, None, :]
    s = jnp.einsum('bdnqhe,bdnkhe->bdnhqk', qr, kw, preferred_element_type=jnp.float32) * (E ** -0.5)
    s = jnp.where(valid[None, None, :, None], s + bias[None, None, None], NEG)
    m = jnp.max(s, axis=-1, keepdims=True)
    p = jnp.exp(s - m)
    den = jnp.sum(p, axis=-1, keepdims=True)
    o = jnp.einsum('bdnhqk,bdnkhe->bdnqhe', (p / den).astype(v.dtype), vw)
    lse = (m + jnp.log(den))[..., 0].swapaxes(3, 4)

    def from_res(t):
        t = t.reshape(Bn, dil, Lp, *t.shape[4:])[:, :, :L]
        return t.swapaxes(1, 2).reshape(Bn, S, *t.shape[3:])

    return from_res(o), from_res(lse)


def apply_rope(t):
    S = t.shape[1]
    inv = ROPE_BASE ** (-jnp.arange(0, D_ROPE, 2, dtype=jnp.float32) / D_ROPE)
    ang = jnp.arange(S, dtype=jnp.float32)[:, None] * inv[None, :]
    shape = (1, S) + (1,) * (t.ndim - 3) + (D_ROPE // 2,)
    cos, sin = jnp.cos(ang).reshape(shape), jnp.sin(ang).reshape(shape)
    tf = t.astype(jnp.float32)
    t1, t2 = tf[..., :D_ROPE // 2], tf[..., D_ROPE // 2:]
    return jnp.concatenate([t1 * cos - t2 * sin, t1 * sin + t2 * cos], axis=-1).astype(t.dtype)


def mla_attention(q_nope, q_rope, k_nope, k_rope, v):
    Bn, S, H, _ = q_nope.shape
    nq = S // Q_BLOCK
    scale = (D_NOPE + D_ROPE) ** -0.5

    def blocks(t):
        return t.reshape(Bn, nq, Q_BLOCK, *t.shape[2:]).swapaxes(0, 1)

    def one_block(args):
        qn, qr = args
        s = (jnp.einsum('bqhe,bkhe->bhqk', qn, k_nope, preferred_element_type=jnp.float32)
             + jnp.einsum('bqhe,bke->bhqk', qr, k_rope, preferred_element_type=jnp.float32)) * scale
        p = jax.nn.softmax(s, axis=-1)
        return jnp.einsum('bhqk,bkhe->bqhe', p.astype(v.dtype), v)

    o = lax.map(one_block, (blocks(q_nope), blocks(q_rope)))
    return o.swapaxes(0, 1).reshape(Bn, S, H, D_V)


def memory_attention(q, mem, mem_gain, w_mem_kv):
    Bn, S = q.shape[:2]
    M = mem.shape[1]
    mkv = (rmsnorm(mem, mem_gain) @ w_mem_kv).reshape(Bn, M, 2, C_HEADS, C_HD)
    mk, mv = mkv[:, :, 0], mkv[:, :, 1]
    s = jnp.einsum('bshe,bmhe->bhsm', q, mk, preferred_element_type=jnp.float32) * (C_HD ** -0.5)
    p = jax.nn.softmax(s, axis=-1)
    return jnp.einsum('bhsm,bmhe->bshe', p.astype(mv.dtype), mv).reshape(Bn, S, C_WIDTH)


def hybrid_layer(x, mem, pre_gain, w_in, q_gain, w_uq, kv_gain, w_ukv, mem_gain, w_mem_kv,
                 w_out, post_gain, rel_bias):
    Bn, S, _ = x.shape
    h = rmsnorm(x, pre_gain)
    offsets = [int(v) for v in np.cumsum(IN_SIZES)[:-1]]
    a_q, a_k, a_v, a_g, b_cq, b_ckv, b_kr, b_g, c_q, c_g = jnp.split(h @ w_in, offsets, axis=-1)

    qa = a_q.reshape(Bn, S, N_DIL, A_HEADS, A_HD)
    ka = a_k.reshape(Bn, S, A_HEADS, A_HD)
    va = a_v.reshape(Bn, S, A_HEADS, A_HD)
    outs, lses = [], []
    for g, (win, dil) in enumerate(DIL_PAIRS):
        o, l = dilated_attention(qa[:, :, g], ka, va, rel_bias[:, g * A_HEADS:(g + 1) * A_HEADS], win, dil)
        outs.append(o)
        lses.append(l)
    alpha = jax.nn.softmax(jnp.stack(lses, axis=0), axis=0)
    oa = jnp.einsum('gbsh,gbshe->bshe', alpha, jnp.stack(outs, axis=0).astype(jnp.float32))
    ya = oa.reshape(Bn, S, A_WIDTH).astype(x.dtype) * jax.nn.silu(a_g)

    qb = (rmsnorm(b_cq, q_gain) @ w_uq).reshape(Bn, S, B_HEADS, D_NOPE + D_ROPE)
    q_nope, q_rope = qb[..., :D_NOPE], apply_rope(qb[..., D_NOPE:])
    kvb = (rmsnorm(b_ckv, kv_gain) @ w_ukv).reshape(Bn, S, B_HEADS, D_NOPE + D_V)
    k_nope, vb = kvb[..., :D_NOPE], kvb[..., D_NOPE:]
    k_rope = apply_rope(b_kr)
    yb = mla_attention(q_nope, q_rope, k_nope, k_rope, vb).reshape(Bn, S, B_WIDTH) * jax.nn.silu(b_g)

    yc = memory_attention(c_q.reshape(Bn, S, C_HEADS, C_HD), mem, mem_gain, w_mem_kv) * jax.nn.silu(c_g)

    y = jnp.concatenate([ya, yb, yc], axis=-1) @ w_out
    return x + rmsnorm(y, post_gain)


def setup_inputs(seed: int = 0) -> dict:
    key = jax.random.key(seed)
    ks = jax.random.split(key, 16)

    def w(k, shape, fan_in):
        return jax.random.normal(k, shape, jnp.float32) * (fan_in ** -0.5)

    def gain(k, dim):
        return 1.0 + 0.01 * jax.random.normal(k, (DEPTH, dim), jnp.float32)

    return {
        'x_prompt': jax.random.normal(ks[0], (BATCH, SEQ, D_MODEL), jnp.float32),
        'x_sample': jax.random.normal(ks[1], (DEC_BATCH, DEC_SEQ, D_MODEL), jnp.float32),
        'mem_prompt': jax.random.normal(ks[2], (BATCH, N_MEM, D_MODEL), jnp.float32),
        'mem_sample': jax.random.normal(ks[3], (DEC_BATCH, N_MEM, D_MODEL), jnp.float32),
        'pre_gain': gain(ks[4], D_MODEL),
        'w_in': w(ks[5], (DEPTH, D_MODEL, D_IN), D_MODEL),
        'q_gain': gain(ks[6], Q_LORA),
        'w_uq': w(ks[7], (DEPTH, Q_LORA, B_HEADS * (D_NOPE + D_ROPE)), Q_LORA),
        'kv_gain': gain(ks[8], KV_LORA),
        'w_ukv': w(ks[9], (DEPTH, KV_LORA, B_HEADS * (D_NOPE + D_V)), KV_LORA),
        'mem_gain': gain(ks[10], D_MODEL),
        'w_mem_kv': w(ks[11], (DEPTH, D_MODEL, 2 * C_WIDTH), D_MODEL),
        'w_out': w(ks[12], (DEPTH, D_MIX, D_MODEL), D_MIX),
        'post_gain': gain(ks[13], D_MODEL),
        'rel_bias': 0.5 * jax.random.normal(ks[14], (N_BUCKETS, N_DIL * A_HEADS), jnp.float32),
    }


def reference(x_prompt, x_sample, mem_prompt, mem_sample, pre_gain, w_in, q_gain, w_uq, kv_gain, w_ukv,
              mem_gain, w_mem_kv, w_out, post_gain, rel_bias):
    def trunk(x, mem):
        for l in range(DEPTH):
            x = hybrid_layer(x, mem, pre_gain[l], w_in[l], q_gain[l], w_uq[l], kv_gain[l], w_ukv[l],
                             mem_gain[l], w_mem_kv[l], w_out[l], post_gain[l], rel_bias)
        return x

    y_prompt = trunk(x_prompt, mem_prompt)
    y_sample = trunk(x_sample, mem_sample)
    return (y_prompt, y_sample)
```

```python
import contextlib
import math
import numpy as np
import concourse.bass as bass
import concourse.mybir as mybir
from concourse.bass_utils import run_bass_kernel_spmd

F32 = mybir.dt.float32
BF16 = mybir.dt.bfloat16
AF = mybir.ActivationFunctionType
ALU = mybir.AluOpType

D = 2048
DIN = 6208
HALO = 1024
DILS = (1, 4, 16)
NCORES = 8
JOBS = (dict(name="p", S=8192, Tq=4096, per_seq=2), dict(name="s", S=4096, Tq=1024, per_seq=4))
NEG = -1e30


class Buf:
    __slots__ = ("w", "r")

    def __init__(self):
        self.w = None
        self.r = []


class Sched:
    def __init__(self, nc, stack):
        self.nc = nc
        self.eng = {"pe": nc.tensor, "act": nc.scalar, "dve": nc.vector, "pool": nc.gpsimd, "sp": nc.sync}
        self.sem = {}
        self.cnt = {}
        self.pending = {}
        for k in ("pe", "act", "dve", "pool"):
            self.sem[k] = stack.enter_context(nc.semaphore("s_" + k))
            self.cnt[k] = 0
            self.pending[k] = False
        self.dq = {}
        for k, n in (("sp", 16), ("pool", 8), ("act", 8)):
            sems = [stack.enter_context(nc.semaphore(f"d_{k}{i}")) for i in range(n)]
            self.dq[k] = dict(sems=sems, n=0)
        self.waited = {k: {} for k in self.eng}
        self.ninst = 0

    def _need(self, e, tok, out):
        if tok is None:
            return
        key, sem, val, src = tok
        if src == "pe" and e == "pe":
            return
        if self.waited[e].get(key, 0) >= val:
            return
        cur = out.get(key)
        if cur is None or cur[1] < val:
            out[key] = (sem, val)

    def _deps(self, e, reads, writes):
        need = {}
        for b in reads:
            self._need(e, b.w, need)
        for b in writes:
            self._need(e, b.w, need)
            for t in b.r:
                self._need(e, t, need)
        return need

    def _commit(self, tok, reads, writes):
        for b in reads:
            b.r.append(tok)
            if len(b.r) > 48:
                best = {}
                for t in b.r:
                    if t[0] not in best or best[t[0]][2] < t[2]:
                        best[t[0]] = t
                b.r = list(best.values())
        for b in writes:
            b.w = tok
            b.r = []

    def op(self, e, fn, reads=(), writes=(), inc=True):
        need = self._deps(e, reads, writes)
        eng = self.eng[e]
        for key, (sem, val) in need.items():
            self.waited[e][key] = val
            eng.wait_ge(sem, val)
        self.ninst += 1
        sem = self.sem[e]
        if inc:
            self.cnt[e] += 1
            val = self.cnt[e]
            fn(eng).then_inc(sem, 1)
            self.pending[e] = False
        else:
            val = self.cnt[e] + 1
            fn(eng)
            self.pending[e] = True
        tok = ("c_" + e, sem, val, e)
        self._commit(tok, reads, writes)
        return tok

    def dma(self, q, out, in_, reads=(), writes=(), **kw):
        need = self._deps(q, reads, writes)
        d = self.dq[q]
        n = d["n"]
        d["n"] += 1
        K = len(d["sems"])
        slot = n % K
        sem = d["sems"][slot]
        key = f"d_{q}{slot}"
        prev = 16 * (n // K)
        val = prev + 16
        if prev > 0 and self.waited[q].get(key, 0) < prev:
            need[key] = (sem, prev)
        eng = self.eng[q]
        for k2, (s, v) in need.items():
            self.waited[q][k2] = v
            eng.wait_ge(s, v)
        self.ninst += 1
        eng.dma_start(out=out, in_=in_, **kw).then_inc(sem, 16)
        tok = (key, sem, val, None)
        self._commit(tok, reads, writes)
        return tok

    def barrier(self):
        for k in self.pending:
            assert not self.pending[k], k
        toks = []
        for k in ("pe", "act", "dve", "pool"):
            if self.cnt[k] > 0:
                toks.append(("c_" + k, self.sem[k], self.cnt[k]))
        for q, d in self.dq.items():
            K = len(d["sems"])
            for slot in range(K):
                cnt = (d["n"] - slot + K - 1) // K if d["n"] > slot else 0
                if cnt > 0:
                    toks.append((f"d_{q}{slot}", d["sems"][slot], 16 * cnt))
        for e, eng in self.eng.items():
            for key, sem, val in toks:
                if key == "c_" + e:
                    continue
                if self.waited[e].get(key, 0) < val:
                    self.waited[e][key] = val
                    eng.wait_ge(sem, val)


def build_program(jobs=JOBS, debug=()):
    nc = bass.Bass("TRN2", target_bir_lowering=False)

    def din(name, shape, dt=F32):
        return nc.dram_tensor(name, list(shape), dt, kind="ExternalInput").ap()

    def dscr(name, shape, dt=BF16):
        kind = "ExternalOutput" if name in debug else "Internal"
        return nc.dram_tensor(name, list(shape), dt, kind=kind).ap()

    w_in = din("w_in", [D, DIN])
    w_uq = din("w_uq", [512, 1536])
    w_ukv = din("w_ukv", [512, 2048])
    w_mem = din("w_mem_kv", [D, 1024])
    w_out = din("w_out", [D, D])
    gains = din("gains", [128, 40])
    pgain = din("pgain", [128, D])
    ident = din("ident", [128, 128])

    WI = dscr("WI", [D, 6272])
    WQ = dscr("WQ", [512, 2048])
    WKV = dscr("WKV", [512, 2048])
    WM = dscr("WM", [D, 1024])
    WO = dscr("WO", [D, D])

    jio = []
    for jb in jobs:
        n, S_, Tq = jb["name"], jb["S"], jb["Tq"]
        Tw = Tq + 2 * HALO
        jio.append(dict(
            xkv=din(f"xkv_{n}", [S_, D]), mem=din(f"mem_{n}", [256, D]),
            ropeq=din(f"ropeq_{n}", [2, 64, Tq]), ropek=din(f"ropek_{n}", [2, 64, S_]),
            bias=din(f"bias_{n}", [128, 48, 128]),
            y=nc.dram_tensor(f"y_{n}", [Tq, D], F32, kind="ExternalOutput").ap(),
            QA=dscr(f"QA_{n}", [12, 128, Tq]), KA=dscr(f"KA_{n}", [4, 128, Tw]), VA=dscr(f"VA_{n}", [Tw, 512]),
            GA=dscr(f"GA_{n}", [4, 128, Tq]), QN=dscr(f"QN_{n}", [8, 128, Tq]), QR=dscr(f"QR_{n}", [8, 64, Tq]),
            KN=dscr(f"KN_{n}", [8, 128, S_]), KR=dscr(f"KR_{n}", [64, S_]), VB=dscr(f"VB_{n}", [S_, 1024]),
            GB=dscr(f"GB_{n}", [8, 128, Tq]), QC=dscr(f"QC_{n}", [4, 128, Tq]), GC=dscr(f"GC_{n}", [4, 128, Tq]),
            YT=dscr(f"YT_{n}", [16, 128, Tq]),
        ))

    with contextlib.ExitStack() as gst:
        S = Sched(nc, gst)

        uid = [0]

        def mk_alloc(st):
            def T(name, shape, dt):
                uid[0] += 1
                return st.enter_context(nc.sbuf_tensor(f"{name}_{uid[0]}", list(shape), dt))

            def P(name, shape, dt):
                uid[0] += 1
                return st.enter_context(nc.psum_tensor(f"{name}_{uid[0]}", list(shape), dt))
            return T, P

        GT, _ = mk_alloc(gst)
        identb = GT("identb", [128, 128], BF16); b_id = Buf()
        ones_f = GT("ones_f", [128, 128], F32); b_1f = Buf()
        ones_b = GT("ones_b", [128, 128], BF16); b_1b = Buf()
        epsb = GT("epsb", [128, 1], F32); b_eps = Buf()
        S.dma("pool", identb[:], ident, writes=[b_id])
        S.op("pool", lambda e: e.memset(ones_f[:], 1.0), writes=[b_1f])
        S.op("pool", lambda e: e.memset(ones_b[:], 1.0), writes=[b_1b])
        S.op("pool", lambda e: e.memset(epsb[:], 1e-6), writes=[b_eps])

        with contextlib.ExitStack() as st:
            T, P = mk_alloc(st)
            g = T("g", [128, 40], F32); b_g = Buf()
            S.dma("sp", g[:], gains, writes=[b_g])
            wf = [T(f"wf{i}", [128, DIN], F32) for i in range(2)]
            wb = [T(f"wb{i}", [128, 6272], BF16) for i in range(2)]
            b_wf = [Buf(), Buf()]
            b_wb = [Buf(), Buf()]
            cnt = [0]

            def conv(src, dst, nrows, nin, nout, gcol, copies):
                for kc in range(nrows // 128):
                    i = cnt[0] % 2
                    cnt[0] += 1
                    S.dma("sp", wf[i][:, 0:nin], src[kc * 128:(kc + 1) * 128, :], writes=[b_wf[i]])
                    for (dfn, sfn) in copies:
                        if gcol is None:
                            S.op("act", lambda e, i=i, dfn=dfn, sfn=sfn: e.activation(dfn(wb[i]), sfn(wf[i]), AF.Copy),
                                 reads=[b_wf[i]], writes=[b_wb[i]])
                        else:
                            S.op("act", lambda e, i=i, dfn=dfn, sfn=sfn, c=gcol + kc: e.activation(
                                dfn(wb[i]), sfn(wf[i]), AF.Copy, scale=g[:, c:c + 1]),
                                reads=[b_wf[i], b_g], writes=[b_wb[i]])
                    S.dma("pool", dst[kc * 128:(kc + 1) * 128, :], wb[i][:, 0:nout], reads=[b_wb[i]])

            conv(w_in, WI, D, DIN, 6272, 0, [
                (lambda t: t[:, 0:DIN], lambda t: t[:, 0:DIN]),
                (lambda t: t[:, 6208:6240], lambda t: t[:, 4128:4160]),
                (lambda t: t[:, 6240:6272], lambda t: t[:, 4096:4128]),
            ])
            qv = lambda t: t[:, 0:1536].rearrange("p (h f) -> p h f", f=192)
            conv(w_uq, WQ, 512, 1536, 2048, 16, [
                (lambda t: t[:, 0:1024].rearrange("p (h e) -> p h e", e=128), lambda t: qv(t)[:, :, 0:128]),
                (lambda t: t[:, 1024:1536].rearrange("p (h e) -> p h e", e=64), lambda t: qv(t)[:, :, 128:192]),
                (lambda t: t[:, 1536:2048].rearrange("p (h e) -> p h e", e=64)[:, :, 0:32], lambda t: qv(t)[:, :, 160:192]),
                (lambda t: t[:, 1536:2048].rearrange("p (h e) -> p h e", e=64)[:, :, 32:64], lambda t: qv(t)[:, :, 128:160]),
            ])
            kvv = lambda t: t[:, 0:2048].rearrange("p (h t e) -> p h t e", t=2, e=128)
            conv(w_ukv, WKV, 512, 2048, 2048, 20, [
                (lambda t: t[:, 0:1024].rearrange("p (h e) -> p h e", e=128), lambda t: kvv(t)[:, :, 0, :]),
                (lambda t: t[:, 1024:2048].rearrange("p (h e) -> p h e", e=128), lambda t: kvv(t)[:, :, 1, :]),
            ])
            conv(w_mem, WM, D, 1024, 1024, 24, [(lambda t: t[:, 0:1024], lambda t: t[:, 0:1024])])
            conv(w_out, WO, D, D, D, None, [(lambda t: t[:, 0:D], lambda t: t[:, 0:D])])
            S.barrier()

        for jb, io in zip(jobs, jio):
            run_job(nc, S, mk_alloc, jb, io, dict(
                identb=identb, b_id=b_id, ones_f=ones_f, b_1f=b_1f, ones_b=ones_b, b_1b=b_1b, epsb=epsb, b_eps=b_eps,
                WI=WI, WQ=WQ, WKV=WKV, WM=WM, WO=WO, pgain=pgain))
        S.barrier()
        print("instructions:", S.ninst, {k: v for k, v in S.cnt.items()})
    return nc


def run_job(nc, S, mk_alloc, jb, io, G):
    S_, Tq = jb["S"], jb["Tq"]
    Tw = Tq + 2 * HALO
    assert S_ >= Tw
    io = dict(io)
    io["xw"] = io["xkv"][0:Tw, :]
    identb, b_id = G["identb"], G["b_id"]
    ones_f, b_1f, ones_b, b_1b = G["ones_f"], G["b_1f"], G["ones_b"], G["b_1b"]
    epsb, b_eps = G["epsb"], G["b_eps"]
    WI, WQ, WKV, WM, WO = G["WI"], G["WQ"], G["WKV"], G["WM"], G["WO"]
    RS = float(1.0 / math.sqrt(2048.0))

    with contextlib.ExitStack() as jst:
        JT, _ = mk_alloc(jst)
        mkT = JT("mkT", [128, 4, 256], BF16); b_mk = Buf()
        mv = JT("mv", [128, 2, 512], BF16); b_mv = Buf()

        def make_norm_T(T, P):
            xs = [T(f"xs{i}", [128, D], F32) for i in range(2)]
            hb = [T(f"hb{i}", [128, D], BF16) for i in range(2)]
            ss = [T(f"ss{i}", [128, 1], F32) for i in range(2)]
            rs = [T(f"rs{i}", [128, 1], F32) for i in range(2)]
            ptr = [P(f"ptr{i}", [128, 8, 128], BF16) for i in range(2)]
            b_xs, b_hb, b_ss, b_rs = [Buf(), Buf()], [Buf(), Buf()], [Buf(), Buf()], [Buf(), Buf()]
            b_ptr = [Buf() for _ in range(2)]
            ctr = [0, 0]

            def norm_a(src):
                b = ctr[0] % 2
                ctr[0] += 1
                S.dma("sp", xs[b][:], src, writes=[b_xs[b]])
                S.op("act", lambda e: e.activation(hb[b][:], xs[b][:], AF.Square, scale=RS, accum_out=ss[b][:]),
                     reads=[b_xs[b]], writes=[b_hb[b], b_ss[b]])
                S.op("act", lambda e: e.activation(rs[b][:], ss[b][:], AF.Ln, bias=epsb[:]),
                     reads=[b_ss[b], b_eps], writes=[b_rs[b]])
                S.op("act", lambda e: e.activation(rs[b][:], rs[b][:], AF.Exp, scale=-0.5),
                     reads=[b_rs[b]], writes=[b_rs[b]])
                S.op("act", lambda e: e.activation(hb[b][:], xs[b][:], AF.Copy, scale=rs[b][:]),
                     reads=[b_xs[b], b_rs[b]], writes=[b_hb[b]])
                return b

            def norm_b(b, hT, b_hT, col0):
                for g4 in range(4):
                    r = ctr[1] % 2
                    ctr[1] += 1
                    reg = ptr[r][:, 0:4, :]
                    for j in range(4):
                        kc = g4 * 4 + j
                        S.op("pe", lambda e: e.transpose(reg[:, j, :], hb[b][:, kc * 128:(kc + 1) * 128], identb[:]),
                             reads=[b_hb[b], b_id], writes=[b_ptr[r]], inc=(j == 3))
                    S.op("dve", lambda e: e.tensor_copy(hT[:, g4 * 4:(g4 + 1) * 4, col0:col0 + 128], reg),
                         reads=[b_ptr[r]], writes=[b_hT])

            def norm_T(src, ntok, hT, b_hT, col0=0):
                for i in range(ntok // 128):
                    b = norm_a(src[i * 128:(i + 1) * 128, :])
                    norm_b(b, hT, b_hT, col0 + i * 128)

            def norm_queue(src_of, hT, b_hT, ntiles):
                slots = {}
                ents = []

                def mk_a(i):
                    def f():
                        slots[i] = norm_a(src_of(i))
                    return f

                def mk_b(i):
                    def f():
                        norm_b(slots[i], hT, b_hT, i * 128)
                    return f
                ents.append(mk_a(0))
                for i in range(ntiles):
                    if i + 1 < ntiles:
                        ents.append(mk_a(i + 1))
                    ents.append(mk_b(i))
                return ents
            norm_T.queue = norm_queue
            return norm_T

        with contextlib.ExitStack() as st:
            T, P = mk_alloc(st)
            norm_T = make_norm_T(T, P)
            hTm = T("hTm", [128, 16, 256], BF16); b_hTm = Buf()
            wm = T("wm", [128, 16, 1024], BF16); b_wm = Buf()
            pm = [P(f"pm{i}", [128, 512], F32) for i in range(2)]
            b_pm = [Buf(), Buf()]
            S.dma("sp", wm[:], WM.rearrange("(kc p) c -> p kc c", p=128), writes=[b_wm])
            norm_T(io["mem"], 256, hTm, b_hTm)
            k = 0
            for h in range(4):
                i = k % 2; k += 1
                for kc in range(16):
                    S.op("pe", lambda e, i=i, h=h, kc=kc: e.matmul(pm[i][:, 0:256], lhsT=wm[:, kc, h * 128:(h + 1) * 128],
                                                                   rhs=hTm[:, kc, :], start=(kc == 0), stop=(kc == 15)),
                         reads=[b_wm, b_hTm], writes=[b_pm[i]], inc=(kc == 15))
                S.op("act", lambda e, i=i, h=h: e.activation(mkT[:, h, :], pm[i][:, 0:256], AF.Copy),
                     reads=[b_pm[i]], writes=[b_mk])
            for m in range(2):
                i = k % 2; k += 1
                for kc in range(16):
                    S.op("pe", lambda e, i=i, m=m, kc=kc: e.matmul(pm[i][:], lhsT=hTm[:, kc, m * 128:(m + 1) * 128],
                                                                   rhs=wm[:, kc, 512:1024], start=(kc == 0), stop=(kc == 15)),
                         reads=[b_wm, b_hTm], writes=[b_pm[i]], inc=(kc == 15))
                S.op("act", lambda e, i=i, m=m: e.activation(mv[:, m, :], pm[i][:], AF.Copy),
                     reads=[b_pm[i]], writes=[b_mv])
            S.barrier()

        TB = 1024

        class Pump:
            def __init__(self):
                self.q = []
                self.k = 0
                self.every = 1

            def tick(self, force=False):
                self.k += 1
                if self.q and (force or self.k % self.every == 0):
                    self.q.pop(0)()

            def drain(self):
                while self.q:
                    self.q.pop(0)()

        def rms_feat_common(cf, b_cf, cn, b_cn, sq, b_sq, px, b_px, rstd, b_rstd, nfeat):
            for kc in range(4):
                s2 = kc % 2
                S.op("pool", lambda e, kc=kc, s2=s2: e.tensor_tensor(sq[:, s2, :], cf[:, kc, :], cf[:, kc, :], ALU.mult),
                     reads=[b_cf], writes=[b_sq[s2]])
                for half in range(2):
                    hs = slice(half * 512, (half + 1) * 512)
                    S.op("pe", lambda e, kc=kc, hs=hs, half=half, s2=s2: e.matmul(px[half][:], lhsT=ones_f[:], rhs=sq[:, s2, hs],
                                                                                 start=(kc == 0), stop=(kc == 3)),
                         reads=[b_1f, b_sq[s2]], writes=[b_px[half]])
            for half in range(2):
                hs = slice(half * 512, (half + 1) * 512)
                S.op("act", lambda e, hs=hs, half=half: e.activation(rstd[:, hs], px[half][:], AF.Ln, bias=epsb[:], scale=1.0 / nfeat),
                     reads=[b_px[half], b_eps], writes=[b_rstd])
                S.op("act", lambda e, hs=hs: e.activation(rstd[:, hs], rstd[:, hs], AF.Exp, scale=-0.5),
                     reads=[b_rstd], writes=[b_rstd])
            for kc in range(4):
                S.op("dve", lambda e, kc=kc: e.tensor_tensor(cn[:, kc, :], cf[:, kc, :], rstd[:], ALU.mult),
                     reads=[b_cf, b_rstd], writes=[b_cn])

        with contextlib.ExitStack() as st:
            T, P = mk_alloc(st)
            norm_T = make_norm_T(T, P)
            hTs = [T(f"hT{i}", [128, 16, TB], BF16) for i in range(2)]
            b_hTs = [Buf(), Buf()]
            wkin = T("wkin", [128, 16, 640], BF16); b_wkin = Buf()
            wkv = T("wkv", [128, 4, 2048], BF16); b_wkv = Buf()
            cf = T("cf", [128, 4, TB], F32); b_cf = Buf()
            sq = T("sq", [128, 2, TB], F32); b_sq = [Buf(), Buf()]
            cn = T("cn", [128, 4, TB], BF16); b_cn = Buf()
            rstd = T("rstd", [128, TB], F32); b_rstd = Buf()
            rcs = T("rcs", [64, 2, TB], F32); b_rcs = Buf()
            t1 = T("t1", [64, 512], F32); b_t1 = Buf()
            t2 = T("t2", [64, 512], F32); b_t2 = Buf()
            stg = [T(f"stg{i}", [128, TB], BF16) for i in range(4)]
            b_stg = [Buf() for _ in range(4)]
            vst = [T(f"vst{i}", [128, 1024], BF16) for i in range(2)]
            b_vst = [Buf(), Buf()]
            pp = [P(f"pp{i}", [128, 512], F32) for i in range(4)]
            b_pp = [Buf() for _ in range(4)]
            px = [P(f"px{i}", [128, 512], F32) for i in range(2)]
            b_px = [Buf(), Buf()]
            cs = dict(pp=0, stg=0, vst=0)
            pump = Pump()
            pump.every = 2

            S.dma("sp", wkin[:, :, 0:576], WI[:, 3584:4160].rearrange("(kc p) c -> p kc c", p=128), writes=[b_wkin])
            S.dma("sp", wkin[:, :, 576:640], WI[:, 6208:6272].rearrange("(kc p) c -> p kc c", p=128), writes=[b_wkin])
            S.dma("sp", wkv[:], WKV.rearrange("(kc p) c -> p kc c", p=128), writes=[b_wkv])

            def queue_kv(blk):
                hb_ = blk % 2
                return norm_T.queue(lambda i: io["xkv"][blk * TB + i * 128: blk * TB + (i + 1) * 128, :], hTs[hb_], b_hTs[hb_], TB // 128)

            nblk = S_ // TB
            pump.q = queue_kv(0)
            pump.drain()
            for blk in range(nblk):
                hT, b_hT = hTs[blk % 2], b_hTs[blk % 2]
                tsl = slice(blk * TB, (blk + 1) * TB)
                if blk + 1 < nblk:
                    pump.q = queue_kv(blk + 1)
                S.dma("sp", rcs[:], io["ropek"][:, :, tsl].rearrange("a p t -> p a t"), writes=[b_rcs])
                for sb in range(4):
                    for half in range(2):
                        hs = slice(half * 512, (half + 1) * 512)
                        pi = cs["pp"] % 4; cs["pp"] += 1
                        for kc in range(16):
                            S.op("pe", lambda e, pi=pi, sb=sb, kc=kc, hs=hs: e.matmul(
                                pp[pi][:], lhsT=wkin[:, kc, sb * 128:(sb + 1) * 128], rhs=hT[:, kc, hs],
                                start=(kc == 0), stop=(kc == 15)),
                                reads=[b_wkin, b_hT], writes=[b_pp[pi]], inc=(kc == 15))
                        S.op("act", lambda e, pi=pi, sb=sb, hs=hs: e.activation(cf[:, sb, hs], pp[pi][:], AF.Copy),
                             reads=[b_pp[pi]], writes=[b_cf])
                        pump.tick()
                rms_feat_common(cf, b_cf, cn, b_cn, sq, b_sq, px, b_px, rstd, b_rstd, 512.0)
                pump.tick(force=True)
                pump.tick(force=True)
                si = cs["stg"] % 4; cs["stg"] += 1
                for half in range(2):
                    hs = slice(half * 512, (half + 1) * 512)
                    for which, col in ((0, 512), (1, 576)):
                        for kc in range(16):
                            S.op("pe", lambda e, which=which, col=col, kc=kc, hs=hs: e.matmul(
                                px[which][0:64, :], lhsT=wkin[:, kc, col:col + 64], rhs=hT[:, kc, hs],
                                start=(kc == 0), stop=(kc == 15)),
                                reads=[b_wkin, b_hT], writes=[b_px[which]], inc=(kc == 15))
                    S.op("dve", lambda e, hs=hs: e.tensor_tensor(t1[:], px[0][0:64, :], rcs[:, 0, hs], ALU.mult),
                         reads=[b_px[0], b_rcs], writes=[b_t1])
                    S.op("dve", lambda e, hs=hs: e.tensor_tensor(t2[:], px[1][0:64, :], rcs[:, 1, hs], ALU.mult),
                         reads=[b_px[1], b_rcs], writes=[b_t2])
                    S.op("pool", lambda e, hs=hs, si=si: e.tensor_tensor(stg[si][0:64, hs], t1[:], t2[:], ALU.add),
                         reads=[b_t1, b_t2], writes=[b_stg[si]])
                    pump.tick()
                S.dma("pool", io["KR"][:, tsl], stg[si][0:64, :], reads=[b_stg[si]])
                for h in range(8):
                    si = cs["stg"] % 4; cs["stg"] += 1
                    for half in range(2):
                        hs = slice(half * 512, (half + 1) * 512)
                        pi = cs["pp"] % 4; cs["pp"] += 1
                        for kc in range(4):
                            S.op("pe", lambda e, pi=pi, h=h, kc=kc, hs=hs: e.matmul(
                                pp[pi][:], lhsT=wkv[:, kc, h * 128:(h + 1) * 128], rhs=cn[:, kc, hs],
                                start=(kc == 0), stop=(kc == 3)),
                                reads=[b_wkv, b_cn], writes=[b_pp[pi]], inc=(kc == 3))
                        if half == 0:
                            S.op("act", lambda e, pi=pi, si=si, hs=hs: e.activation(stg[si][:, hs], pp[pi][:], AF.Copy),
                                 reads=[b_pp[pi]], writes=[b_stg[si]])
                        else:
                            S.op("dve", lambda e, pi=pi, si=si, hs=hs: e.tensor_copy(stg[si][:, hs], pp[pi][:]),
                                 reads=[b_pp[pi]], writes=[b_stg[si]])
                        pump.tick()
                    S.dma("pool", io["KN"][h, :, tsl], stg[si][:], reads=[b_stg[si]])
                for tl in range(TB // 128):
                    vi = cs["vst"] % 2; cs["vst"] += 1
                    for hg in range(2):
                        pi = cs["pp"] % 4; cs["pp"] += 1
                        for kc in range(4):
                            S.op("pe", lambda e, pi=pi, hg=hg, kc=kc, tl=tl: e.matmul(
                                pp[pi][:], lhsT=cn[:, kc, tl * 128:(tl + 1) * 128],
                                rhs=wkv[:, kc, 1024 + hg * 512:1024 + (hg + 1) * 512], start=(kc == 0), stop=(kc == 3)),
                                reads=[b_wkv, b_cn], writes=[b_pp[pi]], inc=(kc == 3))
                        if hg == 0:
                            S.op("act", lambda e, pi=pi, vi=vi, hg=hg: e.activation(vst[vi][:, hg * 512:(hg + 1) * 512], pp[pi][:], AF.Copy),
                                 reads=[b_pp[pi]], writes=[b_vst[vi]])
                        else:
                            S.op("dve", lambda e, pi=pi, vi=vi, hg=hg: e.tensor_copy(vst[vi][:, hg * 512:(hg + 1) * 512], pp[pi][:]),
                                 reads=[b_pp[pi]], writes=[b_vst[vi]])
                        pump.tick()
                    r0 = blk * TB + tl * 128
                    S.dma("pool", io["VB"][r0:r0 + 128, :], vst[vi][:], reads=[b_vst[vi]])
                pump.drain()
            S.barrier()

        with contextlib.ExitStack() as st:
            T, P = mk_alloc(st)
            norm_T = make_norm_T(T, P)
            hTs = [T(f"hT{i}", [128, 16, TB], BF16) for i in range(2)]
            b_hTs = [Buf(), Buf()]
            wblk = [T(f"wblk{i}", [128, 16, 512], BF16) for i in range(2)]
            b_wblk = [Buf(), Buf()]
            wq = T("wq", [128, 4, 2048], BF16); b_wq = Buf()
            cf = T("cf", [128, 4, TB], F32); b_cf = Buf()
            sq = T("sq", [128, 2, TB], F32); b_sq = [Buf(), Buf()]
            cn = T("cn", [128, 4, TB], BF16); b_cn = Buf()
            rstd = T("rstd", [128, TB], F32); b_rstd = Buf()
            rcs = T("rcs", [64, 2, TB], F32); b_rcs = Buf()
            t1 = T("t1", [64, 512], F32); b_t1 = Buf()
            t2 = T("t2", [64, 512], F32); b_t2 = Buf()
            stg = [T(f"stg{i}", [128, TB], BF16) for i in range(4)]
            b_stg = [Buf() for _ in range(4)]
            vst = [T(f"vst{i}", [128, 512], BF16) for i in range(2)]
            b_vst = [Buf(), Buf()]
            pp = [P(f"pp{i}", [128, 512], F32) for i in range(4)]
            b_pp = [Buf() for _ in range(4)]
            px = [P(f"px{i}", [128, 512], F32) for i in range(2)]
            b_px = [Buf(), Buf()]
            cs = dict(pp=0, stg=0, vst=0, wb=0)
            pump = Pump()
            S.dma("sp", wq[:], WQ.rearrange("(kc p) c -> p kc c", p=128), writes=[b_wq])

            def load_wblk(c0):
                wi = cs["wb"] % 2; cs["wb"] += 1
                S.dma("sp", wblk[wi][:], WI[:, c0:c0 + 512].rearrange("(kc p) c -> p kc c", p=128), writes=[b_wblk[wi]])
                return wi

            def feat_sub(hT, b_hT, wi, sbi, evac):
                for half in range(2):
                    hs = slice(half * 512, (half + 1) * 512)
                    pi = cs["pp"] % 4; cs["pp"] += 1
                    for kc in range(16):
                        S.op("pe", lambda e, pi=pi, kc=kc, hs=hs: e.matmul(
                            pp[pi][:], lhsT=wblk[wi][:, kc, sbi * 128:(sbi + 1) * 128], rhs=hT[:, kc, hs],
                            start=(kc == 0), stop=(kc == 15)),
                            reads=[b_wblk[wi], b_hT], writes=[b_pp[pi]], inc=(kc == 15))
                    evac(pi, half, hs)
                pump.tick()

            def store_feat(dst, func, scale):
                si = cs["stg"] % 4; cs["stg"] += 1

                def ev(pi, half, hs):
                    S.op("act", lambda e: e.activation(stg[si][:, hs], pp[pi][:], func, scale=scale),
                         reads=[b_pp[pi]], writes=[b_stg[si]])
                    if half == 1:
                        S.dma("pool", dst, stg[si][:], reads=[b_stg[si]])
                return ev

            def tokmajor_v(hT, b_hT, wi, row0):
                for tl in range(TB // 128):
                    vi = cs["vst"] % 2; cs["vst"] += 1
                    pi = cs["pp"] % 4; cs["pp"] += 1
                    for kc in range(16):
                        S.op("pe", lambda e, pi=pi, kc=kc, tl=tl: e.matmul(
                            pp[pi][:], lhsT=hT[:, kc, tl * 128:(tl + 1) * 128], rhs=wblk[wi][:, kc, :],
                            start=(kc == 0), stop=(kc == 15)),
                            reads=[b_wblk[wi], b_hT], writes=[b_pp[pi]], inc=(kc == 15))
                    S.op("act", lambda e, pi=pi, vi=vi: e.activation(vst[vi][:], pp[pi][:], AF.Copy),
                         reads=[b_pp[pi]], writes=[b_vst[vi]])
                    S.dma("pool", io["VA"][row0 + tl * 128: row0 + (tl + 1) * 128, :], vst[vi][:], reads=[b_vst[vi]])
                    pump.tick()

            SA = float(128.0 ** -0.5)
            SB = float(192.0 ** -0.5)
            nblk_w = Tw // TB
            qb0 = HALO // TB
            qb1 = (HALO + Tq) // TB

            def queue_w(wblki):
                hb_ = wblki % 2
                return norm_T.queue(lambda i: io["xw"][wblki * TB + i * 128: wblki * TB + (i + 1) * 128, :], hTs[hb_], b_hTs[hb_], TB // 128)

            pump.q = queue_w(0)
            pump.drain()
            for wblki in range(nblk_w):
                hT, b_hT = hTs[wblki % 2], b_hTs[wblki % 2]
                row0 = wblki * TB
                wsl = slice(row0, row0 + TB)
                is_q = qb0 <= wblki < qb1
                if wblki + 1 < nblk_w:
                    pump.q = queue_w(wblki + 1)
                pump.k = 0
                pump.every = 3 if is_q else 1
                if is_q:
                    q0t = row0 - HALO
                    qsl = slice(q0t, q0t + TB)
                    S.dma("sp", rcs[:], io["ropeq"][:, :, qsl].rearrange("a p t -> p a t"), writes=[b_rcs])
                    cols = [3072, 0, 512, 1024, 1536, 2048, 5184, 2560, 4160, 4672, 5696]
                else:
                    cols = [1536, 2048]
                wi_next = load_wblk(cols[0])
                for ci, c0 in enumerate(cols):
                    wi = wi_next
                    if ci + 1 < len(cols):
                        wi_next = load_wblk(cols[ci + 1])
                    if c0 == 2048:
                        tokmajor_v(hT, b_hT, wi, row0)
                        continue
                    for sbi in range(4):
                        c = c0 + sbi * 128
                        if c < 1536:
                            feat_sub(hT, b_hT, wi, sbi, store_feat(io["QA"][c // 128, :, qsl], AF.Copy, SA))
                        elif c < 2048:
                            feat_sub(hT, b_hT, wi, sbi, store_feat(io["KA"][(c - 1536) // 128, :, wsl], AF.Copy, 1.0))
                        elif c < 3072:
                            feat_sub(hT, b_hT, wi, sbi, store_feat(io["GA"][(c - 2560) // 128, :, qsl], AF.Silu, 1.0))
                        elif c < 3584:
                            def ev(pi, half, hs, sbi=sbi):
                                S.op("act", lambda e: e.activation(cf[:, sbi, hs], pp[pi][:], AF.Copy),
                                     reads=[b_pp[pi]], writes=[b_cf])
                            feat_sub(hT, b_hT, wi, sbi, ev)
                        elif c < 5184:
                            feat_sub(hT, b_hT, wi, sbi, store_feat(io["GB"][(c - 4160) // 128, :, qsl], AF.Silu, 1.0))
                        elif c < 5696:
                            feat_sub(hT, b_hT, wi, sbi, store_feat(io["QC"][(c - 5184) // 128, :, qsl], AF.Copy, SA))
                        else:
                            feat_sub(hT, b_hT, wi, sbi, store_feat(io["GC"][(c - 5696) // 128, :, qsl], AF.Silu, 1.0))
                    if ci == 1 and is_q:
                        rms_feat_common(cf, b_cf, cn, b_cn, sq, b_sq, px, b_px, rstd, b_rstd, 512.0)
                if not is_q:
                    pump.drain()
                    continue
                for h in range(8):
                    si = cs["stg"] % 4; cs["stg"] += 1
                    for half in range(2):
                        hs = slice(half * 512, (half + 1) * 512)
                        pi = cs["pp"] % 4; cs["pp"] += 1
                        for kc in range(4):
                            S.op("pe", lambda e, pi=pi, h=h, kc=kc, hs=hs: e.matmul(
                                pp[pi][:], lhsT=wq[:, kc, h * 128:(h + 1) * 128], rhs=cn[:, kc, hs],
                                start=(kc == 0), stop=(kc == 3)),
                                reads=[b_wq, b_cn], writes=[b_pp[pi]], inc=(kc == 3))
                        S.op("act", lambda e, pi=pi, si=si, hs=hs: e.activation(stg[si][:, hs], pp[pi][:], AF.Copy, scale=SB),
                             reads=[b_pp[pi]], writes=[b_stg[si]])
                    S.dma("pool", io["QN"][h, :, qsl], stg[si][:], reads=[b_stg[si]])
                    si = cs["stg"] % 4; cs["stg"] += 1
                    for half in range(2):
                        hs = slice(half * 512, (half + 1) * 512)
                        for which, col in ((0, 1024 + h * 64), (1, 1536 + h * 64)):
                            for kc in range(4):
                                S.op("pe", lambda e, which=which, col=col, kc=kc, hs=hs: e.matmul(
                                    px[which][0:64, :], lhsT=wq[:, kc, col:col + 64], rhs=cn[:, kc, hs],
                                    start=(kc == 0), stop=(kc == 3)),
                                    reads=[b_wq, b_cn], writes=[b_px[which]], inc=(kc == 3))
                        S.op("dve", lambda e, hs=hs: e.tensor_tensor(t1[:], px[0][0:64, :], rcs[:, 0, hs], ALU.mult),
                             reads=[b_px[0], b_rcs], writes=[b_t1])
                        S.op("dve", lambda e, hs=hs: e.tensor_tensor(t2[:], px[1][0:64, :], rcs[:, 1, hs], ALU.mult),
                             reads=[b_px[1], b_rcs], writes=[b_t2])
                        S.op("pool", lambda e, hs=hs, si=si: e.tensor_tensor(stg[si][0:64, hs], t1[:], t2[:], ALU.add),
                             reads=[b_t1, b_t2], writes=[b_stg[si]])
                    S.dma("pool", io["QR"][h, :, qsl], stg[si][0:64, :], reads=[b_stg[si]])
                    pump.tick(force=True)
                pump.drain()
            S.barrier()

        def dense_attention(T, P, nheads, nch, kn_of, kr_of, v_of, kv_bufs_of, q_src, qr_src, g_src, y_dst, prefetch):
            QT = 512
            NPT = 12
            LA = 3
            qn = [T(f"qn{i}", [128, QT], BF16) for i in range(2)]
            qr = [T(f"qr{i}", [128, QT], BF16) for i in range(2)] if qr_src is not None else None
            gt = [T(f"gt{i}", [128, QT], BF16) for i in range(2)]
            b_q = [[Buf(), Buf(), Buf()], [Buf(), Buf(), Buf()]]
            ptl = [T(f"pt{i}", [128, QT], BF16) for i in range(NPT)]
            b_pt = [Buf() for _ in range(NPT)]
            accD = [T(f"accD{i}", [128, QT], F32) for i in range(2)]
            accP = [T(f"accP{i}", [128, QT], F32) for i in range(2)]
            b_accD, b_accP = [Buf(), Buf()], [Buf(), Buf()]
            rc = [T(f"rc{i}", [128, QT], F32) for i in range(2)]
            of = [T(f"of{i}", [128, QT], F32) for i in range(2)]
            yo = [T(f"yo{i}", [128, QT], BF16) for i in range(2)]
            b_rc, b_of, b_yo = [Buf(), Buf()], [Buf(), Buf()], [Buf(), Buf()]
            st_ = [P(f"st{i}", [128, QT], F32) for i in range(LA + 1)]
            b_st = [Buf() for _ in range(LA + 1)]
            ot = [P(f"ot{i}", [128, QT], F32) for i in range(2)]
            dn = [P(f"dn{i}", [128, QT], F32) for i in range(2)]
            b_ot, b_dn = [Buf(), Buf()], [Buf(), Buf()]
            nqt = Tq // QT
            tiles = [(h, qt) for h in range(nheads) for qt in range(nqt)]
            if qr is not None:
                for i in range(2):
                    S.op("pool", lambda e, i=i: e.memset(qr[i][64:128, :], 0.0), writes=[b_q[i][1]])

            def load_q(idx):
                h, qt = tiles[idx]
                i = idx % 2
                qs = slice(qt * QT, (qt + 1) * QT)
                S.dma("sp", qn[i][:], q_src[h, :, qs], writes=[b_q[i][0]])
                if qr is not None:
                    S.dma("sp", qr[i][0:64, :], qr_src[h, :, qs], writes=[b_q[i][1]])
                S.dma("sp", gt[i][:], g_src[h, :, qs], writes=[b_q[i][2]])

            def finalize(idx, usedP):
                h, qt = tiles[idx]
                i = idx % 2
                qs = slice(qt * QT, (qt + 1) * QT)
                S.op("pe", lambda e: e.matmul(dn[i][:], lhsT=ones_f[:], rhs=accD[i][:], start=True, stop=(not usedP)),
                     reads=[b_1f, b_accD[i]], writes=[b_dn[i]], inc=(not usedP))
                if usedP:
                    S.op("pe", lambda e: e.matmul(dn[i][:], lhsT=ones_f[:], rhs=accP[i][:], start=False, stop=True),
                         reads=[b_1f, b_accP[i]], writes=[b_dn[i]])
                S.op("dve", lambda e: e.reciprocal(rc[i][:], dn[i][:]), reads=[b_dn[i]], writes=[b_rc[i]])
                S.op("dve", lambda e: e.tensor_tensor(of[i][:], ot[i][:], rc[i][:], ALU.mult),
                     reads=[b_ot[i], b_rc[i]], writes=[b_of[i]])
                S.op("pool", lambda e: e.tensor_tensor(yo[i][:], of[i][:], gt[i][:], ALU.mult),
                     reads=[b_of[i], b_q[i][2]], writes=[b_yo[i]])
                S.dma("pool", y_dst(h)[:, qs], yo[i][:], reads=[b_yo[i]])

            cst = [0, 0]
            load_q(0)
            pending_fin = None
            for idx, (h, qt) in enumerate(tiles):
                if qt == 0 and prefetch is not None:
                    prefetch(h)
                i = idx % 2
                kvb = kv_bufs_of(h)

                def qk(j):
                    s = cst[0] % (LA + 1); cst[0] += 1
                    kr = kr_of(h, j) if kr_of is not None else None
                    S.op("pe", lambda e: e.matmul(st_[s][:], lhsT=kn_of(h, j), rhs=qn[i][:], start=True, stop=(kr is None)),
                         reads=kvb + [b_q[i][0]], writes=[b_st[s]], inc=(kr is None))
                    if kr is not None:
                        S.op("pe", lambda e: e.matmul(st_[s][:], lhsT=kr, rhs=qr[i][:], start=False, stop=True),
                             reads=kvb + [b_q[i][1]], writes=[b_st[s]])
                    return s

                pend = []
                for j in range(min(LA, nch)):
                    pend.append(qk(j))
                usedP = False
                for j in range(nch):
                    s = pend.pop(0)
                    p = cst[1] % NPT; cst[1] += 1
                    S.op("act", lambda e, s=s, p=p: e.activation(ptl[p][:], st_[s][:], AF.Exp),
                         reads=[b_st[s]], writes=[b_pt[p]])
                    if j + LA < nch:
                        pend.append(qk(j + LA))
                    S.op("pe", lambda e, p=p, j=j: e.matmul(ot[i][:], lhsT=v_of(h, j), rhs=ptl[p][:], start=(j == 0), stop=(j == nch - 1)),
                         reads=kvb + [b_pt[p]], writes=[b_ot[i]])
                    if j % 2 == 1:
                        if not usedP:
                            S.op("pool", lambda e, p=p: e.tensor_copy(accP[i][:], ptl[p][:]), reads=[b_pt[p]], writes=[b_accP[i]])
                        else:
                            S.op("pool", lambda e, p=p: e.tensor_tensor(accP[i][:], accP[i][:], ptl[p][:], ALU.add),
                                 reads=[b_pt[p]], writes=[b_accP[i]])
                        usedP = True
                    else:
                        if j == 0:
                            S.op("dve", lambda e, p=p: e.tensor_copy(accD[i][:], ptl[p][:]), reads=[b_pt[p]], writes=[b_accD[i]])
                        else:
                            S.op("dve", lambda e, p=p: e.tensor_tensor(accD[i][:], accD[i][:], ptl[p][:], ALU.add),
                                 reads=[b_pt[p]], writes=[b_accD[i]])
                    if j == min(3, nch - 1):
                        if pending_fin is not None:
                            finalize(*pending_fin)
                            pending_fin = None
                        if idx + 1 < len(tiles):
                            load_q(idx + 1)
                pending_fin = (idx, usedP)
            finalize(*pending_fin)

        with contextlib.ExitStack() as st:
            T, P = mk_alloc(st)
            nch = S_ // 128
            kn = [T(f"kn{i}", [128, S_], BF16) for i in range(2)]
            vv = [T(f"vv{i}", [128, nch, 128], BF16) for i in range(2)]
            krt = T("krt", [128, S_], BF16); b_kr = Buf()
            S.op("pool", lambda e: e.memset(krt[64:128, :], 0.0), writes=[b_kr])
            b_kn = [Buf(), Buf()]
            b_vv = [[Buf() for _ in range((nch + 15) // 16)] for _ in range(2)]
            S.dma("sp", krt[0:64, :], io["KR"], writes=[b_kr])

            def load_kv(h):
                i = h % 2
                S.dma("sp", kn[i][:], io["KN"][h], writes=[b_kn[i]])
                vsrc = io["VB"].rearrange("(j p) c -> p j c", p=128)
                for j0 in range(0, nch, 16):
                    S.dma("sp", vv[i][:, j0:j0 + 16, :], vsrc[:, j0:j0 + 16, h * 128:(h + 1) * 128], writes=[b_vv[i][j0 // 16]])

            load_kv(0)

            def prefetch(h):
                if h + 1 < 8:
                    load_kv(h + 1)

            dense_attention(
                T, P, 8, nch,
                kn_of=lambda h, j: kn[h % 2][:, j * 128:(j + 1) * 128],
                kr_of=lambda h, j: krt[:, j * 128:(j + 1) * 128],
                v_of=lambda h, j: vv[h % 2][:, j, :],
                kv_bufs_of=lambda h: [b_kn[h % 2], b_kr] + b_vv[h % 2],
                q_src=io["QN"], qr_src=io["QR"], g_src=io["GB"],
                y_dst=lambda h: io["YT"][4 + h], prefetch=prefetch)
            S.barrier()

        with contextlib.ExitStack() as st:
            T, P = mk_alloc(st)
            dense_attention(
                T, P, 4, 2,
                kn_of=lambda h, j: mkT[:, h, j * 128:(j + 1) * 128],
                kr_of=None,
                v_of=lambda h, j: mv[:, j, h * 128:(h + 1) * 128],
                kv_bufs_of=lambda h: [b_mk, b_mv],
                q_src=io["QC"], qr_src=None, g_src=io["GC"],
                y_dst=lambda h: io["YT"][12 + h], prefetch=None)
            S.barrier()

        with contextlib.ExitStack() as st:
            T, P = mk_alloc(st)
            bias = T("bias", [128, 48, 128], F32); b_bias = Buf()
            S.dma("sp", bias[:], io["bias"], writes=[b_bias])
            ka = [T(f"ka{i}", [128, Tw], BF16) for i in range(2)]
            b_ka = [Buf(), Buf()]
            qa2 = [[T(f"qa{j}_{i}", [128, Tq], BF16) for i in range(3)] for j in range(2)]
            b_qa2 = [[Buf() for _ in range(3)] for _ in range(2)]
            ga2 = [T(f"ga{j}", [128, Tq], BF16) for j in range(2)]
            b_ga2 = [Buf(), Buf()]

            def load_head(hh):
                j = hh % 2
                if hh > 0:
                    S.dma("sp", ka[j][:], io["KA"][hh], writes=[b_ka[j]])
                for g_ in range(3):
                    S.dma("sp", qa2[j][g_][:], io["QA"][g_ * 4 + hh], writes=[b_qa2[j][g_]])
                S.dma("sp", ga2[j][:], io["GA"][hh], writes=[b_ga2[j]])
            maxch = Tq // 128 + 1
            NV = 3
            vr = [T(f"vr{i}", [128, maxch, 128], BF16) for i in range(NV)]
            b_vr = [[Buf(), Buf()] for _ in range(NV)]
            accO = T("accO", [128, Tq], F32); b_aO = Buf()
            accD = T("accD", [128, Tq], F32); b_aD = Buf()
            sA = [T(f"sA{i}", [128, 2, 512], F32) for i in range(2)]
            pA = [T(f"pA{i}", [128, 2, 512], BF16) for i in range(2)]
            b_sA, b_pA = [[Buf(), Buf()], [Buf(), Buf()]], [[Buf(), Buf()], [Buf(), Buf()]]
            yo = T("yo3", [128, Tq], BF16); b_yo = Buf()
            stp = [[P(f"stp{i}_{c}", [128, 512], F32) for c in range(2)] for i in range(2)]
            otp = [P(f"otp{i}", [128, 512], F32) for i in range(2)]
            dnp = [P(f"dnp{i}", [128, 512], F32) for i in range(2)]
            b_stp, b_otp, b_dnp = [[Buf(), Buf()], [Buf(), Buf()]], [Buf(), Buf()], [Buf(), Buf()]

            def bc(ap2, nb):
                return bass.AP(ap2.tensor, ap2.offset, [list(ap2.ap[0]), [0, nb], list(ap2.ap[1])])

            va_t = io["VA"].tensor
            S.dma("sp", ka[0][:], io["KA"][0], writes=[b_ka[0]])
            vctr = [0]
            bctr = [0]
            for h in range(4):
                kh = ka[h % 2]
                b_kh = b_ka[h % 2]
                if h == 0:
                    load_head(0)
                if h + 1 < 4:
                    load_head(h + 1)
                qa, b_qa = qa2[h % 2], b_qa2[h % 2]
                ga, b_ga = ga2[h % 2], b_ga2[h % 2]
                res = []
                for g, d in enumerate(DILS):
                    nl = Tq // d
                    nq = min(128, nl)
                    nun = nl // nq
                    for r in range(d):
                        res.append(dict(g=g, d=d, r=r, nq=nq, nun=nun, lq0=HALO // d))

                def load_v(ri):
                    R_ = res[ri]
                    vi = vctr[0] % NV; vctr[0] += 1
                    R_["vi"] = vi
                    d, r, nq, nun, lq0 = R_["d"], R_["r"], R_["nq"], R_["nun"], R_["lq0"]
                    nchk = nun + 1
                    base = ((lq0 - 64) * d + r) * 512 + h * 128
                    if nchk > 1:
                        S.dma("sp", vr[vi][:, 0:nchk - 1, :],
                              bass.AP(va_t, base, [[d * 512, 128], [128 * d * 512, nchk - 1], [1, 128]]),
                              writes=[b_vr[vi][0]])
                    S.dma("sp", vr[vi][0:nq, nchk - 1, :],
                          bass.AP(va_t, base + (nchk - 1) * 128 * d * 512, [[d * 512, nq], [1, 128]]),
                          writes=[b_vr[vi][1]])

                batches = []
                for ri, R_ in enumerate(res):
                    for u0 in range(0, R_["nun"], 4):
                        batches.append(dict(ri=ri, u0=u0, nb=min(4, R_["nun"] - u0)))

                def qk(B):
                    R_ = res[B["ri"]]
                    g, d, r, nq, nun, lq0 = R_["g"], R_["d"], R_["r"], R_["nq"], R_["nun"], R_["lq0"]
                    k = bctr[0] % 2; bctr[0] += 1
                    B["k"] = k
                    for b in range(B["nb"]):
                        u = B["u0"] + b
                        l0 = lq0 + nq * u
                        qstart = l0 * d + r - HALO
                        qsl = slice(qstart, qstart + d * (nq - 1) + 1, d)
                        kA0 = (l0 - 64) * d + r
                        kB0 = (l0 + 64) * d + r
                        kAs = slice(kA0, kA0 + d * 127 + 1, d)
                        kBs = slice(kB0, kB0 + d * (nq - 1) + 1, d)
                        last = (b == B["nb"] - 1)
                        S.op("pe", lambda e: e.matmul(stp[k][0][:, b * 128:b * 128 + nq], lhsT=kh[:, kAs], rhs=qa[g][:, qsl], start=True, stop=True),
                             reads=[b_kh, b_qa[g]], writes=[b_stp[k][0]], inc=last)
                        S.op("pe", lambda e: e.matmul(stp[k][1][0:nq, b * 128:b * 128 + nq], lhsT=kh[:, kBs], rhs=qa[g][:, qsl], start=True, stop=True),
                             reads=[b_kh, b_qa[g]], writes=[b_stp[k][1]], inc=last)

                def softmax_part(B):
                    R_ = res[B["ri"]]
                    g, nq, nun = R_["g"], R_["nq"], R_["nun"]
                    k, nb, u0 = B["k"], B["nb"], B["u0"]
                    gh = g * 4 + h
                    W = nb * 128 if nq == 128 else nq

                    def v3(ap2):
                        return ap2.rearrange("p (b q) -> p b q", q=128) if nq == 128 else ap2

                    for c, (var_n, var_e, rows) in enumerate(((0, 1, slice(0, 128)), (2, 3, slice(0, nq)))):
                        bias_n = bias[rows, gh * 4 + var_n, 0:nq]
                        src = stp[k][c][rows, 0:W]
                        dst = sA[k][rows, c, 0:W]
                        if nq == 128:
                            S.op("dve", lambda e: e.tensor_tensor(v3(dst), v3(src), bc(bias_n, nb), ALU.add),
                                 reads=[b_stp[k][c], b_bias], writes=[b_sA[k][c]])
                        else:
                            S.op("dve", lambda e: e.tensor_tensor(dst, src, bias_n, ALU.add),
                                 reads=[b_stp[k][c], b_bias], writes=[b_sA[k][c]])
                        if c == 0 and u0 == 0:
                            S.op("dve", lambda e: e.tensor_tensor(sA[k][rows, c, 0:nq], stp[k][c][rows, 0:nq], bias[rows, gh * 4 + var_e, 0:nq], ALU.add),
                                 reads=[b_stp[k][c], b_bias], writes=[b_sA[k][c]])
                        if c == 1 and u0 + nb == nun:
                            o = (nb - 1) * 128
                            S.op("dve", lambda e: e.tensor_tensor(sA[k][rows, c, o:o + nq], stp[k][c][rows, o:o + nq], bias[rows, gh * 4 + var_e, 0:nq], ALU.add),
                                 reads=[b_stp[k][c], b_bias], writes=[b_sA[k][c]])
                        S.op("act", lambda e: e.activation(pA[k][rows, c, 0:W], sA[k][rows, c, 0:W], AF.Exp),
                             reads=[b_sA[k][c]], writes=[b_pA[k][c]])

                def pv(B):
                    R_ = res[B["ri"]]
                    g, d, r, nq, nun, lq0, vi = R_["g"], R_["d"], R_["r"], R_["nq"], R_["nun"], R_["lq0"], R_["vi"]
                    k, nb, u0 = B["k"], B["nb"], B["u0"]
                    W = nb * 128 if nq == 128 else nq
                    for b in range(nb):
                        u = u0 + b
                        cs_ = slice(b * 128, b * 128 + nq)
                        S.op("pe", lambda e: e.matmul(otp[k][:, cs_], lhsT=vr[vi][:, u, :], rhs=pA[k][:, 0, cs_], start=True, stop=False),
                             reads=b_vr[vi] + [b_pA[k][0]], writes=[b_otp[k]], inc=False)
                        S.op("pe", lambda e: e.matmul(otp[k][:, cs_], lhsT=vr[vi][0:nq, u + 1, :], rhs=pA[k][0:nq, 1, cs_], start=False, stop=True),
                             reads=b_vr[vi] + [b_pA[k][1]], writes=[b_otp[k]], inc=(b == nb - 1))
                    for b in range(nb):
                        cs_ = slice(b * 128, b * 128 + nq)
                        S.op("pe", lambda e: e.matmul(dnp[k][:, cs_], lhsT=ones_b[:], rhs=pA[k][:, 0, cs_], start=True, stop=False),
                             reads=[b_1b, b_pA[k][0]], writes=[b_dnp[k]], inc=False)
                        S.op("pe", lambda e: e.matmul(dnp[k][:, cs_], lhsT=ones_b[0:nq, :], rhs=pA[k][0:nq, 1, cs_], start=False, stop=True),
                             reads=[b_1b, b_pA[k][1]], writes=[b_dnp[k]], inc=(b == nb - 1))
                    l0 = lq0 + nq * u0
                    qstart = l0 * d + r - HALO
                    nqt = nb * nq
                    qsl = slice(qstart, qstart + d * (nqt - 1) + 1, d)
                    if g == 0:
                        S.op("dve", lambda e: e.tensor_copy(accO[:, qsl], otp[k][:, 0:W]), reads=[b_otp[k]], writes=[b_aO])
                        S.op("dve", lambda e: e.tensor_copy(accD[:, qsl], dnp[k][:, 0:W]), reads=[b_dnp[k]], writes=[b_aD])
                    else:
                        S.op("dve", lambda e: e.tensor_tensor(accO[:, qsl], accO[:, qsl], otp[k][:, 0:W], ALU.add),
                             reads=[b_otp[k]], writes=[b_aO])
                        S.op("dve", lambda e: e.tensor_tensor(accD[:, qsl], accD[:, qsl], dnp[k][:, 0:W], ALU.add),
                             reads=[b_dnp[k]], writes=[b_aD])

                load_v(0)
                if len(res) > 1:
                    load_v(1)
                qk(batches[0])
                for bi, B in enumerate(batches):
                    if B["u0"] == 0 and B["ri"] + 2 < len(res):
                        load_v(B["ri"] + 2)
                    softmax_part(B)
                    if bi + 1 < len(batches):
                        qk(batches[bi + 1])
                    pv(B)
                S.op("dve", lambda e: e.reciprocal(accD[:], accD[:]), reads=[], writes=[b_aD])
                S.op("dve", lambda e: e.tensor_tensor(accO[:], accO[:], accD[:], ALU.mult), reads=[b_aD], writes=[b_aO])
                S.op("pool", lambda e: e.tensor_tensor(yo[:], accO[:], ga[:], ALU.mult), reads=[b_aO, b_ga], writes=[b_yo])
                S.dma("pool", io["YT"][h], yo[:], reads=[b_yo])
            S.barrier()

        with contextlib.ExitStack() as st:
            T, P = mk_alloc(st)
            wo = T("wo", [128, 16, D], BF16); b_wo = [Buf() for _ in range(4)]
            pg = T("pg", [128, D], F32); b_pg = Buf()
            for cb in range(4):
                S.dma("sp", wo[:, :, cb * 512:(cb + 1) * 512], WO[:, cb * 512:(cb + 1) * 512].rearrange("(kc p) c -> p kc c", p=128), writes=[b_wo[cb]])
            S.dma("sp", pg[:], G["pgain"], writes=[b_pg])
            yt = [T(f"yt{i}", [128, 16, 512], BF16) for i in range(2)]
            b_yt = [Buf(), Buf()]
            xt = [T(f"xt{i}", [128, D], F32) for i in range(2)]
            b_xt = [Buf(), Buf()]
            yf = [T(f"yf{i}", [128, D], F32) for i in range(2)]
            b_yf = [Buf(), Buf()]
            jk = T("jk", [128, D], BF16)
            ss = [T(f"ssc{i}", [128, 1], F32) for i in range(2)]
            rs = [T(f"rsc{i}", [128, 1], F32) for i in range(2)]
            b_ss, b_rs = [Buf(), Buf()], [Buf(), Buf()]
            ob = [T(f"ob{i}", [128, D], F32) for i in range(2)]
            b_ob = [Buf(), Buf()]
            po = [P(f"po{i}", [128, D], F32) for i in range(2)]
            b_po = [Buf(), Buf()]
            ntile = Tq // 128

            def load_y(blk):
                i = blk % 2
                S.dma("sp", yt[i][:], io["YT"][:, :, blk * 512:(blk + 1) * 512].rearrange("c p t -> p c t"), writes=[b_yt[i]])

            load_y(0)
            for tl in range(ntile):
                blk, sub = tl // 4, tl % 4
                if sub == 0 and blk + 1 < Tq // 512:
                    load_y(blk + 1)
                yi = blk % 2
                i = tl % 2
                S.dma("sp", xt[i][:], io["xw"][HALO + tl * 128: HALO + (tl + 1) * 128, :], writes=[b_xt[i]])
                for cb in range(4):
                    for kc in range(16):
                        S.op("pe", lambda e, i=i, yi=yi, cb=cb, kc=kc, sub=sub: e.matmul(
                            po[i][:, cb * 512:(cb + 1) * 512], lhsT=yt[yi][:, kc, sub * 128:(sub + 1) * 128],
                            rhs=wo[:, kc, cb * 512:(cb + 1) * 512], start=(kc == 0), stop=(kc == 15)),
                            reads=[b_yt[yi], b_wo[cb]], writes=[b_po[i]], inc=(kc == 15 and cb == 3))
                for cb in range(4):
                    S.op("act", lambda e, i=i, cb=cb: e.activation(yf[i][:, cb * 512:(cb + 1) * 512], po[i][:, cb * 512:(cb + 1) * 512], AF.Copy),
                         reads=[b_po[i]], writes=[b_yf[i]])
                S.op("act", lambda e, i=i: e.activation(jk[:], yf[i][:], AF.Square, scale=RS, accum_out=ss[i][:]),
                     reads=[b_yf[i]], writes=[b_ss[i]])
                S.op("act", lambda e, i=i: e.activation(rs[i][:], ss[i][:], AF.Ln, bias=epsb[:]),
                     reads=[b_ss[i], b_eps], writes=[b_rs[i]])
                S.op("act", lambda e, i=i: e.activation(rs[i][:], rs[i][:], AF.Exp, scale=-0.5),
                     reads=[b_rs[i]], writes=[b_rs[i]])
                S.op("dve", lambda e, i=i: e.scalar_tensor_tensor(ob[i][:], yf[i][:], rs[i][:], pg[:], ALU.mult, ALU.mult),
                     reads=[b_yf[i], b_rs[i], b_pg], writes=[b_ob[i]])
                S.op("pool", lambda e, i=i: e.tensor_tensor(ob[i][:], ob[i][:], xt[i][:], ALU.add),
                     reads=[b_xt[i]], writes=[b_ob[i]])
                S.dma("pool", io["y"][tl * 128:(tl + 1) * 128, :], ob[i][:], reads=[b_ob[i]])
            S.barrier()


def _t5_bucket(rel):
    nb = 16
    max_exact = 8
    bucket = np.where(rel > 0, nb, 0)
    n = np.abs(rel)
    nf = np.maximum(n, 1).astype(np.float32)
    large = max_exact + (np.log(nf / np.float32(max_exact)) / np.float32(math.log(1024 / max_exact))
                         * np.float32(nb - max_exact)).astype(np.int32)
    large = np.minimum(large, nb - 1)
    return bucket + np.where(n < max_exact, n, large)


def _bias_tiles(rel_bias, Tq, at_start, at_end):
    out = np.empty((128, 48, 128), np.float32)
    kk = np.arange(128)[:, None]
    qq = np.arange(128)[None, :]
    for g, d in enumerate(DILS):
        nq = min(128, Tq // d)
        for var in range(4):
            rel = (kk - 64 - qq) if var < 2 else (kk + 64 - qq)
            valid = np.abs(rel) <= 64
            if var == 1 and at_start:
                valid = valid & (kk >= 64)
            if var == 3 and at_end:
                valid = valid & (kk < nq - 64)
            bidx = _t5_bucket(rel * d)
            for h in range(4):
                gh = g * 4 + h
                vals = rel_bias[bidx, gh]
                out[:, gh * 4 + var, :] = np.where(valid, vals, np.float32(NEG))
    return out


def _rope_tables(pos, scale):
    inv = (10000.0 ** (-np.arange(0, 64, 2, dtype=np.float32) / np.float32(64))).astype(np.float32)
    ang = (pos.astype(np.float32)[None, :] * inv[:, None]).astype(np.float32)
    c = np.cos(ang).astype(np.float32)
    s = np.sin(ang).astype(np.float32)
    cos = np.concatenate([c, c], 0)
    sin = np.concatenate([-s, s], 0)
    return (np.stack([cos, sin], 0) * np.float32(scale)).astype(np.float32)


def make_in_maps(inputs, jobs=JOBS, ncores=NCORES):
    f = lambda a: np.ascontiguousarray(np.asarray(a, dtype=np.float32))
    gains = np.concatenate([
        f(inputs["pre_gain"]).reshape(16, 128).T, f(inputs["q_gain"]).reshape(4, 128).T,
        f(inputs["kv_gain"]).reshape(4, 128).T, f(inputs["mem_gain"]).reshape(16, 128).T], axis=1)
    shared = dict(
        w_in=f(inputs["w_in"][0]), w_uq=f(inputs["w_uq"][0]), w_ukv=f(inputs["w_ukv"][0]),
        w_mem_kv=f(inputs["w_mem_kv"][0]), w_out=f(inputs["w_out"][0]),
        gains=np.ascontiguousarray(gains), pgain=np.ascontiguousarray(np.broadcast_to(f(inputs["post_gain"]).reshape(1, D), (128, D))),
        ident=np.eye(128, dtype=np.float32))
    rel_bias = f(inputs["rel_bias"])
    xs = dict(p=f(inputs["x_prompt"]), s=f(inputs["x_sample"]))
    mems = dict(p=f(inputs["mem_prompt"]), s=f(inputs["mem_sample"]))
    maps = []
    for c in range(ncores):
        m = dict(shared)
        for jb in jobs:
            n, S_, Tq, per = jb["name"], jb["S"], jb["Tq"], jb["per_seq"]
            b = c // per
            q0 = (c % per) * Tq
            x = xs[n][b]
            m[f"xkv_{n}"] = np.roll(x, -(q0 - HALO), axis=0)
            m[f"mem_{n}"] = mems[n][b]
            m[f"ropeq_{n}"] = _rope_tables(np.arange(q0, q0 + Tq), 192.0 ** -0.5)
            m[f"ropek_{n}"] = _rope_tables((np.arange(S_) + q0 - HALO) % S_, 1.0)
            m[f"bias_{n}"] = _bias_tiles(rel_bias, Tq, q0 == 0, q0 + Tq == S_)
        maps.append(m)
    return maps


_NC_CACHE = {}


def kernel(**inputs):
    jobs = JOBS
    if "nc" not in _NC_CACHE:
        _NC_CACHE["nc"] = build_program(jobs)
    nc = _NC_CACHE["nc"]
    maps = make_in_maps(inputs, jobs)
    res = run_bass_kernel_spmd(nc, maps, core_ids=list(range(NCORES)))
    outs = []
    for jb, key in zip(jobs, ("x_prompt", "x_sample")):
        n, Tq, per = jb["name"], jb["Tq"], jb["per_seq"]
        shp = np.asarray(inputs[key]).shape
        y = np.empty(shp, np.float32)
        for c in range(NCORES):
            b = c // per
            q0 = (c % per) * Tq
            y[b, q0:q0 + Tq] = res.results[c][f"y_{n}"]
        outs.append(y)
    return tuple(outs)
```

```python
import contextlib
import math
import numpy as np
import concourse.bass as bass
import concourse.mybir as mybir
from concourse.bass_utils import run_bass_kernel_spmd

F32 = mybir.dt.float32
BF16 = mybir.dt.bfloat16
AF = mybir.ActivationFunctionType
ALU = mybir.AluOpType

D = 2048
DIN = 6208
HALO = 1024
DILS = (1, 4, 16)
NCORES = 8
JOBS = (dict(name="p", S=8192, Tq=4096, per_seq=2), dict(name="s", S=4096, Tq=1024, per_seq=4))
NEG = -1e30


class Buf:
    __slots__ = ("w", "r")

    def __init__(self):
        self.w = None
        self.r = []


class Sched:
    def __init__(self, nc, stack):
        self.nc = nc
        self.eng = {"pe": nc.tensor, "act": nc.scalar, "dve": nc.vector, "pool": nc.gpsimd, "sp": nc.sync}
        self.sem = {}
        self.cnt = {}
        self.pending = {}
        for k in ("pe", "act", "dve", "pool"):
            self.sem[k] = stack.enter_context(nc.semaphore("s_" + k))
            self.cnt[k] = 0
            self.pending[k] = False
        self.dq = {}
        for k, n in (("sp", 16), ("pool", 8), ("act", 8)):
            sems = [stack.enter_context(nc.semaphore(f"d_{k}{i}")) for i in range(n)]
            self.dq[k] = dict(sems=sems, n=0)
        self.waited = {k: {} for k in self.eng}
        self.ninst = 0

    def _need(self, e, tok, out):
        if tok is None:
            return
        key, sem, val, src = tok
        if src == "pe" and e == "pe":
            return
        if self.waited[e].get(key, 0) >= val:
            return
        cur = out.get(key)
        if cur is None or cur[1] < val:
            out[key] = (sem, val)

    def _deps(self, e, reads, writes):
        need = {}
        for b in reads:
            self._need(e, b.w, need)
        for b in writes:
            self._need(e, b.w, need)
            for t in b.r:
                self._need(e, t, need)
        return need

    def _commit(self, tok, reads, writes):
        for b in reads:
            b.r.append(tok)
            if len(b.r) > 48:
                best = {}
                for t in b.r:
                    if t[0] not in best or best[t[0]][2] < t[2]:
                        best[t[0]] = t
                b.r = list(best.values())
        for b in writes:
            b.w = tok
            b.r = []

    def op(self, e, fn, reads=(), writes=(), inc=True):
        need = self._deps(e, reads, writes)
        eng = self.eng[e]
        for key, (sem, val) in need.items():
            self.waited[e][key] = val
            eng.wait_ge(sem, val)
        self.ninst += 1
        sem = self.sem[e]
        if inc:
            self.cnt[e] += 1
            val = self.cnt[e]
            fn(eng).then_inc(sem, 1)
            self.pending[e] = False
        else:
            val = self.cnt[e] + 1
            fn(eng)
            self.pending[e] = True
        tok = ("c_" + e, sem, val, e)
        self._commit(tok, reads, writes)
        return tok

    def dma(self, q, out, in_, reads=(), writes=(), **kw):
        need = self._deps(q, reads, writes)
        d = self.dq[q]
        n = d["n"]
        d["n"] += 1
        K = len(d["sems"])
        slot = n % K
        sem = d["sems"][slot]
        key = f"d_{q}{slot}"
        prev = 16 * (n // K)
        val = prev + 16
        if prev > 0 and self.waited[q].get(key, 0) < prev:
            need[key] = (sem, prev)
        eng = self.eng[q]
        for k2, (s, v) in need.items():
            self.waited[q][k2] = v
            eng.wait_ge(s, v)
        self.ninst += 1
        eng.dma_start(out=out, in_=in_, **kw).then_inc(sem, 16)
        tok = (key, sem, val, None)
        self._commit(tok, reads, writes)
        return tok

    def barrier(self):
        for k in self.pending:
            assert not self.pending[k], k
        toks = []
        for k in ("pe", "act", "dve", "pool"):
            if self.cnt[k] > 0:
                toks.append(("c_" + k, self.sem[k], self.cnt[k]))
        for q, d in self.dq.items():
            K = len(d["sems"])
            for slot in range(K):
                cnt = (d["n"] - slot + K - 1) // K if d["n"] > slot else 0
                if cnt > 0:
                    toks.append((f"d_{q}{slot}", d["sems"][slot], 16 * cnt))
        for e, eng in self.eng.items():
            for key, sem, val in toks:
                if key == "c_" + e:
                    continue
                if self.waited[e].get(key, 0) < val:
                    self.waited[e][key] = val
                    eng.wait_ge(sem, val)


def build_program(jobs=JOBS, debug=()):
    nc = bass.Bass("TRN2", target_bir_lowering=False)

    def din(name, shape, dt=F32):
        return nc.dram_tensor(name, list(shape), dt, kind="ExternalInput").ap()

    def dscr(name, shape, dt=BF16):
        kind = "ExternalOutput" if name in debug else "Internal"
        return nc.dram_tensor(name, list(shape), dt, kind=kind).ap()

    w_in = din("w_in", [D, DIN])
    w_uq = din("w_uq", [512, 1536])
    w_ukv = din("w_ukv", [512, 2048])
    w_mem = din("w_mem_kv", [D, 1024])
    w_out = din("w_out", [D, D])
    gains = din("gains", [128, 40])
    pgain = din("pgain", [128, D])
    ident = din("ident", [128, 128])

    WI = dscr("WI", [D, 6272])
    WQ = dscr("WQ", [512, 2048])
    WKV = dscr("WKV", [512, 2048])
    WM = dscr("WM", [D, 1024])
    WO = dscr("WO", [D, D])

    jio = []
    for jb in jobs:
        n, S_, Tq = jb["name"], jb["S"], jb["Tq"]
        Tw = Tq + 2 * HALO
        jio.append(dict(
            xkv=din(f"xkv_{n}", [S_, D]), mem=din(f"mem_{n}", [256, D]),
            ropeq=din(f"ropeq_{n}", [2, 64, Tq]), ropek=din(f"ropek_{n}", [2, 64, S_]),
            bias=din(f"bias_{n}", [128, 48, 128]),
            y=nc.dram_tensor(f"y_{n}", [Tq, D], F32, kind="ExternalOutput").ap(),
            QA=dscr(f"QA_{n}", [12, 128, Tq]), KA=dscr(f"KA_{n}", [4, 128, Tw]), VA=dscr(f"VA_{n}", [Tw, 512]),
            GA=dscr(f"GA_{n}", [4, 128, Tq]), QN=dscr(f"QN_{n}", [8, 128, Tq]), QR=dscr(f"QR_{n}", [8, 64, Tq]),
            KN=dscr(f"KN_{n}", [8, 128, S_]), KR=dscr(f"KR_{n}", [64, S_]), VB=dscr(f"VB_{n}", [S_, 1024]),
            GB=dscr(f"GB_{n}", [8, 128, Tq]), QC=dscr(f"QC_{n}", [4, 128, Tq]), GC=dscr(f"GC_{n}", [4, 128, Tq]),
            YT=dscr(f"YT_{n}", [16, 128, Tq]),
        ))

    with contextlib.ExitStack() as gst:
        S = Sched(nc, gst)

        uid = [0]

        def mk_alloc(st):
            def T(name, shape, dt):
                uid[0] += 1
                return st.enter_context(nc.sbuf_tensor(f"{name}_{uid[0]}", list(shape), dt))

            def P(name, shape, dt):
                uid[0] += 1
                return st.enter_context(nc.psum_tensor(f"{name}_{uid[0]}", list(shape), dt))
            return T, P

        GT, _ = mk_alloc(gst)
        identb = GT("identb", [128, 128], BF16); b_id = Buf()
        ones_f = GT("ones_f", [128, 128], F32); b_1f = Buf()
        ones_b = GT("ones_b", [128, 128], BF16); b_1b = Buf()
        epsb = GT("epsb", [128, 1], F32); b_eps = Buf()
        S.dma("pool", identb[:], ident, writes=[b_id])
        S.op("pool", lambda e: e.memset(ones_f[:], 1.0), writes=[b_1f])
        S.op("pool", lambda e: e.memset(ones_b[:], 1.0), writes=[b_1b])
        S.op("pool", lambda e: e.memset(epsb[:], 1e-6), writes=[b_eps])

        with contextlib.ExitStack() as st:
            T, P = mk_alloc(st)
            g = T("g", [128, 40], F32); b_g = Buf()
            S.dma("sp", g[:], gains, writes=[b_g])
            wf = [T(f"wf{i}", [128, DIN], F32) for i in range(2)]
            wb = [T(f"wb{i}", [128, 6272], BF16) for i in range(2)]
            b_wf = [Buf(), Buf()]
            b_wb = [Buf(), Buf()]
            cnt = [0]

            def conv(src, dst, nrows, nin, nout, gcol, copies):
                for kc in range(nrows // 128):
                    i = cnt[0] % 2
                    cnt[0] += 1
                    S.dma("sp", wf[i][:, 0:nin], src[kc * 128:(kc + 1) * 128, :], writes=[b_wf[i]])
                    for (dfn, sfn) in copies:
                        if gcol is None:
                            S.op("act", lambda e, i=i, dfn=dfn, sfn=sfn: e.activation(dfn(wb[i]), sfn(wf[i]), AF.Copy),
                                 reads=[b_wf[i]], writes=[b_wb[i]])
                        else:
                            S.op("act", lambda e, i=i, dfn=dfn, sfn=sfn, c=gcol + kc: e.activation(
                                dfn(wb[i]), sfn(wf[i]), AF.Copy, scale=g[:, c:c + 1]),
                                reads=[b_wf[i], b_g], writes=[b_wb[i]])
                    S.dma("pool", dst[kc * 128:(kc + 1) * 128, :], wb[i][:, 0:nout], reads=[b_wb[i]])

            conv(w_in, WI, D, DIN, 6272, 0, [
                (lambda t: t[:, 0:DIN], lambda t: t[:, 0:DIN]),
                (lambda t: t[:, 6208:6240], lambda t: t[:, 4128:4160]),
                (lambda t: t[:, 6240:6272], lambda t: t[:, 4096:4128]),
            ])
            qv = lambda t: t[:, 0:1536].rearrange("p (h f) -> p h f", f=192)
            conv(w_uq, WQ, 512, 1536, 2048, 16, [
                (lambda t: t[:, 0:1024].rearrange("p (h e) -> p h e", e=128), lambda t: qv(t)[:, :, 0:128]),
                (lambda t: t[:, 1024:1536].rearrange("p (h e) -> p h e", e=64), lambda t: qv(t)[:, :, 128:192]),
                (lambda t: t[:, 1536:2048].rearrange("p (h e) -> p h e", e=64)[:, :, 0:32], lambda t: qv(t)[:, :, 160:192]),
                (lambda t: t[:, 1536:2048].rearrange("p (h e) -> p h e", e=64)[:, :, 32:64], lambda t: qv(t)[:, :, 128:160]),
            ])
            kvv = lambda t: t[:, 0:2048].rearrange("p (h t e) -> p h t e", t=2, e=128)
            conv(w_ukv, WKV, 512, 2048, 2048, 20, [
                (lambda t: t[:, 0:1024].rearrange("p (h e) -> p h e", e=128), lambda t: kvv(t)[:, :, 0, :]),
                (lambda t: t[:, 1024:2048].rearrange("p (h e) -> p h e", e=128), lambda t: kvv(t)[:, :, 1, :]),
            ])
            conv(w_mem, WM, D, 1024, 1024, 24, [(lambda t: t[:, 0:1024], lambda t: t[:, 0:1024])])
            conv(w_out, WO, D, D, D, None, [(lambda t: t[:, 0:D], lambda t: t[:, 0:D])])
            S.barrier()

        for jb, io in zip(jobs, jio):
            run_job(nc, S, mk_alloc, jb, io, dict(
                identb=identb, b_id=b_id, ones_f=ones_f, b_1f=b_1f, ones_b=ones_b, b_1b=b_1b, epsb=epsb, b_eps=b_eps,
                WI=WI, WQ=WQ, WKV=WKV, WM=WM, WO=WO, pgain=pgain))
        S.barrier()
        print("instructions:", S.ninst, {k: v for k, v in S.cnt.items()})
    return nc


def run_job(nc, S, mk_alloc, jb, io, G):
    S_, Tq = jb["S"], jb["Tq"]
    Tw = Tq + 2 * HALO
    assert S_ >= Tw
    io = dict(io)
    io["xw"] = io["xkv"][0:Tw, :]
    identb, b_id = G["identb"], G["b_id"]
    ones_f, b_1f, ones_b, b_1b = G["ones_f"], G["b_1f"], G["ones_b"], G["b_1b"]
    epsb, b_eps = G["epsb"], G["b_eps"]
    WI, WQ, WKV, WM, WO = G["WI"], G["WQ"], G["WKV"], G["WM"], G["WO"]
    RS = float(1.0 / math.sqrt(2048.0))

    with contextlib.ExitStack() as jst:
        JT, _ = mk_alloc(jst)
        mkT = JT("mkT", [128, 4, 256], BF16); b_mk = Buf()
        mv = JT("mv", [128, 2, 512], BF16); b_mv = Buf()

        def make_norm_T(T, P):
            xs = [T(f"xs{i}", [128, D], F32) for i in range(2)]
            hb = [T(f"hb{i}", [128, D], BF16) for i in range(2)]
            ss = [T(f"ss{i}", [128, 1], F32) for i in range(2)]
            rs = [T(f"rs{i}", [128, 1], F32) for i in range(2)]
            ptr = [P(f"ptr{i}", [128, 8, 128], BF16) for i in range(2)]
            b_xs, b_hb, b_ss, b_rs = [Buf(), Buf()], [Buf(), Buf()], [Buf(), Buf()], [Buf(), Buf()]
            b_ptr = [Buf() for _ in range(2)]
            ctr = [0, 0]

            def norm_a(src):
                b = ctr[0] % 2
                ctr[0] += 1
                S.dma("sp", xs[b][:], src, writes=[b_xs[b]])
                S.op("act", lambda e: e.activation(hb[b][:], xs[b][:], AF.Square, scale=RS, accum_out=ss[b][:]),
                     reads=[b_xs[b]], writes=[b_hb[b], b_ss[b]])
                S.op("act", lambda e: e.activation(rs[b][:], ss[b][:], AF.Ln, bias=epsb[:]),
                     reads=[b_ss[b], b_eps], writes=[b_rs[b]])
                S.op("act", lambda e: e.activation(rs[b][:], rs[b][:], AF.Exp, scale=-0.5),
                     reads=[b_rs[b]], writes=[b_rs[b]])
                S.op("act", lambda e: e.activation(hb[b][:], xs[b][:], AF.Copy, scale=rs[b][:]),
                     reads=[b_xs[b], b_rs[b]], writes=[b_hb[b]])
                return b

            def norm_b(b, hT, b_hT, col0):
                for g8 in range(2):
                    r = ctr[1] % 2
                    ctr[1] += 1
                    reg = ptr[r][:, 0:8, :]
                    for j in range(8):
                        kc = g8 * 8 + j
                        S.op("pe", lambda e: e.transpose(reg[:, j, :], hb[b][:, kc * 128:(kc + 1) * 128], identb[:]),
                             reads=[b_hb[b], b_id], writes=[b_ptr[r]], inc=(j == 7))
                    S.op("dve", lambda e: e.tensor_copy(hT[:, g8 * 8:(g8 + 1) * 8, col0:col0 + 128], reg),
                         reads=[b_ptr[r]], writes=[b_hT])

            def norm_T(src, ntok, hT, b_hT, col0=0):
                for i in range(ntok // 128):
                    b = norm_a(src[i * 128:(i + 1) * 128, :])
                    norm_b(b, hT, b_hT, col0 + i * 128)

            def norm_queue(src_of, hT, b_hT, ntiles):
                slots = {}
                ents = []

                def mk_a(i):
                    def f():
                        slots[i] = norm_a(src_of(i))
                    return f

                def mk_b(i):
                    def f():
                        norm_b(slots[i], hT, b_hT, i * 128)
                    return f
                ents.append(mk_a(0))
                for i in range(ntiles):
                    if i + 1 < ntiles:
                        ents.append(mk_a(i + 1))
                    ents.append(mk_b(i))
                return ents
            norm_T.queue = norm_queue
            return norm_T

        with contextlib.ExitStack() as st:
            T, P = mk_alloc(st)
            norm_T = make_norm_T(T, P)
            hTm = T("hTm", [128, 16, 256], BF16); b_hTm = Buf()
            wm = T("wm", [128, 16, 1024], BF16); b_wm = Buf()
            pm = [P(f"pm{i}", [128, 512], F32) for i in range(2)]
            b_pm = [Buf(), Buf()]
            S.dma("sp", wm[:], WM.rearrange("(kc p) c -> p kc c", p=128), writes=[b_wm])
            norm_T(io["mem"], 256, hTm, b_hTm)
            k = 0
            for h in range(4):
                i = k % 2; k += 1
                for kc in range(16):
                    S.op("pe", lambda e, i=i, h=h, kc=kc: e.matmul(pm[i][:, 0:256], lhsT=wm[:, kc, h * 128:(h + 1) * 128],
                                                                   rhs=hTm[:, kc, :], start=(kc == 0), stop=(kc == 15)),
                         reads=[b_wm, b_hTm], writes=[b_pm[i]], inc=(kc == 15))
                S.op("act", lambda e, i=i, h=h: e.activation(mkT[:, h, :], pm[i][:, 0:256], AF.Copy),
                     reads=[b_pm[i]], writes=[b_mk])
            for m in range(2):
                i = k % 2; k += 1
                for kc in range(16):
                    S.op("pe", lambda e, i=i, m=m, kc=kc: e.matmul(pm[i][:], lhsT=hTm[:, kc, m * 128:(m + 1) * 128],
                                                                   rhs=wm[:, kc, 512:1024], start=(kc == 0), stop=(kc == 15)),
                         reads=[b_wm, b_hTm], writes=[b_pm[i]], inc=(kc == 15))
                S.op("act", lambda e, i=i, m=m: e.activation(mv[:, m, :], pm[i][:], AF.Copy),
                     reads=[b_pm[i]], writes=[b_mv])
            S.barrier()

        TB = 1024

        class Pump:
            def __init__(self):
                self.q = []
                self.k = 0
                self.every = 1

            def tick(self, force=False):
                self.k += 1
                if self.q and (force or self.k % self.every == 0):
                    self.q.pop(0)()

            def drain(self):
                while self.q:
                    self.q.pop(0)()

        def rms_feat_common(cf, b_cf, cn, b_cn, sq, b_sq, px, b_px, rstd, b_rstd, nfeat):
            for kc in range(4):
                s2 = kc % 2
                S.op("pool", lambda e, kc=kc, s2=s2: e.tensor_tensor(sq[:, s2, :], cf[:, kc, :], cf[:, kc, :], ALU.mult),
                     reads=[b_cf], writes=[b_sq[s2]])
                for half in range(2):
                    hs = slice(half * 512, (half + 1) * 512)
                    S.op("pe", lambda e, kc=kc, hs=hs, half=half, s2=s2: e.matmul(px[half][:], lhsT=ones_f[:], rhs=sq[:, s2, hs],
                                                                                 start=(kc == 0), stop=(kc == 3)),
                         reads=[b_1f, b_sq[s2]], writes=[b_px[half]])
            for half in range(2):
                hs = slice(half * 512, (half + 1) * 512)
                S.op("act", lambda e, hs=hs, half=half: e.activation(rstd[:, hs], px[half][:], AF.Ln, bias=epsb[:], scale=1.0 / nfeat),
                     reads=[b_px[half], b_eps], writes=[b_rstd])
                S.op("act", lambda e, hs=hs: e.activation(rstd[:, hs], rstd[:, hs], AF.Exp, scale=-0.5),
                     reads=[b_rstd], writes=[b_rstd])
            for kc in range(4):
                S.op("dve", lambda e, kc=kc: e.tensor_tensor(cn[:, kc, :], cf[:, kc, :], rstd[:], ALU.mult),
                     reads=[b_cf, b_rstd], writes=[b_cn])

        with contextlib.ExitStack() as st:
            T, P = mk_alloc(st)
            norm_T = make_norm_T(T, P)
            hTs = [T(f"hT{i}", [128, 16, TB], BF16) for i in range(2)]
            b_hTs = [Buf(), Buf()]
            wkin = T("wkin", [128, 16, 640], BF16); b_wkin = Buf()
            wkv = T("wkv", [128, 4, 2048], BF16); b_wkv = Buf()
            cf = T("cf", [128, 4, TB], F32); b_cf = Buf()
            sq = T("sq", [128, 2, TB], F32); b_sq = [Buf(), Buf()]
            cn = T("cn", [128, 4, TB], BF16); b_cn = Buf()
            rstd = T("rstd", [128, TB], F32); b_rstd = Buf()
            rcs = T("rcs", [64, 2, TB], F32); b_rcs = Buf()
            t1 = T("t1", [64, 512], F32); b_t1 = Buf()
            t2 = T("t2", [64, 512], F32); b_t2 = Buf()
            stg = [T(f"stg{i}", [128, TB], BF16) for i in range(4)]
            b_stg = [Buf() for _ in range(4)]
            vst = [T(f"vst{i}", [128, 1024], BF16) for i in range(2)]
            b_vst = [Buf(), Buf()]
            pp = [P(f"pp{i}", [128, 512], F32) for i in range(4)]
            b_pp = [Buf() for _ in range(4)]
            px = [P(f"px{i}", [128, 512], F32) for i in range(2)]
            b_px = [Buf(), Buf()]
            cs = dict(pp=0, stg=0, vst=0)
            pump = Pump()
            pump.every = 2

            S.dma("sp", wkin[:, :, 0:576], WI[:, 3584:4160].rearrange("(kc p) c -> p kc c", p=128), writes=[b_wkin])
            S.dma("sp", wkin[:, :, 576:640], WI[:, 6208:6272].rearrange("(kc p) c -> p kc c", p=128), writes=[b_wkin])
            S.dma("sp", wkv[:], WKV.rearrange("(kc p) c -> p kc c", p=128), writes=[b_wkv])

            def queue_kv(blk):
                hb_ = blk % 2
                return norm_T.queue(lambda i: io["xkv"][blk * TB + i * 128: blk * TB + (i + 1) * 128, :], hTs[hb_], b_hTs[hb_], TB // 128)

            nblk = S_ // TB
            pump.q = queue_kv(0)
            pump.drain()
            for blk in range(nblk):
                hT, b_hT = hTs[blk % 2], b_hTs[blk % 2]
                tsl = slice(blk * TB, (blk + 1) * TB)
                if blk + 1 < nblk:
                    pump.q = queue_kv(blk + 1)
                S.dma("sp", rcs[:], io["ropek"][:, :, tsl].rearrange("a p t -> p a t"), writes=[b_rcs])
                for sb in range(4):
                    for half in range(2):
                        hs = slice(half * 512, (half + 1) * 512)
                        pi = cs["pp"] % 4; cs["pp"] += 1
                        for kc in range(16):
                            S.op("pe", lambda e, pi=pi, sb=sb, kc=kc, hs=hs: e.matmul(
                                pp[pi][:], lhsT=wkin[:, kc, sb * 128:(sb + 1) * 128], rhs=hT[:, kc, hs],
                                start=(kc == 0), stop=(kc == 15)),
                                reads=[b_wkin, b_hT], writes=[b_pp[pi]], inc=(kc == 15))
                        S.op("act", lambda e, pi=pi, sb=sb, hs=hs: e.activation(cf[:, sb, hs], pp[pi][:], AF.Copy),
                             reads=[b_pp[pi]], writes=[b_cf])
                        pump.tick()
                rms_feat_common(cf, b_cf, cn, b_cn, sq, b_sq, px, b_px, rstd, b_rstd, 512.0)
                pump.tick(force=True)
                pump.tick(force=True)
                si = cs["stg"] % 4; cs["stg"] += 1
                for half in range(2):
                    hs = slice(half * 512, (half + 1) * 512)
                    for which, col in ((0, 512), (1, 576)):
                        for kc in range(16):
                            S.op("pe", lambda e, which=which, col=col, kc=kc, hs=hs: e.matmul(
                                px[which][0:64, :], lhsT=wkin[:, kc, col:col + 64], rhs=hT[:, kc, hs],
                                start=(kc == 0), stop=(kc == 15)),
                                reads=[b_wkin, b_hT], writes=[b_px[which]], inc=(kc == 15))
                    S.op("dve", lambda e, hs=hs: e.tensor_tensor(t1[:], px[0][0:64, :], rcs[:, 0, hs], ALU.mult),
                         reads=[b_px[0], b_rcs], writes=[b_t1])
                    S.op("dve", lambda e, hs=hs: e.tensor_tensor(t2[:], px[1][0:64, :], rcs[:, 1, hs], ALU.mult),
                         reads=[b_px[1], b_rcs], writes=[b_t2])
                    S.op("pool", lambda e, hs=hs, si=si: e.tensor_tensor(stg[si][0:64, hs], t1[:], t2[:], ALU.add),
                         reads=[b_t1, b_t2], writes=[b_stg[si]])
                    pump.tick()
                S.dma("pool", io["KR"][:, tsl], stg[si][0:64, :], reads=[b_stg[si]])
                for h in range(8):
                    si = cs["stg"] % 4; cs["stg"] += 1
                    for half in range(2):
                        hs = slice(half * 512, (half + 1) * 512)
                        pi = cs["pp"] % 4; cs["pp"] += 1
                        for kc in range(4):
                            S.op("pe", lambda e, pi=pi, h=h, kc=kc, hs=hs: e.matmul(
                                pp[pi][:], lhsT=wkv[:, kc, h * 128:(h + 1) * 128], rhs=cn[:, kc, hs],
                                start=(kc == 0), stop=(kc == 3)),
                                reads=[b_wkv, b_cn], writes=[b_pp[pi]], inc=(kc == 3))
                        if half == 0:
                            S.op("act", lambda e, pi=pi, si=si, hs=hs: e.activation(stg[si][:, hs], pp[pi][:], AF.Copy),
                                 reads=[b_pp[pi]], writes=[b_stg[si]])
                        else:
                            S.op("dve", lambda e, pi=pi, si=si, hs=hs: e.tensor_copy(stg[si][:, hs], pp[pi][:]),
                                 reads=[b_pp[pi]], writes=[b_stg[si]])
                        pump.tick()
                    S.dma("pool", io["KN"][h, :, tsl], stg[si][:], reads=[b_stg[si]])
                for tl in range(TB // 128):
                    vi = cs["vst"] % 2; cs["vst"] += 1
                    for hg in range(2):
                        pi = cs["pp"] % 4; cs["pp"] += 1
                        for kc in range(4):
                            S.op("pe", lambda e, pi=pi, hg=hg, kc=kc, tl=tl: e.matmul(
                                pp[pi][:], lhsT=cn[:, kc, tl * 128:(tl + 1) * 128],
                                rhs=wkv[:, kc, 1024 + hg * 512:1024 + (hg + 1) * 512], start=(kc == 0), stop=(kc == 3)),
                                reads=[b_wkv, b_cn], writes=[b_pp[pi]], inc=(kc == 3))
                        if hg == 0:
                            S.op("act", lambda e, pi=pi, vi=vi, hg=hg: e.activation(vst[vi][:, hg * 512:(hg + 1) * 512], pp[pi][:], AF.Copy),
                                 reads=[b_pp[pi]], writes=[b_vst[vi]])
                        else:
                            S.op("dve", lambda e, pi=pi, vi=vi, hg=hg: e.tensor_copy(vst[vi][:, hg * 512:(hg + 1) * 512], pp[pi][:]),
                                 reads=[b_pp[pi]], writes=[b_vst[vi]])
                        pump.tick()
                    r0 = blk * TB + tl * 128
                    S.dma("pool", io["VB"][r0:r0 + 128, :], vst[vi][:], reads=[b_vst[vi]])
                pump.drain()
            S.barrier()

        with contextlib.ExitStack() as st:
            T, P = mk_alloc(st)
            norm_T = make_norm_T(T, P)
            hTs = [T(f"hT{i}", [128, 16, TB], BF16) for i in range(2)]
            b_hTs = [Buf(), Buf()]
            wblk = [T(f"wblk{i}", [128, 16, 512], BF16) for i in range(2)]
            b_wblk = [Buf(), Buf()]
            wq = T("wq", [128, 4, 2048], BF16); b_wq = Buf()
            cf = T("cf", [128, 4, TB], F32); b_cf = Buf()
            sq = T("sq", [128, 2, TB], F32); b_sq = [Buf(), Buf()]
            cn = T("cn", [128, 4, TB], BF16); b_cn = Buf()
            rstd = T("rstd", [128, TB], F32); b_rstd = Buf()
            rcs = T("rcs", [64, 2, TB], F32); b_rcs = Buf()
            t1 = T("t1", [64, 512], F32); b_t1 = Buf()
            t2 = T("t2", [64, 512], F32); b_t2 = Buf()
            stg = [T(f"stg{i}", [128, TB], BF16) for i in range(4)]
            b_stg = [Buf() for _ in range(4)]
            vst = [T(f"vst{i}", [128, 512], BF16) for i in range(2)]
            b_vst = [Buf(), Buf()]
            pp = [P(f"pp{i}", [128, 512], F32) for i in range(4)]
            b_pp = [Buf() for _ in range(4)]
            px = [P(f"px{i}", [128, 512], F32) for i in range(2)]
            b_px = [Buf(), Buf()]
            cs = dict(pp=0, stg=0, vst=0, wb=0)
            pump = Pump()
            S.dma("sp", wq[:], WQ.rearrange("(kc p) c -> p kc c", p=128), writes=[b_wq])

            def load_wblk(c0):
                wi = cs["wb"] % 2; cs["wb"] += 1
                S.dma("sp", wblk[wi][:], WI[:, c0:c0 + 512].rearrange("(kc p) c -> p kc c", p=128), writes=[b_wblk[wi]])
                return wi

            def feat_sub(hT, b_hT, wi, sbi, evac):
                for half in range(2):
                    hs = slice(half * 512, (half + 1) * 512)
                    pi = cs["pp"] % 4; cs["pp"] += 1
                    for kc in range(16):
                        S.op("pe", lambda e, pi=pi, kc=kc, hs=hs: e.matmul(
                            pp[pi][:], lhsT=wblk[wi][:, kc, sbi * 128:(sbi + 1) * 128], rhs=hT[:, kc, hs],
                            start=(kc == 0), stop=(kc == 15)),
                            reads=[b_wblk[wi], b_hT], writes=[b_pp[pi]], inc=(kc == 15))
                    evac(pi, half, hs)
                pump.tick()

            def store_feat(dst, func, scale):
                si = cs["stg"] % 4; cs["stg"] += 1

                def ev(pi, half, hs):
                    S.op("act", lambda e: e.activation(stg[si][:, hs], pp[pi][:], func, scale=scale),
                         reads=[b_pp[pi]], writes=[b_stg[si]])
                    if half == 1:
                        S.dma("pool", dst, stg[si][:], reads=[b_stg[si]])
                return ev

            def tokmajor_v(hT, b_hT, wi, row0):
                for tl in range(TB // 128):
                    vi = cs["vst"] % 2; cs["vst"] += 1
                    pi = cs["pp"] % 4; cs["pp"] += 1
                    for kc in range(16):
                        S.op("pe", lambda e, pi=pi, kc=kc, tl=tl: e.matmul(
                            pp[pi][:], lhsT=hT[:, kc, tl * 128:(tl + 1) * 128], rhs=wblk[wi][:, kc, :],
                            start=(kc == 0), stop=(kc == 15)),
                            reads=[b_wblk[wi], b_hT], writes=[b_pp[pi]], inc=(kc == 15))
                    S.op("act", lambda e, pi=pi, vi=vi: e.activation(vst[vi][:], pp[pi][:], AF.Copy),
                         reads=[b_pp[pi]], writes=[b_vst[vi]])
                    S.dma("pool", io["VA"][row0 + tl * 128: row0 + (tl + 1) * 128, :], vst[vi][:], reads=[b_vst[vi]])
                    pump.tick()

            SA = float(128.0 ** -0.5)
            SB = float(192.0 ** -0.5)
            nblk_w = Tw // TB
            qb0 = HALO // TB
            qb1 = (HALO + Tq) // TB

            def queue_w(wblki):
                hb_ = wblki % 2
                return norm_T.queue(lambda i: io["xw"][wblki * TB + i * 128: wblki * TB + (i + 1) * 128, :], hTs[hb_], b_hTs[hb_], TB // 128)

            pump.q = queue_w(0)
            pump.drain()
            for wblki in range(nblk_w):
                hT, b_hT = hTs[wblki % 2], b_hTs[wblki % 2]
                row0 = wblki * TB
                wsl = slice(row0, row0 + TB)
                is_q = qb0 <= wblki < qb1
                if wblki + 1 < nblk_w:
                    pump.q = queue_w(wblki + 1)
                pump.k = 0
                pump.every = 3 if is_q else 1
                if is_q:
                    q0t = row0 - HALO
                    qsl = slice(q0t, q0t + TB)
                    S.dma("sp", rcs[:], io["ropeq"][:, :, qsl].rearrange("a p t -> p a t"), writes=[b_rcs])
                    cols = [3072, 0, 512, 1024, 1536, 2048, 5184, 2560, 4160, 4672, 5696]
                else:
                    cols = [1536, 2048]
                wi_next = load_wblk(cols[0])
                for ci, c0 in enumerate(cols):
                    wi = wi_next
                    if ci + 1 < len(cols):
                        wi_next = load_wblk(cols[ci + 1])
                    if c0 == 2048:
                        tokmajor_v(hT, b_hT, wi, row0)
                        continue
                    for sbi in range(4):
                        c = c0 + sbi * 128
                        if c < 1536:
                            feat_sub(hT, b_hT, wi, sbi, store_feat(io["QA"][c // 128, :, qsl], AF.Copy, SA))
                        elif c < 2048:
                            feat_sub(hT, b_hT, wi, sbi, store_feat(io["KA"][(c - 1536) // 128, :, wsl], AF.Copy, 1.0))
                        elif c < 3072:
                            feat_sub(hT, b_hT, wi, sbi, store_feat(io["GA"][(c - 2560) // 128, :, qsl], AF.Silu, 1.0))
                        elif c < 3584:
                            def ev(pi, half, hs, sbi=sbi):
                                S.op("act", lambda e: e.activation(cf[:, sbi, hs], pp[pi][:], AF.Copy),
                                     reads=[b_pp[pi]], writes=[b_cf])
                            feat_sub(hT, b_hT, wi, sbi, ev)
                        elif c < 5184:
                            feat_sub(hT, b_hT, wi, sbi, store_feat(io["GB"][(c - 4160) // 128, :, qsl], AF.Silu, 1.0))
                        elif c < 5696:
                            feat_sub(hT, b_hT, wi, sbi, store_feat(io["QC"][(c - 5184) // 128, :, qsl], AF.Copy, SA))
                        else:
                            feat_sub(hT, b_hT, wi, sbi, store_feat(io["GC"][(c - 5696) // 128, :, qsl], AF.Silu, 1.0))
                    if ci == 1 and is_q:
                        rms_feat_common(cf, b_cf, cn, b_cn, sq, b_sq, px, b_px, rstd, b_rstd, 512.0)
                if not is_q:
                    pump.drain()
                    continue
                for h in range(8):
                    si = cs["stg"] % 4; cs["stg"] += 1
                    for half in range(2):
                        hs = slice(half * 512, (half + 1) * 512)
                        pi = cs["pp"] % 4; cs["pp"] += 1
                        for kc in range(4):
                            S.op("pe", lambda e, pi=pi, h=h, kc=kc, hs=hs: e.matmul(
                                pp[pi][:], lhsT=wq[:, kc, h * 128:(h + 1) * 128], rhs=cn[:, kc, hs],
                                start=(kc == 0), stop=(kc == 3)),
                                reads=[b_wq, b_cn], writes=[b_pp[pi]], inc=(kc == 3))
                        S.op("act", lambda e, pi=pi, si=si, hs=hs: e.activation(stg[si][:, hs], pp[pi][:], AF.Copy, scale=SB),
                             reads=[b_pp[pi]], writes=[b_stg[si]])
                    S.dma("pool", io["QN"][h, :, qsl], stg[si][:], reads=[b_stg[si]])
                    si = cs["stg"] % 4; cs["stg"] += 1
                    for half in range(2):
                        hs = slice(half * 512, (half + 1) * 512)
                        for which, col in ((0, 1024 + h * 64), (1, 1536 + h * 64)):
                            for kc in range(4):
                                S.op("pe", lambda e, which=which, col=col, kc=kc, hs=hs: e.matmul(
                                    px[which][0:64, :], lhsT=wq[:, kc, col:col + 64], rhs=cn[:, kc, hs],
                                    start=(kc == 0), stop=(kc == 3)),
                                    reads=[b_wq, b_cn], writes=[b_px[which]], inc=(kc == 3))
                        S.op("dve", lambda e, hs=hs: e.tensor_tensor(t1[:], px[0][0:64, :], rcs[:, 0, hs], ALU.mult),
                             reads=[b_px[0], b_rcs], writes=[b_t1])
                        S.op("dve", lambda e, hs=hs: e.tensor_tensor(t2[:], px[1][0:64, :], rcs[:, 1, hs], ALU.mult),
                             reads=[b_px[1], b_rcs], writes=[b_t2])
                        S.op("pool", lambda e, hs=hs, si=si: e.tensor_tensor(stg[si][0:64, hs], t1[:], t2[:], ALU.add),
                             reads=[b_t1, b_t2], writes=[b_stg[si]])
                    S.dma("pool", io["QR"][h, :, qsl], stg[si][0:64, :], reads=[b_stg[si]])
                    pump.tick(force=True)
                pump.drain()
            S.barrier()

        def dense_attention(T, P, nheads, nch, kn_of, kr_of, v_of, kv_bufs_of, q_src, qr_src, g_src, y_dst, prefetch):
            QT = 512
            NPT = 12
            LA = 3
            qn = [T(f"qn{i}", [128, QT], BF16) for i in range(2)]
            qr = [T(f"qr{i}", [128, QT], BF16) for i in range(2)] if qr_src is not None else None
            gt = [T(f"gt{i}", [128, QT], BF16) for i in range(2)]
            b_q = [[Buf(), Buf(), Buf()], [Buf(), Buf(), Buf()]]
            ptl = [T(f"pt{i}", [128, QT], BF16) for i in range(NPT)]
            b_pt = [Buf() for _ in range(NPT)]
            accD = [T(f"accD{i}", [128, QT], F32) for i in range(2)]
            accP = [T(f"accP{i}", [128, QT], F32) for i in range(2)]
            b_accD, b_accP = [Buf(), Buf()], [Buf(), Buf()]
            rc = [T(f"rc{i}", [128, QT], F32) for i in range(2)]
            of = [T(f"of{i}", [128, QT], F32) for i in range(2)]
            yo = [T(f"yo{i}", [128, QT], BF16) for i in range(2)]
            b_rc, b_of, b_yo = [Buf(), Buf()], [Buf(), Buf()], [Buf(), Buf()]
            st_ = [P(f"st{i}", [128, QT], F32) for i in range(LA + 1)]
            b_st = [Buf() for _ in range(LA + 1)]
            ot = [P(f"ot{i}", [128, QT], F32) for i in range(2)]
            dn = [P(f"dn{i}", [128, QT], F32) for i in range(2)]
            b_ot, b_dn = [Buf(), Buf()], [Buf(), Buf()]
            nqt = Tq // QT
            tiles = [(h, qt) for h in range(nheads) for qt in range(nqt)]
            if qr is not None:
                for i in range(2):
                    S.op("pool", lambda e, i=i: e.memset(qr[i][64:128, :], 0.0), writes=[b_q[i][1]])

            def load_q(idx):
                h, qt = tiles[idx]
                i = idx % 2
                qs = slice(qt * QT, (qt + 1) * QT)
                S.dma("sp", qn[i][:], q_src[h, :, qs], writes=[b_q[i][0]])
                if qr is not None:
                    S.dma("sp", qr[i][0:64, :], qr_src[h, :, qs], writes=[b_q[i][1]])
                S.dma("sp", gt[i][:], g_src[h, :, qs], writes=[b_q[i][2]])

            def finalize(idx, usedP):
                h, qt = tiles[idx]
                i = idx % 2
                qs = slice(qt * QT, (qt + 1) * QT)
                S.op("pe", lambda e: e.matmul(dn[i][:], lhsT=ones_f[:], rhs=accD[i][:], start=True, stop=(not usedP)),
                     reads=[b_1f, b_accD[i]], writes=[b_dn[i]], inc=(not usedP))
                if usedP:
                    S.op("pe", lambda e: e.matmul(dn[i][:], lhsT=ones_f[:], rhs=accP[i][:], start=False, stop=True),
                         reads=[b_1f, b_accP[i]], writes=[b_dn[i]])
                S.op("dve", lambda e: e.reciprocal(rc[i][:], dn[i][:]), reads=[b_dn[i]], writes=[b_rc[i]])
                S.op("dve", lambda e: e.tensor_tensor(of[i][:], ot[i][:], rc[i][:], ALU.mult),
                     reads=[b_ot[i], b_rc[i]], writes=[b_of[i]])
                S.op("pool", lambda e: e.tensor_tensor(yo[i][:], of[i][:], gt[i][:], ALU.mult),
                     reads=[b_of[i], b_q[i][2]], writes=[b_yo[i]])
                S.dma("pool", y_dst(h)[:, qs], yo[i][:], reads=[b_yo[i]])

            cst = [0, 0]
            load_q(0)
            pending_fin = None
            for idx, (h, qt) in enumerate(tiles):
                if qt == 0 and prefetch is not None:
                    prefetch(h)
                i = idx % 2
                kvb = kv_bufs_of(h)

                def qk(j):
                    s = cst[0] % (LA + 1); cst[0] += 1
                    kr = kr_of(h, j) if kr_of is not None else None
                    S.op("pe", lambda e: e.matmul(st_[s][:], lhsT=kn_of(h, j), rhs=qn[i][:], start=True, stop=(kr is None)),
                         reads=kvb + [b_q[i][0]], writes=[b_st[s]], inc=(kr is None))
                    if kr is not None:
                        S.op("pe", lambda e: e.matmul(st_[s][:], lhsT=kr, rhs=qr[i][:], start=False, stop=True),
                             reads=kvb + [b_q[i][1]], writes=[b_st[s]])
                    return s

                pend = []
                for j in range(min(LA, nch)):
                    pend.append(qk(j))
                usedP = False
                for j in range(nch):
                    s = pend.pop(0)
                    p = cst[1] % NPT; cst[1] += 1
                    S.op("act", lambda e, s=s, p=p: e.activation(ptl[p][:], st_[s][:], AF.Exp),
                         reads=[b_st[s]], writes=[b_pt[p]])
                    if j + LA < nch:
                        pend.append(qk(j + LA))
                    S.op("pe", lambda e, p=p, j=j: e.matmul(ot[i][:], lhsT=v_of(h, j), rhs=ptl[p][:], start=(j == 0), stop=(j == nch - 1)),
                         reads=kvb + [b_pt[p]], writes=[b_ot[i]])
                    if j % 2 == 1:
                        if not usedP:
                            S.op("pool", lambda e, p=p: e.tensor_copy(accP[i][:], ptl[p][:]), reads=[b_pt[p]], writes=[b_accP[i]])
                        else:
                            S.op("pool", lambda e, p=p: e.tensor_tensor(accP[i][:], accP[i][:], ptl[p][:], ALU.add),
                                 reads=[b_pt[p]], writes=[b_accP[i]])
                        usedP = True
                    else:
                        if j == 0:
                            S.op("dve", lambda e, p=p: e.tensor_copy(accD[i][:], ptl[p][:]), reads=[b_pt[p]], writes=[b_accD[i]])
                        else:
                            S.op("dve", lambda e, p=p: e.tensor_tensor(accD[i][:], accD[i][:], ptl[p][:], ALU.add),
                                 reads=[b_pt[p]], writes=[b_accD[i]])
                    if j == min(3, nch - 1):
                        if pending_fin is not None:
                            finalize(*pending_fin)
                            pending_fin = None
                        if idx + 1 < len(tiles):
                            load_q(idx + 1)
                pending_fin = (idx, usedP)
            finalize(*pending_fin)

        with contextlib.ExitStack() as st:
            T, P = mk_alloc(st)
            nch = S_ // 128
            kn = [T(f"kn{i}", [128, S_], BF16) for i in range(2)]
            vv = [T(f"vv{i}", [128, nch, 128], BF16) for i in range(2)]
            krt = T("krt", [128, S_], BF16); b_kr = Buf()
            S.op("pool", lambda e: e.memset(krt[64:128, :], 0.0), writes=[b_kr])
            b_kn = [Buf(), Buf()]
            b_vv = [[Buf() for _ in range((nch + 15) // 16)] for _ in range(2)]
            S.dma("sp", krt[0:64, :], io["KR"], writes=[b_kr])

            def load_kv(h):
                i = h % 2
                S.dma("sp", kn[i][:], io["KN"][h], writes=[b_kn[i]])
                vsrc = io["VB"].rearrange("(j p) c -> p j c", p=128)
                for j0 in range(0, nch, 16):
                    S.dma("sp", vv[i][:, j0:j0 + 16, :], vsrc[:, j0:j0 + 16, h * 128:(h + 1) * 128], writes=[b_vv[i][j0 // 16]])

            load_kv(0)

            def prefetch(h):
                if h + 1 < 8:
                    load_kv(h + 1)

            dense_attention(
                T, P, 8, nch,
                kn_of=lambda h, j: kn[h % 2][:, j * 128:(j + 1) * 128],
                kr_of=lambda h, j: krt[:, j * 128:(j + 1) * 128],
                v_of=lambda h, j: vv[h % 2][:, j, :],
                kv_bufs_of=lambda h: [b_kn[h % 2], b_kr] + b_vv[h % 2],
                q_src=io["QN"], qr_src=io["QR"], g_src=io["GB"],
                y_dst=lambda h: io["YT"][4 + h], prefetch=prefetch)
            S.barrier()

        with contextlib.ExitStack() as st:
            T, P = mk_alloc(st)
            dense_attention(
                T, P, 4, 2,
                kn_of=lambda h, j: mkT[:, h, j * 128:(j + 1) * 128],
                kr_of=None,
                v_of=lambda h, j: mv[:, j, h * 128:(h + 1) * 128],
                kv_bufs_of=lambda h: [b_mk, b_mv],
                q_src=io["QC"], qr_src=None, g_src=io["GC"],
                y_dst=lambda h: io["YT"][12 + h], prefetch=None)
            S.barrier()

        with contextlib.ExitStack() as st:
            T, P = mk_alloc(st)
            bias = T("bias", [128, 48, 128], F32); b_bias = Buf()
            S.dma("sp", bias[:], io["bias"], writes=[b_bias])
            ka = [T(f"ka{i}", [128, Tw], BF16) for i in range(2)]
            b_ka = [Buf(), Buf()]
            qa2 = [[T(f"qa{j}_{i}", [128, Tq], BF16) for i in range(3)] for j in range(2)]
            b_qa2 = [[Buf() for _ in range(3)] for _ in range(2)]
            ga2 = [T(f"ga{j}", [128, Tq], BF16) for j in range(2)]
            b_ga2 = [Buf(), Buf()]

            def load_head(hh):
                j = hh % 2
                if hh > 0:
                    S.dma("sp", ka[j][:], io["KA"][hh], writes=[b_ka[j]])
                for g_ in range(3):
                    S.dma("sp", qa2[j][g_][:], io["QA"][g_ * 4 + hh], writes=[b_qa2[j][g_]])
                S.dma("sp", ga2[j][:], io["GA"][hh], writes=[b_ga2[j]])
            maxch = Tq // 128 + 1
            NV = 3
            vr = [T(f"vr{i}", [128, maxch, 128], BF16) for i in range(NV)]
            b_vr = [[Buf(), Buf()] for _ in range(NV)]
            accO = T("accO", [128, Tq], F32); b_aO = Buf()
            accD = T("accD", [128, Tq], F32); b_aD = Buf()
            sA = [T(f"sA{i}", [128, 2, 512], F32) for i in range(2)]
            pA = [T(f"pA{i}", [128, 2, 512], BF16) for i in range(2)]
            b_sA, b_pA = [[Buf(), Buf()], [Buf(), Buf()]], [[Buf(), Buf()], [Buf(), Buf()]]
            yo = T("yo3", [128, Tq], BF16); b_yo = Buf()
            stp = [[P(f"stp{i}_{c}", [128, 512], F32) for c in range(2)] for i in range(2)]
            otp = [P(f"otp{i}", [128, 512], F32) for i in range(2)]
            dnp = [P(f"dnp{i}", [128, 512], F32) for i in range(2)]
            b_stp, b_otp, b_dnp = [[Buf(), Buf()], [Buf(), Buf()]], [Buf(), Buf()], [Buf(), Buf()]

            def bc(ap2, nb):
                return bass.AP(ap2.tensor, ap2.offset, [list(ap2.ap[0]), [0, nb], list(ap2.ap[1])])

            va_t = io["VA"].tensor
            S.dma("sp", ka[0][:], io["KA"][0], writes=[b_ka[0]])
            vctr = [0]
            bctr = [0]
            for h in range(4):
                kh = ka[h % 2]
                b_kh = b_ka[h % 2]
                if h == 0:
                    load_head(0)
                if h + 1 < 4:
                    load_head(h + 1)
                qa, b_qa = qa2[h % 2], b_qa2[h % 2]
                ga, b_ga = ga2[h % 2], b_ga2[h % 2]
                res = []
                for g, d in enumerate(DILS):
                    nl = Tq // d
                    nq = min(128, nl)
                    nun = nl // nq
                    for r in range(d):
                        res.append(dict(g=g, d=d, r=r, nq=nq, nun=nun, lq0=HALO // d))

                def load_v(ri):
                    R_ = res[ri]
                    vi = vctr[0] % NV; vctr[0] += 1
                    R_["vi"] = vi
                    d, r, nq, nun, lq0 = R_["d"], R_["r"], R_["nq"], R_["nun"], R_["lq0"]
                    nchk = nun + 1
                    base = ((lq0 - 64) * d + r) * 512 + h * 128
                    if nchk > 1:
                        S.dma("sp", vr[vi][:, 0:nchk - 1, :],
                              bass.AP(va_t, base, [[d * 512, 128], [128 * d * 512, nchk - 1], [1, 128]]),
                              writes=[b_vr[vi][0]])
                    S.dma("sp", vr[vi][0:nq, nchk - 1, :],
                          bass.AP(va_t, base + (nchk - 1) * 128 * d * 512, [[d * 512, nq], [1, 128]]),
                          writes=[b_vr[vi][1]])

                batches = []
                for ri, R_ in enumerate(res):
                    for u0 in range(0, R_["nun"], 4):
                        batches.append(dict(ri=ri, u0=u0, nb=min(4, R_["nun"] - u0)))

                def qk(B):
                    R_ = res[B["ri"]]
                    g, d, r, nq, nun, lq0 = R_["g"], R_["d"], R_["r"], R_["nq"], R_["nun"], R_["lq0"]
                    k = bctr[0] % 2; bctr[0] += 1
                    B["k"] = k
                    for b in range(B["nb"]):
                        u = B["u0"] + b
                        l0 = lq0 + nq * u
                        qstart = l0 * d + r - HALO
                        qsl = slice(qstart, qstart + d * (nq - 1) + 1, d)
                        kA0 = (l0 - 64) * d + r
                        kB0 = (l0 + 64) * d + r
                        kAs = slice(kA0, kA0 + d * 127 + 1, d)
                        kBs = slice(kB0, kB0 + d * (nq - 1) + 1, d)
                        last = (b == B["nb"] - 1)
                        S.op("pe", lambda e: e.matmul(stp[k][0][:, b * 128:b * 128 + nq], lhsT=kh[:, kAs], rhs=qa[g][:, qsl], start=True, stop=True),
                             reads=[b_kh, b_qa[g]], writes=[b_stp[k][0]], inc=last)
                        S.op("pe", lambda e: e.matmul(stp[k][1][0:nq, b * 128:b * 128 + nq], lhsT=kh[:, kBs], rhs=qa[g][:, qsl], start=True, stop=True),
                             reads=[b_kh, b_qa[g]], writes=[b_stp[k][1]], inc=last)

                def softmax_part(B):
                    R_ = res[B["ri"]]
                    g, nq, nun = R_["g"], R_["nq"], R_["nun"]
                    k, nb, u0 = B["k"], B["nb"], B["u0"]
                    gh = g * 4 + h
                    W = nb * 128 if nq == 128 else nq

                    def v3(ap2):
                        return ap2.rearrange("p (b q) -> p b q", q=128) if nq == 128 else ap2

                    for c, (var_n, var_e, rows) in enumerate(((0, 1, slice(0, 128)), (2, 3, slice(0, nq)))):
                        bias_n = bias[rows, gh * 4 + var_n, 0:nq]
                        src = stp[k][c][rows, 0:W]
                        dst = sA[k][rows, c, 0:W]
                        if nq == 128:
                            S.op("dve", lambda e: e.tensor_tensor(v3(dst), v3(src), bc(bias_n, nb), ALU.add),
                                 reads=[b_stp[k][c], b_bias], writes=[b_sA[k][c]])
                        else:
                            S.op("dve", lambda e: e.tensor_tensor(dst, src, bias_n, ALU.add),
                                 reads=[b_stp[k][c], b_bias], writes=[b_sA[k][c]])
                        if c == 0 and u0 == 0:
                            S.op("dve", lambda e: e.tensor_tensor(sA[k][rows, c, 0:nq], stp[k][c][rows, 0:nq], bias[rows, gh * 4 + var_e, 0:nq], ALU.add),
                                 reads=[b_stp[k][c], b_bias], writes=[b_sA[k][c]])
                        if c == 1 and u0 + nb == nun:
                            o = (nb - 1) * 128
                            S.op("dve", lambda e: e.tensor_tensor(sA[k][rows, c, o:o + nq], stp[k][c][rows, o:o + nq], bias[rows, gh * 4 + var_e, 0:nq], ALU.add),
                                 reads=[b_stp[k][c], b_bias], writes=[b_sA[k][c]])
                        S.op("act", lambda e: e.activation(pA[k][rows, c, 0:W], sA[k][rows, c, 0:W], AF.Exp),
                             reads=[b_sA[k][c]], writes=[b_pA[k][c]])

                def pv(B):
                    R_ = res[B["ri"]]
                    g, d, r, nq, nun, lq0, vi = R_["g"], R_["d"], R_["r"], R_["nq"], R_["nun"], R_["lq0"], R_["vi"]
                    k, nb, u0 = B["k"], B["nb"], B["u0"]
                    W = nb * 128 if nq == 128 else nq
                    for b in range(nb):
                        u = u0 + b
                        cs_ = slice(b * 128, b * 128 + nq)
                        S.op("pe", lambda e: e.matmul(otp[k][:, cs_], lhsT=vr[vi][:, u, :], rhs=pA[k][:, 0, cs_], start=True, stop=False),
                             reads=b_vr[vi] + [b_pA[k][0]], writes=[b_otp[k]], inc=False)
                        S.op("pe", lambda e: e.matmul(otp[k][:, cs_], lhsT=vr[vi][0:nq, u + 1, :], rhs=pA[k][0:nq, 1, cs_], start=False, stop=True),
                             reads=b_vr[vi] + [b_pA[k][1]], writes=[b_otp[k]], inc=(b == nb - 1))
                    for b in range(nb):
                        cs_ = slice(b * 128, b * 128 + nq)
                        S.op("pe", lambda e: e.matmul(dnp[k][:, cs_], lhsT=ones_b[:], rhs=pA[k][:, 0, cs_], start=True, stop=False),
                             reads=[b_1b, b_pA[k][0]], writes=[b_dnp[k]], inc=False)
                        S.op("pe", lambda e: e.matmul(dnp[k][:, cs_], lhsT=ones_b[0:nq, :], rhs=pA[k][0:nq, 1, cs_], start=False, stop=True),
                             reads=[b_1b, b_pA[k][1]], writes=[b_dnp[k]], inc=(b == nb - 1))
                    l0 = lq0 + nq * u0
                    qstart = l0 * d + r - HALO
                    nqt = nb * nq
                    qsl = slice(qstart, qstart + d * (nqt - 1) + 1, d)
                    if g == 0:
                        S.op("dve", lambda e: e.tensor_copy(accO[:, qsl], otp[k][:, 0:W]), reads=[b_otp[k]], writes=[b_aO])
                        S.op("dve", lambda e: e.tensor_copy(accD[:, qsl], dnp[k][:, 0:W]), reads=[b_dnp[k]], writes=[b_aD])
                    else:
                        S.op("dve", lambda e: e.tensor_tensor(accO[:, qsl], accO[:, qsl], otp[k][:, 0:W], ALU.add),
                             reads=[b_otp[k]], writes=[b_aO])
                        S.op("dve", lambda e: e.tensor_tensor(accD[:, qsl], accD[:, qsl], dnp[k][:, 0:W], ALU.add),
                             reads=[b_dnp[k]], writes=[b_aD])

                load_v(0)
                if len(res) > 1:
                    load_v(1)
                qk(batches[0])
                for bi, B in enumerate(batches):
                    if B["u0"] == 0 and B["ri"] + 2 < len(res):
                        load_v(B["ri"] + 2)
                    softmax_part(B)
                    if bi + 1 < len(batches):
                        qk(batches[bi + 1])
                    pv(B)
                S.op("dve", lambda e: e.reciprocal(accD[:], accD[:]), reads=[], writes=[b_aD])
                S.op("dve", lambda e: e.tensor_tensor(accO[:], accO[:], accD[:], ALU.mult), reads=[b_aD], writes=[b_aO])
                S.op("pool", lambda e: e.tensor_tensor(yo[:], accO[:], ga[:], ALU.mult), reads=[b_aO, b_ga], writes=[b_yo])
                S.dma("pool", io["YT"][h], yo[:], reads=[b_yo])
            S.barrier()

        with contextlib.ExitStack() as st:
            T, P = mk_alloc(st)
            wo = T("wo", [128, 16, D], BF16); b_wo = [Buf() for _ in range(4)]
            pg = T("pg", [128, D], F32); b_pg = Buf()
            for cb in range(4):
                S.dma("sp", wo[:, :, cb * 512:(cb + 1) * 512], WO[:, cb * 512:(cb + 1) * 512].rearrange("(kc p) c -> p kc c", p=128), writes=[b_wo[cb]])
            S.dma("sp", pg[:], G["pgain"], writes=[b_pg])
            yt = [T(f"yt{i}", [128, 16, 512], BF16) for i in range(2)]
            b_yt = [Buf(), Buf()]
            xt = [T(f"xt{i}", [128, D], F32) for i in range(2)]
            b_xt = [Buf(), Buf()]
            yf = [T(f"yf{i}", [128, D], F32) for i in range(2)]
            b_yf = [Buf(), Buf()]
            jk = T("jk", [128, D], BF16)
            ss = [T(f"ssc{i}", [128, 1], F32) for i in range(2)]
            rs = [T(f"rsc{i}", [128, 1], F32) for i in range(2)]
            b_ss, b_rs = [Buf(), Buf()], [Buf(), Buf()]
            ob = [T(f"ob{i}", [128, D], F32) for i in range(2)]
            b_ob = [Buf(), Buf()]
            po = [P(f"po{i}", [128, D], F32) for i in range(2)]
            b_po = [Buf(), Buf()]
            ntile = Tq // 128

            def load_y(blk):
                i = blk % 2
                S.dma("sp", yt[i][:], io["YT"][:, :, blk * 512:(blk + 1) * 512].rearrange("c p t -> p c t"), writes=[b_yt[i]])

            load_y(0)
            for tl in range(ntile):
                blk, sub = tl // 4, tl % 4
                if sub == 0 and blk + 1 < Tq // 512:
                    load_y(blk + 1)
                yi = blk % 2
                i = tl % 2
                S.dma("sp", xt[i][:], io["xw"][HALO + tl * 128: HALO + (tl + 1) * 128, :], writes=[b_xt[i]])
                for cb in range(4):
                    for kc in range(16):
                        S.op("pe", lambda e, i=i, yi=yi, cb=cb, kc=kc, sub=sub: e.matmul(
                            po[i][:, cb * 512:(cb + 1) * 512], lhsT=yt[yi][:, kc, sub * 128:(sub + 1) * 128],
                            rhs=wo[:, kc, cb * 512:(cb + 1) * 512], start=(kc == 0), stop=(kc == 15)),
                            reads=[b_yt[yi], b_wo[cb]], writes=[b_po[i]], inc=(kc == 15 and cb == 3))
                for cb in range(4):
                    S.op("act", lambda e, i=i, cb=cb: e.activation(yf[i][:, cb * 512:(cb + 1) * 512], po[i][:, cb * 512:(cb + 1) * 512], AF.Copy),
                         reads=[b_po[i]], writes=[b_yf[i]])
                S.op("act", lambda e, i=i: e.activation(jk[:], yf[i][:], AF.Square, scale=RS, accum_out=ss[i][:]),
                     reads=[b_yf[i]], writes=[b_ss[i]])
                S.op("act", lambda e, i=i: e.activation(rs[i][:], ss[i][:], AF.Ln, bias=epsb[:]),
                     reads=[b_ss[i], b_eps], writes=[b_rs[i]])
                S.op("act", lambda e, i=i: e.activation(rs[i][:], rs[i][:], AF.Exp, scale=-0.5),
                     reads=[b_rs[i]], writes=[b_rs[i]])
                S.op("dve", lambda e, i=i: e.scalar_tensor_tensor(ob[i][:], yf[i][:], rs[i][:], pg[:], ALU.mult, ALU.mult),
                     reads=[b_yf[i], b_rs[i], b_pg], writes=[b_ob[i]])
                S.op("pool", lambda e, i=i: e.tensor_tensor(ob[i][:], ob[i][:], xt[i][:], ALU.add),
                     reads=[b_xt[i]], writes=[b_ob[i]])
                S.dma("pool", io["y"][tl * 128:(tl + 1) * 128, :], ob[i][:], reads=[b_ob[i]])
            S.barrier()


def _t5_bucket(rel):
    nb = 16
    max_exact = 8
    bucket = np.where(rel > 0, nb, 0)
    n = np.abs(rel)
    nf = np.maximum(n, 1).astype(np.float32)
    large = max_exact + (np.log(nf / np.float32(max_exact)) / np.float32(math.log(1024 / max_exact))
                         * np.float32(nb - max_exact)).astype(np.int32)
    large = np.minimum(large, nb - 1)
    return bucket + np.where(n < max_exact, n, large)


def _bias_tiles(rel_bias, Tq, at_start, at_end):
    out = np.empty((128, 48, 128), np.float32)
    kk = np.arange(128)[:, None]
    qq = np.arange(128)[None, :]
    for g, d in enumerate(DILS):
        nq = min(128, Tq // d)
        for var in range(4):
            rel = (kk - 64 - qq) if var < 2 else (kk + 64 - qq)
            valid = np.abs(rel) <= 64
            if var == 1 and at_start:
                valid = valid & (kk >= 64)
            if var == 3 and at_end:
                valid = valid & (kk < nq - 64)
            bidx = _t5_bucket(rel * d)
            for h in range(4):
                gh = g * 4 + h
                vals = rel_bias[bidx, gh]
                out[:, gh * 4 + var, :] = np.where(valid, vals, np.float32(NEG))
    return out


def _rope_tables(pos, scale):
    inv = (10000.0 ** (-np.arange(0, 64, 2, dtype=np.float32) / np.float32(64))).astype(np.float32)
    ang = (pos.astype(np.float32)[None, :] * inv[:, None]).astype(np.float32)
    c = np.cos(ang).astype(np.float32)
    s = np.sin(ang).astype(np.float32)
    cos = np.concatenate([c, c], 0)
    sin = np.concatenate([-s, s], 0)
    return (np.stack([cos, sin], 0) * np.float32(scale)).astype(np.float32)


def make_in_maps(inputs, jobs=JOBS, ncores=NCORES):
    f = lambda a: np.ascontiguousarray(np.asarray(a, dtype=np.float32))
    gains = np.concatenate([
        f(inputs["pre_gain"]).reshape(16, 128).T, f(inputs["q_gain"]).reshape(4, 128).T,
        f(inputs["kv_gain"]).reshape(4, 128).T, f(inputs["mem_gain"]).reshape(16, 128).T], axis=1)
    shared = dict(
        w_in=f(inputs["w_in"][0]), w_uq=f(inputs["w_uq"][0]), w_ukv=f(inputs["w_ukv"][0]),
        w_mem_kv=f(inputs["w_mem_kv"][0]), w_out=f(inputs["w_out"][0]),
        gains=np.ascontiguousarray(gains), pgain=np.ascontiguousarray(np.broadcast_to(f(inputs["post_gain"]).reshape(1, D), (128, D))),
        ident=np.eye(128, dtype=np.float32))
    rel_bias = f(inputs["rel_bias"])
    xs = dict(p=f(inputs["x_prompt"]), s=f(inputs["x_sample"]))
    mems = dict(p=f(inputs["mem_prompt"]), s=f(inputs["mem_sample"]))
    maps = []
    for c in range(ncores):
        m = dict(shared)
        for jb in jobs:
            n, S_, Tq, per = jb["name"], jb["S"], jb["Tq"], jb["per_seq"]
            b = c // per
            q0 = (c % per) * Tq
            x = xs[n][b]
            m[f"xkv_{n}"] = np.roll(x, -(q0 - HALO), axis=0)
            m[f"mem_{n}"] = mems[n][b]
            m[f"ropeq_{n}"] = _rope_tables(np.arange(q0, q0 + Tq), 192.0 ** -0.5)
            m[f"ropek_{n}"] = _rope_tables((np.arange(S_) + q0 - HALO) % S_, 1.0)
            m[f"bias_{n}"] = _bias_tiles(rel_bias, Tq, q0 == 0, q0 + Tq == S_)
        maps.append(m)
    return maps


_NC_CACHE = {}


def kernel(**inputs):
    jobs = JOBS
    if "nc" not in _NC_CACHE:
        _NC_CACHE["nc"] = build_program(jobs)
    nc = _NC_CACHE["nc"]
    maps = make_in_maps(inputs, jobs)
    res = run_bass_kernel_spmd(nc, maps, core_ids=list(range(NCORES)))
    outs = []
    for jb, key in zip(jobs, ("x_prompt", "x_sample")):
        n, Tq, per = jb["name"], jb["Tq"], jb["per_seq"]
        shp = np.asarray(inputs[key]).shape
        y = np.empty(shp, np.float32)
        for c in range(NCORES):
            b = c // per
            q0 = (c % per) * Tq
            y[b, q0:q0 + Tq] = res.results[c][f"y_{n}"]
        outs.append(y)
    return tuple(outs)
```

```python
import contextlib
import math
import numpy as np
import concourse.bass as bass
import concourse.mybir as mybir
from concourse.bass_utils import run_bass_kernel_spmd

F32 = mybir.dt.float32
BF16 = mybir.dt.bfloat16
AF = mybir.ActivationFunctionType
ALU = mybir.AluOpType

D = 2048
DIN = 6208
HALO = 1024
DILS = (1, 4, 16)
NCORES = 8
JOBS = (dict(name="p", S=8192, Tq=4096, per_seq=2), dict(name="s", S=4096, Tq=1024, per_seq=4))
NEG = -1e30


class Buf:
    __slots__ = ("w", "r")

    def __init__(self):
        self.w = None
        self.r = []


class Sched:
    def __init__(self, nc, stack):
        self.nc = nc
        self.eng = {"pe": nc.tensor, "act": nc.scalar, "dve": nc.vector, "pool": nc.gpsimd, "sp": nc.sync}
        self.sem = {}
        self.cnt = {}
        self.pending = {}
        for k in ("pe", "act", "dve", "pool"):
            self.sem[k] = stack.enter_context(nc.semaphore("s_" + k))
            self.cnt[k] = 0
            self.pending[k] = False
        self.dq = {}
        for k, n in (("sp", 16), ("pool", 8), ("act", 8)):
            sems = [stack.enter_context(nc.semaphore(f"d_{k}{i}")) for i in range(n)]
            self.dq[k] = dict(sems=sems, n=0)
        self.waited = {k: {} for k in self.eng}
        self.ninst = 0

    def _need(self, e, tok, out):
        if tok is None:
            return
        key, sem, val, src = tok
        if src == "pe" and e == "pe":
            return
        if self.waited[e].get(key, 0) >= val:
            return
        cur = out.get(key)
        if cur is None or cur[1] < val:
            out[key] = (sem, val)

    def _deps(self, e, reads, writes):
        need = {}
        for b in reads:
            self._need(e, b.w, need)
        for b in writes:
            self._need(e, b.w, need)
            for t in b.r:
                self._need(e, t, need)
        return need

    def _commit(self, tok, reads, writes):
        for b in reads:
            b.r.append(tok)
            if len(b.r) > 48:
                best = {}
                for t in b.r:
                    if t[0] not in best or best[t[0]][2] < t[2]:
                        best[t[0]] = t
                b.r = list(best.values())
        for b in writes:
            b.w = tok
            b.r = []

    def op(self, e, fn, reads=(), writes=(), inc=True):
        need = self._deps(e, reads, writes)
        eng = self.eng[e]
        for key, (sem, val) in need.items():
            self.waited[e][key] = val
            eng.wait_ge(sem, val)
        self.ninst += 1
        sem = self.sem[e]
        if inc:
            self.cnt[e] += 1
            val = self.cnt[e]
            fn(eng).then_inc(sem, 1)
            self.pending[e] = False
        else:
            val = self.cnt[e] + 1
            fn(eng)
            self.pending[e] = True
        tok = ("c_" + e, sem, val, e)
        self._commit(tok, reads, writes)
        return tok

    def dma(self, q, out, in_, reads=(), writes=(), **kw):
        need = self._deps(q, reads, writes)
        d = self.dq[q]
        n = d["n"]
        d["n"] += 1
        K = len(d["sems"])
        slot = n % K
        sem = d["sems"][slot]
        key = f"d_{q}{slot}"
        prev = 16 * (n // K)
        val = prev + 16
        if prev > 0 and self.waited[q].get(key, 0) < prev:
            need[key] = (sem, prev)
        eng = self.eng[q]
        for k2, (s, v) in need.items():
            self.waited[q][k2] = v
            eng.wait_ge(s, v)
        self.ninst += 1
        eng.dma_start(out=out, in_=in_, **kw).then_inc(sem, 16)
        tok = (key, sem, val, None)
        self._commit(tok, reads, writes)
        return tok

    def barrier(self):
        for k in self.pending:
            assert not self.pending[k], k
        toks = []
        for k in ("pe", "act", "dve", "pool"):
            if self.cnt[k] > 0:
                toks.append(("c_" + k, self.sem[k], self.cnt[k]))
        for q, d in self.dq.items():
            K = len(d["sems"])
            for slot in range(K):
                cnt = (d["n"] - slot + K - 1) // K if d["n"] > slot else 0
                if cnt > 0:
                    toks.append((f"d_{q}{slot}", d["sems"][slot], 16 * cnt))
        for e, eng in self.eng.items():
            for key, sem, val in toks:
                if key == "c_" + e:
                    continue
                if self.waited[e].get(key, 0) < val:
                    self.waited[e][key] = val
                    eng.wait_ge(sem, val)


def build_program(jobs=JOBS, debug=()):
    nc = bass.Bass("TRN2", target_bir_lowering=False)

    def din(name, shape, dt=F32):
        return nc.dram_tensor(name, list(shape), dt, kind="ExternalInput").ap()

    def dscr(name, shape, dt=BF16):
        kind = "ExternalOutput" if name in debug else "Internal"
        return nc.dram_tensor(name, list(shape), dt, kind=kind).ap()

    w_in = din("w_in", [D, DIN])
    w_uq = din("w_uq", [512, 1536])
    w_ukv = din("w_ukv", [512, 2048])
    w_mem = din("w_mem_kv", [D, 1024])
    w_out = din("w_out", [D, D])
    gains = din("gains", [128, 40])
    pgain = din("pgain", [128, D])
    ident = din("ident", [128, 128])

    WI = dscr("WI", [D, 6272])
    WQ = dscr("WQ", [512, 2048])
    WKV = dscr("WKV", [512, 2048])
    WM = dscr("WM", [D, 1024])
    WO = dscr("WO", [D, D])

    jio = []
    for jb in jobs:
        n, S_, Tq = jb["name"], jb["S"], jb["Tq"]
        Tw = Tq + 2 * HALO
        jio.append(dict(
            xkv=din(f"xkv_{n}", [S_, D]), mem=din(f"mem_{n}", [256, D]),
            ropeq=din(f"ropeq_{n}", [2, 64, Tq]), ropek=din(f"ropek_{n}", [2, 64, S_]),
            bias=din(f"bias_{n}", [128, 48, 128]),
            y=nc.dram_tensor(f"y_{n}", [Tq, D], F32, kind="ExternalOutput").ap(),
            QA=dscr(f"QA_{n}", [12, 128, Tq]), KA=dscr(f"KA_{n}", [4, 128, Tw]), VA=dscr(f"VA_{n}", [Tw, 512]),
            GA=dscr(f"GA_{n}", [4, 128, Tq]), QN=dscr(f"QN_{n}", [8, 128, Tq]), QR=dscr(f"QR_{n}", [8, 64, Tq]),
            KN=dscr(f"KN_{n}", [8, 128, S_]), KR=dscr(f"KR_{n}", [64, S_]), VB=dscr(f"VB_{n}", [S_, 1024]),
            GB=dscr(f"GB_{n}", [8, 128, Tq]), QC=dscr(f"QC_{n}", [4, 128, Tq]), GC=dscr(f"GC_{n}", [4, 128, Tq]),
            YT=dscr(f"YT_{n}", [16, 128, Tq]),
        ))

    with contextlib.ExitStack() as gst:
        S = Sched(nc, gst)

        uid = [0]

        def mk_alloc(st):
            def T(name, shape, dt):
                uid[0] += 1
                return st.enter_context(nc.sbuf_tensor(f"{name}_{uid[0]}", list(shape), dt))

            def P(name, shape, dt):
                uid[0] += 1
                return st.enter_context(nc.psum_tensor(f"{name}_{uid[0]}", list(shape), dt))
            return T, P

        GT, _ = mk_alloc(gst)
        identb = GT("identb", [128, 128], BF16); b_id = Buf()
        ones_f = GT("ones_f", [128, 128], F32); b_1f = Buf()
        ones_b = GT("ones_b", [128, 128], BF16); b_1b = Buf()
        epsb = GT("epsb", [128, 1], F32); b_eps = Buf()
        S.dma("pool", identb[:], ident, writes=[b_id])
        S.op("pool", lambda e: e.memset(ones_f[:], 1.0), writes=[b_1f])
        S.op("pool", lambda e: e.memset(ones_b[:], 1.0), writes=[b_1b])
        S.op("pool", lambda e: e.memset(epsb[:], 1e-6), writes=[b_eps])

        with contextlib.ExitStack() as st:
            T, P = mk_alloc(st)
            g = T("g", [128, 40], F32); b_g = Buf()
            S.dma("sp", g[:], gains, writes=[b_g])
            wf = [T(f"wf{i}", [128, DIN], F32) for i in range(2)]
            wb = [T(f"wb{i}", [128, 6272], BF16) for i in range(2)]
            b_wf = [Buf(), Buf()]
            b_wb = [Buf(), Buf()]
            cnt = [0]

            def conv(src, dst, nrows, nin, nout, gcol, copies):
                for kc in range(nrows // 128):
                    i = cnt[0] % 2
                    cnt[0] += 1
                    S.dma("sp", wf[i][:, 0:nin], src[kc * 128:(kc + 1) * 128, :], writes=[b_wf[i]])
                    for (dfn, sfn) in copies:
                        if gcol is None:
                            S.op("act", lambda e, i=i, dfn=dfn, sfn=sfn: e.activation(dfn(wb[i]), sfn(wf[i]), AF.Copy),
                                 reads=[b_wf[i]], writes=[b_wb[i]])
                        else:
                            S.op("act", lambda e, i=i, dfn=dfn, sfn=sfn, c=gcol + kc: e.activation(
                                dfn(wb[i]), sfn(wf[i]), AF.Copy, scale=g[:, c:c + 1]),
                                reads=[b_wf[i], b_g], writes=[b_wb[i]])
                    S.dma("pool", dst[kc * 128:(kc + 1) * 128, :], wb[i][:, 0:nout], reads=[b_wb[i]])

            conv(w_in, WI, D, DIN, 6272, 0, [
                (lambda t: t[:, 0:DIN], lambda t: t[:, 0:DIN]),
                (lambda t: t[:, 6208:6240], lambda t: t[:, 4128:4160]),
                (lambda t: t[:, 6240:6272], lambda t: t[:, 4096:4128]),
            ])
            qv = lambda t: t[:, 0:1536].rearrange("p (h f) -> p h f", f=192)
            conv(w_uq, WQ, 512, 1536, 2048, 16, [
                (lambda t: t[:, 0:1024].rearrange("p (h e) -> p h e", e=128), lambda t: qv(t)[:, :, 0:128]),
                (lambda t: t[:, 1024:1536].rearrange("p (h e) -> p h e", e=64), lambda t: qv(t)[:, :, 128:192]),
                (lambda t: t[:, 1536:2048].rearrange("p (h e) -> p h e", e=64)[:, :, 0:32], lambda t: qv(t)[:, :, 160:192]),
                (lambda t: t[:, 1536:2048].rearrange("p (h e) -> p h e", e=64)[:, :, 32:64], lambda t: qv(t)[:, :, 128:160]),
            ])
            kvv = lambda t: t[:, 0:2048].rearrange("p (h t e) -> p h t e", t=2, e=128)
            conv(w_ukv, WKV, 512, 2048, 2048, 20, [
                (lambda t: t[:, 0:1024].rearrange("p (h e) -> p h e", e=128), lambda t: kvv(t)[:, :, 0, :]),
                (lambda t: t[:, 1024:2048].rearrange("p (h e) -> p h e", e=128), lambda t: kvv(t)[:, :, 1, :]),
            ])
            conv(w_mem, WM, D, 1024, 1024, 24, [(lambda t: t[:, 0:1024], lambda t: t[:, 0:1024])])
            conv(w_out, WO, D, D, D, None, [(lambda t: t[:, 0:D], lambda t: t[:, 0:D])])
            S.barrier()

        for jb, io in zip(jobs, jio):
            run_job(nc, S, mk_alloc, jb, io, dict(
                identb=identb, b_id=b_id, ones_f=ones_f, b_1f=b_1f, ones_b=ones_b, b_1b=b_1b, epsb=epsb, b_eps=b_eps,
                WI=WI, WQ=WQ, WKV=WKV, WM=WM, WO=WO, pgain=pgain))
        S.barrier()
        print("instructions:", S.ninst, {k: v for k, v in S.cnt.items()})
    return nc


def run_job(nc, S, mk_alloc, jb, io, G):
    S_, Tq = jb["S"], jb["Tq"]
    Tw = Tq + 2 * HALO
    assert S_ >= Tw
    io = dict(io)
    io["xw"] = io["xkv"][0:Tw, :]
    identb, b_id = G["identb"], G["b_id"]
    ones_f, b_1f, ones_b, b_1b = G["ones_f"], G["b_1f"], G["ones_b"], G["b_1b"]
    epsb, b_eps = G["epsb"], G["b_eps"]
    WI, WQ, WKV, WM, WO = G["WI"], G["WQ"], G["WKV"], G["WM"], G["WO"]
    RS = float(1.0 / math.sqrt(2048.0))

    with contextlib.ExitStack() as jst:
        JT, _ = mk_alloc(jst)
        mkT = JT("mkT", [128, 4, 256], BF16); b_mk = Buf()
        mv = JT("mv", [128, 2, 512], BF16); b_mv = Buf()

        def make_norm_T(T, P):
            xs = [T(f"xs{i}", [128, D], F32) for i in range(2)]
            hb = [T(f"hb{i}", [128, D], BF16) for i in range(2)]
            ss = [T(f"ss{i}", [128, 1], F32) for i in range(2)]
            rs = [T(f"rs{i}", [128, 1], F32) for i in range(2)]
            ptr = [P(f"ptr{i}", [128, 8, 128], BF16) for i in range(2)]
            b_xs, b_hb, b_ss, b_rs = [Buf(), Buf()], [Buf(), Buf()], [Buf(), Buf()], [Buf(), Buf()]
            b_ptr = [Buf() for _ in range(2)]
            ctr = [0, 0]

            def norm_a(src):
                b = ctr[0] % 2
                ctr[0] += 1
                S.dma("sp", xs[b][:], src, writes=[b_xs[b]])
                S.op("act", lambda e: e.activation(hb[b][:], xs[b][:], AF.Square, scale=RS, accum_out=ss[b][:]),
                     reads=[b_xs[b]], writes=[b_hb[b], b_ss[b]])
                S.op("act", lambda e: e.activation(rs[b][:], ss[b][:], AF.Ln, bias=epsb[:]),
                     reads=[b_ss[b], b_eps], writes=[b_rs[b]])
                S.op("act", lambda e: e.activation(rs[b][:], rs[b][:], AF.Exp, scale=-0.5),
                     reads=[b_rs[b]], writes=[b_rs[b]])
                S.op("act", lambda e: e.activation(hb[b][:], xs[b][:], AF.Copy, scale=rs[b][:]),
                     reads=[b_xs[b], b_rs[b]], writes=[b_hb[b]])
                return b

            def norm_b(b, hT, b_hT, col0):
                for g8 in range(2):
                    r = ctr[1] % 2
                    ctr[1] += 1
                    reg = ptr[r][:, 0:8, :]
                    for j in range(8):
                        kc = g8 * 8 + j
                        S.op("pe", lambda e: e.transpose(reg[:, j, :], hb[b][:, kc * 128:(kc + 1) * 128], identb[:]),
                             reads=[b_hb[b], b_id], writes=[b_ptr[r]], inc=(j == 7))
                    S.op("dve", lambda e: e.tensor_copy(hT[:, g8 * 8:(g8 + 1) * 8, col0:col0 + 128], reg),
                         reads=[b_ptr[r]], writes=[b_hT])

            def norm_T(src, ntok, hT, b_hT, col0=0):
                for i in range(ntok // 128):
                    b = norm_a(src[i * 128:(i + 1) * 128, :])
                    norm_b(b, hT, b_hT, col0 + i * 128)

            def norm_queue(src_of, hT, b_hT, ntiles):
                slots = {}
                ents = []

                def mk_a(i):
                    def f():
                        slots[i] = norm_a(src_of(i))
                    return f

                def mk_b(i):
                    def f():
                        norm_b(slots[i], hT, b_hT, i * 128)
                    return f
                ents.append(mk_a(0))
                for i in range(ntiles):
                    if i + 1 < ntiles:
                        ents.append(mk_a(i + 1))
                    ents.append(mk_b(i))
                return ents
            norm_T.queue = norm_queue
            return norm_T

        with contextlib.ExitStack() as st:
            T, P = mk_alloc(st)
            norm_T = make_norm_T(T, P)
            hTm = T("hTm", [128, 16, 256], BF16); b_hTm = Buf()
            wm = T("wm", [128, 16, 1024], BF16); b_wm = Buf()
            pm = [P(f"pm{i}", [128, 512], F32) for i in range(2)]
            b_pm = [Buf(), Buf()]
            S.dma("sp", wm[:], WM.rearrange("(kc p) c -> p kc c", p=128), writes=[b_wm])
            norm_T(io["mem"], 256, hTm, b_hTm)
            k = 0
            for h in range(4):
                i = k % 2; k += 1
                for kc in range(16):
                    S.op("pe", lambda e, i=i, h=h, kc=kc: e.matmul(pm[i][:, 0:256], lhsT=wm[:, kc, h * 128:(h + 1) * 128],
                                                                   rhs=hTm[:, kc, :], start=(kc == 0), stop=(kc == 15)),
                         reads=[b_wm, b_hTm], writes=[b_pm[i]], inc=(kc == 15))
                S.op("act", lambda e, i=i, h=h: e.activation(mkT[:, h, :], pm[i][:, 0:256], AF.Copy),
                     reads=[b_pm[i]], writes=[b_mk])
            for m in range(2):
                i = k % 2; k += 1
                for kc in range(16):
                    S.op("pe", lambda e, i=i, m=m, kc=kc: e.matmul(pm[i][:], lhsT=hTm[:, kc, m * 128:(m + 1) * 128],
                                                                   rhs=wm[:, kc, 512:1024], start=(kc == 0), stop=(kc == 15)),
                         reads=[b_wm, b_hTm], writes=[b_pm[i]], inc=(kc == 15))
                S.op("act", lambda e, i=i, m=m: e.activation(mv[:, m, :], pm[i][:], AF.Copy),
                     reads=[b_pm[i]], writes=[b_mv])
            S.barrier()

        TB = 1024

        class Pump:
            def __init__(self):
                self.q = []
                self.k = 0
                self.every = 1

            def tick(self, force=False):
                self.k += 1
                if self.q and (force or self.k % self.every == 0):
                    self.q.pop(0)()

            def drain(self):
                while self.q:
                    self.q.pop(0)()

        def rms_feat_common(cf, b_cf, cn, b_cn, sq, b_sq, px, b_px, rstd, b_rstd, nfeat):
            for kc in range(4):
                s2 = kc % 2
                S.op("act", lambda e, kc=kc, s2=s2: e.activation(sq[:, s2, :], cf[:, kc, :], AF.Square),
                     reads=[b_cf], writes=[b_sq[s2]])
                for half in range(2):
                    hs = slice(half * 512, (half + 1) * 512)
                    S.op("pe", lambda e, kc=kc, hs=hs, half=half, s2=s2: e.matmul(px[half][:], lhsT=ones_f[:], rhs=sq[:, s2, hs],
                                                                                 start=(kc == 0), stop=(kc == 3)),
                         reads=[b_1f, b_sq[s2]], writes=[b_px[half]])
            for half in range(2):
                hs = slice(half * 512, (half + 1) * 512)
                S.op("act", lambda e, hs=hs, half=half: e.activation(rstd[:, hs], px[half][:], AF.Ln, bias=epsb[:], scale=1.0 / nfeat),
                     reads=[b_px[half], b_eps], writes=[b_rstd])
                S.op("act", lambda e, hs=hs: e.activation(rstd[:, hs], rstd[:, hs], AF.Exp, scale=-0.5),
                     reads=[b_rstd], writes=[b_rstd])
            for kc in range(4):
                S.op("dve", lambda e, kc=kc: e.tensor_tensor(cn[:, kc, :], cf[:, kc, :], rstd[:], ALU.mult),
                     reads=[b_cf, b_rstd], writes=[b_cn])

        with contextlib.ExitStack() as st:
            T, P = mk_alloc(st)
            norm_T = make_norm_T(T, P)
            hTs = [T(f"hT{i}", [128, 16, TB], BF16) for i in range(2)]
            b_hTs = [Buf(), Buf()]
            wkin = T("wkin", [128, 16, 640], BF16); b_wkin = Buf()
            wkv = T("wkv", [128, 4, 2048], BF16); b_wkv = Buf()
            cf = T("cf", [128, 4, TB], F32); b_cf = Buf()
            sq = T("sq", [128, 2, TB], F32); b_sq = [Buf(), Buf()]
            cn = T("cn", [128, 4, TB], BF16); b_cn = Buf()
            rstd = T("rstd", [128, TB], F32); b_rstd = Buf()
            rcs = T("rcs", [64, 2, TB], F32); b_rcs = Buf()
            t1 = T("t1", [64, 512], F32); b_t1 = Buf()
            t2 = T("t2", [64, 512], F32); b_t2 = Buf()
            stg = [T(f"stg{i}", [128, TB], BF16) for i in range(4)]
            b_stg = [Buf() for _ in range(4)]
            vst = [T(f"vst{i}", [128, 1024], BF16) for i in range(2)]
            b_vst = [Buf(), Buf()]
            pp = [P(f"pp{i}", [128, 512], F32) for i in range(4)]
            b_pp = [Buf() for _ in range(4)]
            px = [P(f"px{i}", [128, 512], F32) for i in range(2)]
            b_px = [Buf(), Buf()]
            cs = dict(pp=0, stg=0, vst=0)
            pump = Pump()
            pump.every = 2

            S.dma("sp", wkin[:, :, 0:576], WI[:, 3584:4160].rearrange("(kc p) c -> p kc c", p=128), writes=[b_wkin])
            S.dma("sp", wkin[:, :, 576:640], WI[:, 6208:6272].rearrange("(kc p) c -> p kc c", p=128), writes=[b_wkin])
            S.dma("sp", wkv[:], WKV.rearrange("(kc p) c -> p kc c", p=128), writes=[b_wkv])

            def queue_kv(blk):
                hb_ = blk % 2
                return norm_T.queue(lambda i: io["xkv"][blk * TB + i * 128: blk * TB + (i + 1) * 128, :], hTs[hb_], b_hTs[hb_], TB // 128)

            nblk = S_ // TB
            pump.q = queue_kv(0)
            pump.drain()
            for blk in range(nblk):
                hT, b_hT = hTs[blk % 2], b_hTs[blk % 2]
                tsl = slice(blk * TB, (blk + 1) * TB)
                if blk + 1 < nblk:
                    pump.q = queue_kv(blk + 1)
                S.dma("sp", rcs[:], io["ropek"][:, :, tsl].rearrange("a p t -> p a t"), writes=[b_rcs])
                for sb in range(4):
                    for half in range(2):
                        hs = slice(half * 512, (half + 1) * 512)
                        pi = cs["pp"] % 4; cs["pp"] += 1
                        for kc in range(16):
                            S.op("pe", lambda e, pi=pi, sb=sb, kc=kc, hs=hs: e.matmul(
                                pp[pi][:], lhsT=wkin[:, kc, sb * 128:(sb + 1) * 128], rhs=hT[:, kc, hs],
                                start=(kc == 0), stop=(kc == 15)),
                                reads=[b_wkin, b_hT], writes=[b_pp[pi]], inc=(kc == 15))
                        S.op("act", lambda e, pi=pi, sb=sb, hs=hs: e.activation(cf[:, sb, hs], pp[pi][:], AF.Copy),
                             reads=[b_pp[pi]], writes=[b_cf])
                        pump.tick()
                rms_feat_common(cf, b_cf, cn, b_cn, sq, b_sq, px, b_px, rstd, b_rstd, 512.0)
                pump.tick(force=True)
                pump.tick(force=True)
                si = cs["stg"] % 4; cs["stg"] += 1
                for half in range(2):
                    hs = slice(half * 512, (half + 1) * 512)
                    for which, col in ((0, 512), (1, 576)):
                        for kc in range(16):
                            S.op("pe", lambda e, which=which, col=col, kc=kc, hs=hs: e.matmul(
                                px[which][0:64, :], lhsT=wkin[:, kc, col:col + 64], rhs=hT[:, kc, hs],
                                start=(kc == 0), stop=(kc == 15)),
                                reads=[b_wkin, b_hT], writes=[b_px[which]], inc=(kc == 15))
                    S.op("dve", lambda e, hs=hs: e.tensor_tensor(t1[:], px[0][0:64, :], rcs[:, 0, hs], ALU.mult),
                         reads=[b_px[0], b_rcs], writes=[b_t1])
                    S.op("dve", lambda e, hs=hs: e.tensor_tensor(t2[:], px[1][0:64, :], rcs[:, 1, hs], ALU.mult),
                         reads=[b_px[1], b_rcs], writes=[b_t2])
                    S.op("pool", lambda e, hs=hs, si=si: e.tensor_tensor(stg[si][0:64, hs], t1[:], t2[:], ALU.add),
                         reads=[b_t1, b_t2], writes=[b_stg[si]])
                    pump.tick()
                S.dma("pool", io["KR"][:, tsl], stg[si][0:64, :], reads=[b_stg[si]])
                for h in range(8):
                    si = cs["stg"] % 4; cs["stg"] += 1
                    for half in range(2):
                        hs = slice(half * 512, (half + 1) * 512)
                        pi = cs["pp"] % 4; cs["pp"] += 1
                        for kc in range(4):
                            S.op("pe", lambda e, pi=pi, h=h, kc=kc, hs=hs: e.matmul(
                                pp[pi][:], lhsT=wkv[:, kc, h * 128:(h + 1) * 128], rhs=cn[:, kc, hs],
                                start=(kc == 0), stop=(kc == 3)),
                                reads=[b_wkv, b_cn], writes=[b_pp[pi]], inc=(kc == 3))
                        if half == 0:
                            S.op("act", lambda e, pi=pi, si=si, hs=hs: e.activation(stg[si][:, hs], pp[pi][:], AF.Copy),
                                 reads=[b_pp[pi]], writes=[b_stg[si]])
                        else:
                            S.op("dve", lambda e, pi=pi, si=si, hs=hs: e.tensor_copy(stg[si][:, hs], pp[pi][:]),
                                 reads=[b_pp[pi]], writes=[b_stg[si]])
                        pump.tick()
                    S.dma("pool", io["KN"][h, :, tsl], stg[si][:], reads=[b_stg[si]])
                for tl in range(TB // 128):
                    vi = cs["vst"] % 2; cs["vst"] += 1
                    for hg in range(2):
                        pi = cs["pp"] % 4; cs["pp"] += 1
                        for kc in range(4):
                            S.op("pe", lambda e, pi=pi, hg=hg, kc=kc, tl=tl: e.matmul(
                                pp[pi][:], lhsT=cn[:, kc, tl * 128:(tl + 1) * 128],
                                rhs=wkv[:, kc, 1024 + hg * 512:1024 + (hg + 1) * 512], start=(kc == 0), stop=(kc == 3)),
                                reads=[b_wkv, b_cn], writes=[b_pp[pi]], inc=(kc == 3))
                        if hg == 0:
                            S.op("act", lambda e, pi=pi, vi=vi, hg=hg: e.activation(vst[vi][:, hg * 512:(hg + 1) * 512], pp[pi][:], AF.Copy),
                                 reads=[b_pp[pi]], writes=[b_vst[vi]])
                        else:
                            S.op("dve", lambda e, pi=pi, vi=vi, hg=hg: e.tensor_copy(vst[vi][:, hg * 512:(hg + 1) * 512], pp[pi][:]),
                                 reads=[b_pp[pi]], writes=[b_vst[vi]])
                        pump.tick()
                    r0 = blk * TB + tl * 128
                    S.dma("pool", io["VB"][r0:r0 + 128, :], vst[vi][:], reads=[b_vst[vi]])
                pump.drain()
            S.barrier()

        with contextlib.ExitStack() as st:
            T, P = mk_alloc(st)
            norm_T = make_norm_T(T, P)
            hTs = [T(f"hT{i}", [128, 16, TB], BF16) for i in range(2)]
            b_hTs = [Buf(), Buf()]
            wblk = [T(f"wblk{i}", [128, 16, 512], BF16) for i in range(2)]
            b_wblk = [Buf(), Buf()]
            wq = T("wq", [128, 4, 2048], BF16); b_wq = Buf()
            cf = T("cf", [128, 4, TB], F32); b_cf = Buf()
            sq = T("sq", [128, 2, TB], F32); b_sq = [Buf(), Buf()]
            cn = T("cn", [128, 4, TB], BF16); b_cn = Buf()
            rstd = T("rstd", [128, TB], F32); b_rstd = Buf()
            rcs = T("rcs", [64, 2, TB], F32); b_rcs = Buf()
            t1 = T("t1", [64, 512], F32); b_t1 = Buf()
            t2 = T("t2", [64, 512], F32); b_t2 = Buf()
            stg = [T(f"stg{i}", [128, TB], BF16) for i in range(4)]
            b_stg = [Buf() for _ in range(4)]
            vst = [T(f"vst{i}", [128, 512], BF16) for i in range(2)]
            b_vst = [Buf(), Buf()]
            pp = [P(f"pp{i}", [128, 512], F32) for i in range(4)]
            b_pp = [Buf() for _ in range(4)]
            px = [P(f"px{i}", [128, 512], F32) for i in range(2)]
            b_px = [Buf(), Buf()]
            cs = dict(pp=0, stg=0, vst=0, wb=0)
            pump = Pump()
            S.dma("sp", wq[:], WQ.rearrange("(kc p) c -> p kc c", p=128), writes=[b_wq])

            def load_wblk(c0):
                wi = cs["wb"] % 2; cs["wb"] += 1
                S.dma("sp", wblk[wi][:], WI[:, c0:c0 + 512].rearrange("(kc p) c -> p kc c", p=128), writes=[b_wblk[wi]])
                return wi

            def feat_sub(hT, b_hT, wi, sbi, evac):
                for half in range(2):
                    hs = slice(half * 512, (half + 1) * 512)
                    pi = cs["pp"] % 4; cs["pp"] += 1
                    for kc in range(16):
                        S.op("pe", lambda e, pi=pi, kc=kc, hs=hs: e.matmul(
                            pp[pi][:], lhsT=wblk[wi][:, kc, sbi * 128:(sbi + 1) * 128], rhs=hT[:, kc, hs],
                            start=(kc == 0), stop=(kc == 15)),
                            reads=[b_wblk[wi], b_hT], writes=[b_pp[pi]], inc=(kc == 15))
                    evac(pi, half, hs)
                pump.tick()

            def store_feat(dst, func, scale):
                si = cs["stg"] % 4; cs["stg"] += 1

                def ev(pi, half, hs):
                    S.op("act", lambda e: e.activation(stg[si][:, hs], pp[pi][:], func, scale=scale),
                         reads=[b_pp[pi]], writes=[b_stg[si]])
                    if half == 1:
                        S.dma("pool", dst, stg[si][:], reads=[b_stg[si]])
                return ev

            def tokmajor_v(hT, b_hT, wi, row0):
                for tl in range(TB // 128):
                    vi = cs["vst"] % 2; cs["vst"] += 1
                    pi = cs["pp"] % 4; cs["pp"] += 1
                    for kc in range(16):
                        S.op("pe", lambda e, pi=pi, kc=kc, tl=tl: e.matmul(
                            pp[pi][:], lhsT=hT[:, kc, tl * 128:(tl + 1) * 128], rhs=wblk[wi][:, kc, :],
                            start=(kc == 0), stop=(kc == 15)),
                            reads=[b_wblk[wi], b_hT], writes=[b_pp[pi]], inc=(kc == 15))
                    S.op("act", lambda e, pi=pi, vi=vi: e.activation(vst[vi][:], pp[pi][:], AF.Copy),
                         reads=[b_pp[pi]], writes=[b_vst[vi]])
                    S.dma("pool", io["VA"][row0 + tl * 128: row0 + (tl + 1) * 128, :], vst[vi][:], reads=[b_vst[vi]])
                    pump.tick()

            SA = float(128.0 ** -0.5)
            SB = float(192.0 ** -0.5)
            nblk_w = Tw // TB
            qb0 = HALO // TB
            qb1 = (HALO + Tq) // TB

            def queue_w(wblki):
                hb_ = wblki % 2
                return norm_T.queue(lambda i: io["xw"][wblki * TB + i * 128: wblki * TB + (i + 1) * 128, :], hTs[hb_], b_hTs[hb_], TB // 128)

            pump.q = queue_w(0)
            pump.drain()
            for wblki in range(nblk_w):
                hT, b_hT = hTs[wblki % 2], b_hTs[wblki % 2]
                row0 = wblki * TB
                wsl = slice(row0, row0 + TB)
                is_q = qb0 <= wblki < qb1
                if wblki + 1 < nblk_w:
                    pump.q = queue_w(wblki + 1)
                pump.k = 0
                pump.every = 3 if is_q else 1
                if is_q:
                    q0t = row0 - HALO
                    qsl = slice(q0t, q0t + TB)
                    S.dma("sp", rcs[:], io["ropeq"][:, :, qsl].rearrange("a p t -> p a t"), writes=[b_rcs])
                    cols = [3072, 0, 512, 1024, 1536, 2048, 5184, 2560, 4160, 4672, 5696]
                else:
                    cols = [1536, 2048]
                wi_next = load_wblk(cols[0])
                for ci, c0 in enumerate(cols):
                    wi = wi_next
                    if ci + 1 < len(cols):
                        wi_next = load_wblk(cols[ci + 1])
                    if c0 == 2048:
                        tokmajor_v(hT, b_hT, wi, row0)
                        continue
                    for sbi in range(4):
                        c = c0 + sbi * 128
                        if c < 1536:
                            feat_sub(hT, b_hT, wi, sbi, store_feat(io["QA"][c // 128, :, qsl], AF.Copy, SA))
                        elif c < 2048:
                            feat_sub(hT, b_hT, wi, sbi, store_feat(io["KA"][(c - 1536) // 128, :, wsl], AF.Copy, 1.0))
                        elif c < 3072:
                            feat_sub(hT, b_hT, wi, sbi, store_feat(io["GA"][(c - 2560) // 128, :, qsl], AF.Silu, 1.0))
                        elif c < 3584:
                            def ev(pi, half, hs, sbi=sbi):
                                S.op("act", lambda e: e.activation(cf[:, sbi, hs], pp[pi][:], AF.Copy),
                                     reads=[b_pp[pi]], writes=[b_cf])
                            feat_sub(hT, b_hT, wi, sbi, ev)
                        elif c < 5184:
                            feat_sub(hT, b_hT, wi, sbi, store_feat(io["GB"][(c - 4160) // 128, :, qsl], AF.Silu, 1.0))
                        elif c < 5696:
                            feat_sub(hT, b_hT, wi, sbi, store_feat(io["QC"][(c - 5184) // 128, :, qsl], AF.Copy, SA))
                        else:
                            feat_sub(hT, b_hT, wi, sbi, store_feat(io["GC"][(c - 5696) // 128, :, qsl], AF.Silu, 1.0))
                    if ci == 1 and is_q:
                        rms_feat_common(cf, b_cf, cn, b_cn, sq, b_sq, px, b_px, rstd, b_rstd, 512.0)
                if not is_q:
                    pump.drain()
                    continue
                for h in range(8):
                    si = cs["stg"] % 4; cs["stg"] += 1
                    for half in range(2):
                        hs = slice(half * 512, (half + 1) * 512)
                        pi = cs["pp"] % 4; cs["pp"] += 1
                        for kc in range(4):
                            S.op("pe", lambda e, pi=pi, h=h, kc=kc, hs=hs: e.matmul(
                                pp[pi][:], lhsT=wq[:, kc, h * 128:(h + 1) * 128], rhs=cn[:, kc, hs],
                                start=(kc == 0), stop=(kc == 3)),
                                reads=[b_wq, b_cn], writes=[b_pp[pi]], inc=(kc == 3))
                        S.op("act", lambda e, pi=pi, si=si, hs=hs: e.activation(stg[si][:, hs], pp[pi][:], AF.Copy, scale=SB),
                             reads=[b_pp[pi]], writes=[b_stg[si]])
                    S.dma("pool", io["QN"][h, :, qsl], stg[si][:], reads=[b_stg[si]])
                    si = cs["stg"] % 4; cs["stg"] += 1
                    for half in range(2):
                        hs = slice(half * 512, (half + 1) * 512)
                        for which, col in ((0, 1024 + h * 64), (1, 1536 + h * 64)):
                            for kc in range(4):
                                S.op("pe", lambda e, which=which, col=col, kc=kc, hs=hs: e.matmul(
                                    px[which][0:64, :], lhsT=wq[:, kc, col:col + 64], rhs=cn[:, kc, hs],
                                    start=(kc == 0), stop=(kc == 3)),
                                    reads=[b_wq, b_cn], writes=[b_px[which]], inc=(kc == 3))
                        S.op("dve", lambda e, hs=hs: e.tensor_tensor(t1[:], px[0][0:64, :], rcs[:, 0, hs], ALU.mult),
                             reads=[b_px[0], b_rcs], writes=[b_t1])
                        S.op("dve", lambda e, hs=hs: e.tensor_tensor(t2[:], px[1][0:64, :], rcs[:, 1, hs], ALU.mult),
                             reads=[b_px[1], b_rcs], writes=[b_t2])
                        S.op("pool", lambda e, hs=hs, si=si: e.tensor_tensor(stg[si][0:64, hs], t1[:], t2[:], ALU.add),
                             reads=[b_t1, b_t2], writes=[b_stg[si]])
                    S.dma("pool", io["QR"][h, :, qsl], stg[si][0:64, :], reads=[b_stg[si]])
                    pump.tick(force=True)
                pump.drain()
            S.barrier()

        def dense_attention(T, P, nheads, nch, kn_of, kr_of, v_of, kv_bufs_of, q_src, qr_src, g_src, y_dst, prefetch):
            QT = 512
            NPT = 12
            LA = 3
            qn = [T(f"qn{i}", [128, QT], BF16) for i in range(2)]
            qr = [T(f"qr{i}", [128, QT], BF16) for i in range(2)] if qr_src is not None else None
            gt = [T(f"gt{i}", [128, QT], BF16) for i in range(2)]
            b_q = [[Buf(), Buf(), Buf()], [Buf(), Buf(), Buf()]]
            ptl = [T(f"pt{i}", [128, QT], BF16) for i in range(NPT)]
            b_pt = [Buf() for _ in range(NPT)]
            accD = [T(f"accD{i}", [128, QT], F32) for i in range(2)]
            accP = [T(f"accP{i}", [128, QT], F32) for i in range(2)]
            b_accD, b_accP = [Buf(), Buf()], [Buf(), Buf()]
            rc = [T(f"rc{i}", [128, QT], F32) for i in range(2)]
            of = [T(f"of{i}", [128, QT], F32) for i in range(2)]
            yo = [T(f"yo{i}", [128, QT], BF16) for i in range(2)]
            b_rc, b_of, b_yo = [Buf(), Buf()], [Buf(), Buf()], [Buf(), Buf()]
            st_ = [P(f"st{i}", [128, QT], F32) for i in range(LA + 1)]
            b_st = [Buf() for _ in range(LA + 1)]
            ot = [P(f"ot{i}", [128, QT], F32) for i in range(2)]
            dn = [P(f"dn{i}", [128, QT], F32) for i in range(2)]
            b_ot, b_dn = [Buf(), Buf()], [Buf(), Buf()]
            nqt = Tq // QT
            tiles = [(h, qt) for h in range(nheads) for qt in range(nqt)]
            if qr is not None:
                for i in range(2):
                    S.op("pool", lambda e, i=i: e.memset(qr[i][64:128, :], 0.0), writes=[b_q[i][1]])

            def load_q(idx):
                h, qt = tiles[idx]
                i = idx % 2
                qs = slice(qt * QT, (qt + 1) * QT)
                S.dma("sp", qn[i][:], q_src[h, :, qs], writes=[b_q[i][0]])
                if qr is not None:
                    S.dma("sp", qr[i][0:64, :], qr_src[h, :, qs], writes=[b_q[i][1]])
                S.dma("sp", gt[i][:], g_src[h, :, qs], writes=[b_q[i][2]])

            def finalize(idx, usedP):
                h, qt = tiles[idx]
                i = idx % 2
                qs = slice(qt * QT, (qt + 1) * QT)
                S.op("pe", lambda e: e.matmul(dn[i][:], lhsT=ones_f[:], rhs=accD[i][:], start=True, stop=(not usedP)),
                     reads=[b_1f, b_accD[i]], writes=[b_dn[i]], inc=(not usedP))
                if usedP:
                    S.op("pe", lambda e: e.matmul(dn[i][:], lhsT=ones_f[:], rhs=accP[i][:], start=False, stop=True),
                         reads=[b_1f, b_accP[i]], writes=[b_dn[i]])
                S.op("dve", lambda e: e.reciprocal(rc[i][:], dn[i][:]), reads=[b_dn[i]], writes=[b_rc[i]])
                S.op("dve", lambda e: e.tensor_tensor(of[i][:], ot[i][:], rc[i][:], ALU.mult),
                     reads=[b_ot[i], b_rc[i]], writes=[b_of[i]])
                S.op("pool", lambda e: e.tensor_tensor(yo[i][:], of[i][:], gt[i][:], ALU.mult),
                     reads=[b_of[i], b_q[i][2]], writes=[b_yo[i]])
                S.dma("pool", y_dst(h)[:, qs], yo[i][:], reads=[b_yo[i]])

            cst = [0, 0]
            load_q(0)
            pending_fin = None
            for idx, (h, qt) in enumerate(tiles):
                if qt == 0 and prefetch is not None:
                    prefetch(h)
                i = idx % 2
                kvb = kv_bufs_of(h)

                def qk(j):
                    s = cst[0] % (LA + 1); cst[0] += 1
                    kr = kr_of(h, j) if kr_of is not None else None
                    S.op("pe", lambda e: e.matmul(st_[s][:], lhsT=kn_of(h, j), rhs=qn[i][:], start=True, stop=(kr is None)),
                         reads=kvb + [b_q[i][0]], writes=[b_st[s]], inc=(kr is None))
                    if kr is not None:
                        S.op("pe", lambda e: e.matmul(st_[s][:], lhsT=kr, rhs=qr[i][:], start=False, stop=True),
                             reads=kvb + [b_q[i][1]], writes=[b_st[s]])
                    return s

                pend = []
                for j in range(min(LA, nch)):
                    pend.append(qk(j))
                usedP = False
                for j in range(nch):
                    s = pend.pop(0)
                    p = cst[1] % NPT; cst[1] += 1
                    S.op("act", lambda e, s=s, p=p: e.activation(ptl[p][:], st_[s][:], AF.Exp),
                         reads=[b_st[s]], writes=[b_pt[p]])
                    if j + LA < nch:
                        pend.append(qk(j + LA))
                    S.op("pe", lambda e, p=p, j=j: e.matmul(ot[i][:], lhsT=v_of(h, j), rhs=ptl[p][:], start=(j == 0), stop=(j == nch - 1)),
                         reads=kvb + [b_pt[p]], writes=[b_ot[i]])
                    if j % 2 == 1:
                        if not usedP:
                            S.op("pool", lambda e, p=p: e.tensor_copy(accP[i][:], ptl[p][:]), reads=[b_pt[p]], writes=[b_accP[i]])
                        else:
                            S.op("pool", lambda e, p=p: e.tensor_tensor(accP[i][:], accP[i][:], ptl[p][:], ALU.add),
                                 reads=[b_pt[p]], writes=[b_accP[i]])
                        usedP = True
                    else:
                        if j == 0:
                            S.op("dve", lambda e, p=p: e.tensor_copy(accD[i][:], ptl[p][:]), reads=[b_pt[p]], writes=[b_accD[i]])
                        else:
                            S.op("dve", lambda e, p=p: e.tensor_tensor(accD[i][:], accD[i][:], ptl[p][:], ALU.add),
                                 reads=[b_pt[p]], writes=[b_accD[i]])
                    if j == min(3, nch - 1):
                        if pending_fin is not None:
                            finalize(*pending_fin)
                            pending_fin = None
                        if idx + 1 < len(tiles):
                            load_q(idx + 1)
                pending_fin = (idx, usedP)
            finalize(*pending_fin)

        with contextlib.ExitStack() as st:
            T, P = mk_alloc(st)
            nch = S_ // 128
            kn = [T(f"kn{i}", [128, S_], BF16) for i in range(2)]
            vv = [T(f"vv{i}", [128, nch, 128], BF16) for i in range(2)]
            krt = T("krt", [128, S_], BF16); b_kr = Buf()
            S.op("pool", lambda e: e.memset(krt[64:128, :], 0.0), writes=[b_kr])
            b_kn = [Buf(), Buf()]
            b_vv = [[Buf() for _ in range((nch + 15) // 16)] for _ in range(2)]
            S.dma("sp", krt[0:64, :], io["KR"], writes=[b_kr])

            def load_kv(h):
                i = h % 2
                S.dma("sp", kn[i][:], io["KN"][h], writes=[b_kn[i]])
                vsrc = io["VB"].rearrange("(j p) c -> p j c", p=128)
                for j0 in range(0, nch, 16):
                    S.dma("sp", vv[i][:, j0:j0 + 16, :], vsrc[:, j0:j0 + 16, h * 128:(h + 1) * 128], writes=[b_vv[i][j0 // 16]])

            load_kv(0)

            def prefetch(h):
                if h + 1 < 8:
                    load_kv(h + 1)

            dense_attention(
                T, P, 8, nch,
                kn_of=lambda h, j: kn[h % 2][:, j * 128:(j + 1) * 128],
                kr_of=lambda h, j: krt[:, j * 128:(j + 1) * 128],
                v_of=lambda h, j: vv[h % 2][:, j, :],
                kv_bufs_of=lambda h: [b_kn[h % 2], b_kr] + b_vv[h % 2],
                q_src=io["QN"], qr_src=io["QR"], g_src=io["GB"],
                y_dst=lambda h: io["YT"][4 + h], prefetch=prefetch)
            S.barrier()

        with contextlib.ExitStack() as st:
            T, P = mk_alloc(st)
            dense_attention(
                T, P, 4, 2,
                kn_of=lambda h, j: mkT[:, h, j * 128:(j + 1) * 128],
                kr_of=None,
                v_of=lambda h, j: mv[:, j, h * 128:(h + 1) * 128],
                kv_bufs_of=lambda h: [b_mk, b_mv],
                q_src=io["QC"], qr_src=None, g_src=io["GC"],
                y_dst=lambda h: io["YT"][12 + h], prefetch=None)
            S.barrier()

        with contextlib.ExitStack() as st:
            T, P = mk_alloc(st)
            bias = T("bias", [128, 48, 128], F32); b_bias = Buf()
            S.dma("sp", bias[:], io["bias"], writes=[b_bias])
            ka = [T(f"ka{i}", [128, Tw], BF16) for i in range(2)]
            b_ka = [Buf(), Buf()]
            qa2 = [[T(f"qa{j}_{i}", [128, Tq], BF16) for i in range(3)] for j in range(2)]
            b_qa2 = [[Buf() for _ in range(3)] for _ in range(2)]
            ga2 = [T(f"ga{j}", [128, Tq], BF16) for j in range(2)]
            b_ga2 = [Buf(), Buf()]

            def load_head(hh):
                j = hh % 2
                if hh > 0:
                    S.dma("sp", ka[j][:], io["KA"][hh], writes=[b_ka[j]])
                for g_ in range(3):
                    S.dma("sp", qa2[j][g_][:], io["QA"][g_ * 4 + hh], writes=[b_qa2[j][g_]])
                S.dma("sp", ga2[j][:], io["GA"][hh], writes=[b_ga2[j]])
            maxch = Tq // 128 + 1
            NV = 3
            vr = [T(f"vr{i}", [128, maxch, 128], BF16) for i in range(NV)]
            b_vr = [[Buf(), Buf()] for _ in range(NV)]
            accO = T("accO", [128, Tq], F32); b_aO = Buf()
            accD = T("accD", [128, Tq], F32); b_aD = Buf()
            sA = [T(f"sA{i}", [128, 2, 512], F32) for i in range(2)]
            pA = [T(f"pA{i}", [128, 2, 512], BF16) for i in range(2)]
            b_sA, b_pA = [[Buf(), Buf()], [Buf(), Buf()]], [[Buf(), Buf()], [Buf(), Buf()]]
            yo = T("yo3", [128, Tq], BF16); b_yo = Buf()
            stp = [[P(f"stp{i}_{c}", [128, 512], F32) for c in range(2)] for i in range(2)]
            otp = [P(f"otp{i}", [128, 512], F32) for i in range(2)]
            dnp = [P(f"dnp{i}", [128, 512], F32) for i in range(2)]
            b_stp, b_otp, b_dnp = [[Buf(), Buf()], [Buf(), Buf()]], [Buf(), Buf()], [Buf(), Buf()]

            def bc(ap2, nb):
                return bass.AP(ap2.tensor, ap2.offset, [list(ap2.ap[0]), [0, nb], list(ap2.ap[1])])

            va_t = io["VA"].tensor
            S.dma("sp", ka[0][:], io["KA"][0], writes=[b_ka[0]])
            vctr = [0]
            bctr = [0]
            for h in range(4):
                kh = ka[h % 2]
                b_kh = b_ka[h % 2]
                if h == 0:
                    load_head(0)
                if h + 1 < 4:
                    load_head(h + 1)
                qa, b_qa = qa2[h % 2], b_qa2[h % 2]
                ga, b_ga = ga2[h % 2], b_ga2[h % 2]
                res = []
                for g, d in enumerate(DILS):
                    nl = Tq // d
                    nq = min(128, nl)
                    nun = nl // nq
                    for r in range(d):
                        res.append(dict(g=g, d=d, r=r, nq=nq, nun=nun, lq0=HALO // d))

                def load_v(ri):
                    R_ = res[ri]
                    vi = vctr[0] % NV; vctr[0] += 1
                    R_["vi"] = vi
                    d, r, nq, nun, lq0 = R_["d"], R_["r"], R_["nq"], R_["nun"], R_["lq0"]
                    nchk = nun + 1
                    base = ((lq0 - 64) * d + r) * 512 + h * 128
                    if nchk > 1:
                        S.dma("sp", vr[vi][:, 0:nchk - 1, :],
                              bass.AP(va_t, base, [[d * 512, 128], [128 * d * 512, nchk - 1], [1, 128]]),
                              writes=[b_vr[vi][0]])
                    S.dma("sp", vr[vi][0:nq, nchk - 1, :],
                          bass.AP(va_t, base + (nchk - 1) * 128 * d * 512, [[d * 512, nq], [1, 128]]),
                          writes=[b_vr[vi][1]])

                batches = []
                for ri, R_ in enumerate(res):
                    for u0 in range(0, R_["nun"], 4):
                        batches.append(dict(ri=ri, u0=u0, nb=min(4, R_["nun"] - u0)))

                def qk(B):
                    R_ = res[B["ri"]]
                    g, d, r, nq, nun, lq0 = R_["g"], R_["d"], R_["r"], R_["nq"], R_["nun"], R_["lq0"]
                    k = bctr[0] % 2; bctr[0] += 1
                    B["k"] = k
                    for b in range(B["nb"]):
                        u = B["u0"] + b
                        l0 = lq0 + nq * u
                        qstart = l0 * d + r - HALO
                        qsl = slice(qstart, qstart + d * (nq - 1) + 1, d)
                        kA0 = (l0 - 64) * d + r
                        kB0 = (l0 + 64) * d + r
                        kAs = slice(kA0, kA0 + d * 127 + 1, d)
                        kBs = slice(kB0, kB0 + d * (nq - 1) + 1, d)
                        last = (b == B["nb"] - 1)
                        S.op("pe", lambda e: e.matmul(stp[k][0][:, b * 128:b * 128 + nq], lhsT=kh[:, kAs], rhs=qa[g][:, qsl], start=True, stop=True),
                             reads=[b_kh, b_qa[g]], writes=[b_stp[k][0]], inc=last)
                        S.op("pe", lambda e: e.matmul(stp[k][1][0:nq, b * 128:b * 128 + nq], lhsT=kh[:, kBs], rhs=qa[g][:, qsl], start=True, stop=True),
                             reads=[b_kh, b_qa[g]], writes=[b_stp[k][1]], inc=last)

                def softmax_part(B):
                    R_ = res[B["ri"]]
                    g, nq, nun = R_["g"], R_["nq"], R_["nun"]
                    k, nb, u0 = B["k"], B["nb"], B["u0"]
                    gh = g * 4 + h
                    W = nb * 128 if nq == 128 else nq

                    def v3(ap2):
                        return ap2.rearrange("p (b q) -> p b q", q=128) if nq == 128 else ap2

                    for c, (var_n, var_e, rows) in enumerate(((0, 1, slice(0, 128)), (2, 3, slice(0, nq)))):
                        bias_n = bias[rows, gh * 4 + var_n, 0:nq]
                        src = stp[k][c][rows, 0:W]
                        dst = sA[k][rows, c, 0:W]
                        if nq == 128:
                            S.op("dve", lambda e: e.tensor_tensor(v3(dst), v3(src), bc(bias_n, nb), ALU.add),
                                 reads=[b_stp[k][c], b_bias], writes=[b_sA[k][c]])
                        else:
                            S.op("dve", lambda e: e.tensor_tensor(dst, src, bias_n, ALU.add),
                                 reads=[b_stp[k][c], b_bias], writes=[b_sA[k][c]])
                        if c == 0 and u0 == 0:
                            S.op("dve", lambda e: e.tensor_tensor(sA[k][rows, c, 0:nq], stp[k][c][rows, 0:nq], bias[rows, gh * 4 + var_e, 0:nq], ALU.add),
                                 reads=[b_stp[k][c], b_bias], writes=[b_sA[k][c]])
                        if c == 1 and u0 + nb == nun:
                            o = (nb - 1) * 128
                            S.op("dve", lambda e: e.tensor_tensor(sA[k][rows, c, o:o + nq], stp[k][c][rows, o:o + nq], bias[rows, gh * 4 + var_e, 0:nq], ALU.add),
                                 reads=[b_stp[k][c], b_bias], writes=[b_sA[k][c]])
                        S.op("act", lambda e: e.activation(pA[k][rows, c, 0:W], sA[k][rows, c, 0:W], AF.Exp),
                             reads=[b_sA[k][c]], writes=[b_pA[k][c]])

                def pv(B):
                    R_ = res[B["ri"]]
                    g, d, r, nq, nun, lq0, vi = R_["g"], R_["d"], R_["r"], R_["nq"], R_["nun"], R_["lq0"], R_["vi"]
                    k, nb, u0 = B["k"], B["nb"], B["u0"]
                    W = nb * 128 if nq == 128 else nq
                    for b in range(nb):
                        u = u0 + b
                        cs_ = slice(b * 128, b * 128 + nq)
                        S.op("pe", lambda e: e.matmul(otp[k][:, cs_], lhsT=vr[vi][:, u, :], rhs=pA[k][:, 0, cs_], start=True, stop=False),
                             reads=b_vr[vi] + [b_pA[k][0]], writes=[b_otp[k]], inc=False)
                        S.op("pe", lambda e: e.matmul(otp[k][:, cs_], lhsT=vr[vi][0:nq, u + 1, :], rhs=pA[k][0:nq, 1, cs_], start=False, stop=True),
                             reads=b_vr[vi] + [b_pA[k][1]], writes=[b_otp[k]], inc=(b == nb - 1))
                    for b in range(nb):
                        cs_ = slice(b * 128, b * 128 + nq)
                        S.op("pe", lambda e: e.matmul(dnp[k][:, cs_], lhsT=ones_b[:], rhs=pA[k][:, 0, cs_], start=True, stop=False),
                             reads=[b_1b, b_pA[k][0]], writes=[b_dnp[k]], inc=False)
                        S.op("pe", lambda e: e.matmul(dnp[k][:, cs_], lhsT=ones_b[0:nq, :], rhs=pA[k][0:nq, 1, cs_], start=False, stop=True),
                             reads=[b_1b, b_pA[k][1]], writes=[b_dnp[k]], inc=(b == nb - 1))
                    l0 = lq0 + nq * u0
                    qstart = l0 * d + r - HALO
                    nqt = nb * nq
                    qsl = slice(qstart, qstart + d * (nqt - 1) + 1, d)
                    if g == 0:
                        S.op("dve", lambda e: e.tensor_copy(accO[:, qsl], otp[k][:, 0:W]), reads=[b_otp[k]], writes=[b_aO])
                        S.op("dve", lambda e: e.tensor_copy(accD[:, qsl], dnp[k][:, 0:W]), reads=[b_dnp[k]], writes=[b_aD])
                    else:
                        S.op("dve", lambda e: e.tensor_tensor(accO[:, qsl], accO[:, qsl], otp[k][:, 0:W], ALU.add),
                             reads=[b_otp[k]], writes=[b_aO])
                        S.op("dve", lambda e: e.tensor_tensor(accD[:, qsl], accD[:, qsl], dnp[k][:, 0:W], ALU.add),
                             reads=[b_dnp[k]], writes=[b_aD])

                load_v(0)
                if len(res) > 1:
                    load_v(1)
                qk(batches[0])
                for bi, B in enumerate(batches):
                    if B["u0"] == 0 and B["ri"] + 2 < len(res):
                        load_v(B["ri"] + 2)
                    softmax_part(B)
                    if bi + 1 < len(batches):
                        qk(batches[bi + 1])
                    pv(B)
                S.op("dve", lambda e: e.reciprocal(accD[:], accD[:]), reads=[], writes=[b_aD])
                S.op("dve", lambda e: e.tensor_tensor(accO[:], accO[:], accD[:], ALU.mult), reads=[b_aD], writes=[b_aO])
                S.op("pool", lambda e: e.tensor_tensor(yo[:], accO[:], ga[:], ALU.mult), reads=[b_aO, b_ga], writes=[b_yo])
                S.dma("pool", io["YT"][h], yo[:], reads=[b_yo])
            S.barrier()

        with contextlib.ExitStack() as st:
            T, P = mk_alloc(st)
            wo = T("wo", [128, 16, D], BF16); b_wo = [Buf() for _ in range(4)]
            pg = T("pg", [128, D], F32); b_pg = Buf()
            for cb in range(4):
                S.dma("sp", wo[:, :, cb * 512:(cb + 1) * 512], WO[:, cb * 512:(cb + 1) * 512].rearrange("(kc p) c -> p kc c", p=128), writes=[b_wo[cb]])
            S.dma("sp", pg[:], G["pgain"], writes=[b_pg])
            yt = [T(f"yt{i}", [128, 16, 512], BF16) for i in range(2)]
            b_yt = [Buf(), Buf()]
            xt = [T(f"xt{i}", [128, D], F32) for i in range(2)]
            b_xt = [Buf(), Buf()]
            yf = [T(f"yf{i}", [128, D], F32) for i in range(2)]
            b_yf = [Buf(), Buf()]
            jk = T("jk", [128, D], BF16)
            ss = [T(f"ssc{i}", [128, 1], F32) for i in range(2)]
            rs = [T(f"rsc{i}", [128, 1], F32) for i in range(2)]
            b_ss, b_rs = [Buf(), Buf()], [Buf(), Buf()]
            ob = [T(f"ob{i}", [128, D], F32) for i in range(2)]
            b_ob = [Buf(), Buf()]
            po = [P(f"po{i}", [128, D], F32) for i in range(2)]
            b_po = [Buf(), Buf()]
            ntile = Tq // 128

            def load_y(blk):
                i = blk % 2
                S.dma("sp", yt[i][:], io["YT"][:, :, blk * 512:(blk + 1) * 512].rearrange("c p t -> p c t"), writes=[b_yt[i]])

            load_y(0)
            for tl in range(ntile):
                blk, sub = tl // 4, tl % 4
                if sub == 0 and blk + 1 < Tq // 512:
                    load_y(blk + 1)
                yi = blk % 2
                i = tl % 2
                S.dma("sp", xt[i][:], io["xw"][HALO + tl * 128: HALO + (tl + 1) * 128, :], writes=[b_xt[i]])
                for cb in range(4):
                    for kc in range(16):
                        S.op("pe", lambda e, i=i, yi=yi, cb=cb, kc=kc, sub=sub: e.matmul(
                            po[i][:, cb * 512:(cb + 1) * 512], lhsT=yt[yi][:, kc, sub * 128:(sub + 1) * 128],
                            rhs=wo[:, kc, cb * 512:(cb + 1) * 512], start=(kc == 0), stop=(kc == 15)),
                            reads=[b_yt[yi], b_wo[cb]], writes=[b_po[i]], inc=(kc == 15 and cb == 3))
                for cb in range(4):
                    S.op("act", lambda e, i=i, cb=cb: e.activation(yf[i][:, cb * 512:(cb + 1) * 512], po[i][:, cb * 512:(cb + 1) * 512], AF.Copy),
                         reads=[b_po[i]], writes=[b_yf[i]])
                S.op("act", lambda e, i=i: e.activation(jk[:], yf[i][:], AF.Square, scale=RS, accum_out=ss[i][:]),
                     reads=[b_yf[i]], writes=[b_ss[i]])
                S.op("act", lambda e, i=i: e.activation(rs[i][:], ss[i][:], AF.Ln, bias=epsb[:]),
                     reads=[b_ss[i], b_eps], writes=[b_rs[i]])
                S.op("act", lambda e, i=i: e.activation(rs[i][:], rs[i][:], AF.Exp, scale=-0.5),
                     reads=[b_rs[i]], writes=[b_rs[i]])
                S.op("dve", lambda e, i=i: e.scalar_tensor_tensor(ob[i][:], yf[i][:], rs[i][:], pg[:], ALU.mult, ALU.mult),
                     reads=[b_yf[i], b_rs[i], b_pg], writes=[b_ob[i]])
                S.op("pool", lambda e, i=i: e.tensor_tensor(ob[i][:], ob[i][:], xt[i][:], ALU.add),
                     reads=[b_xt[i]], writes=[b_ob[i]])
                S.dma("pool", io["y"][tl * 128:(tl + 1) * 128, :], ob[i][:], reads=[b_ob[i]])
            S.barrier()


def _t5_bucket(rel):
    nb = 16
    max_exact = 8
    bucket = np.where(rel > 0, nb, 0)
    n = np.abs(rel)
    nf = np.maximum(n, 1).astype(np.float32)
    large = max_exact + (np.log(nf / np.float32(max_exact)) / np.float32(math.log(1024 / max_exact))
                         * np.float32(nb - max_exact)).astype(np.int32)
    large = np.minimum(large, nb - 1)
    return bucket + np.where(n < max_exact, n, large)


def _bias_tiles(rel_bias, Tq, at_start, at_end):
    out = np.empty((128, 48, 128), np.float32)
    kk = np.arange(128)[:, None]
    qq = np.arange(128)[None, :]
    for g, d in enumerate(DILS):
        nq = min(128, Tq // d)
        for var in range(4):
            rel = (kk - 64 - qq) if var < 2 else (kk + 64 - qq)
            valid = np.abs(rel) <= 64
            if var == 1 and at_start:
                valid = valid & (kk >= 64)
            if var == 3 and at_end:
                valid = valid & (kk < nq - 64)
            bidx = _t5_bucket(rel * d)
            for h in range(4):
                gh = g * 4 + h
                vals = rel_bias[bidx, gh]
                out[:, gh * 4 + var, :] = np.where(valid, vals, np.float32(NEG))
    return out


def _rope_tables(pos, scale):
    inv = (10000.0 ** (-np.arange(0, 64, 2, dtype=np.float32) / np.float32(64))).astype(np.float32)
    ang = (pos.astype(np.float32)[None, :] * inv[:, None]).astype(np.float32)
    c = np.cos(ang).astype(np.float32)
    s = np.sin(ang).astype(np.float32)
    cos = np.concatenate([c, c], 0)
    sin = np.concatenate([-s, s], 0)
    return (np.stack([cos, sin], 0) * np.float32(scale)).astype(np.float32)


def make_in_maps(inputs, jobs=JOBS, ncores=NCORES):
    f = lambda a: np.ascontiguousarray(np.asarray(a, dtype=np.float32))
    gains = np.concatenate([
        f(inputs["pre_gain"]).reshape(16, 128).T, f(inputs["q_gain"]).reshape(4, 128).T,
        f(inputs["kv_gain"]).reshape(4, 128).T, f(inputs["mem_gain"]).reshape(16, 128).T], axis=1)
    shared = dict(
        w_in=f(inputs["w_in"][0]), w_uq=f(inputs["w_uq"][0]), w_ukv=f(inputs["w_ukv"][0]),
        w_mem_kv=f(inputs["w_mem_kv"][0]), w_out=f(inputs["w_out"][0]),
        gains=np.ascontiguousarray(gains), pgain=np.ascontiguousarray(np.broadcast_to(f(inputs["post_gain"]).reshape(1, D), (128, D))),
        ident=np.eye(128, dtype=np.float32))
    rel_bias = f(inputs["rel_bias"])
    xs = dict(p=f(inputs["x_prompt"]), s=f(inputs["x_sample"]))
    mems = dict(p=f(inputs["mem_prompt"]), s=f(inputs["mem_sample"]))
    maps = []
    for c in range(ncores):
        m = dict(shared)
        for jb in jobs:
            n, S_, Tq, per = jb["name"], jb["S"], jb["Tq"], jb["per_seq"]
            b = c // per
            q0 = (c % per) * Tq
            x = xs[n][b]
            m[f"xkv_{n}"] = np.roll(x, -(q0 - HALO), axis=0)
            m[f"mem_{n}"] = mems[n][b]
            m[f"ropeq_{n}"] = _rope_tables(np.arange(q0, q0 + Tq), 192.0 ** -0.5)
            m[f"ropek_{n}"] = _rope_tables((np.arange(S_) + q0 - HALO) % S_, 1.0)
            m[f"bias_{n}"] = _bias_tiles(rel_bias, Tq, q0 == 0, q0 + Tq == S_)
        maps.append(m)
    return maps


_NC_CACHE = {}


def kernel(**inputs):
    jobs = JOBS
    if "nc" not in _NC_CACHE:
        _NC_CACHE["nc"] = build_program(jobs)
    nc = _NC_CACHE["nc"]
    maps = make_in_maps(inputs, jobs)
    res = run_bass_kernel_spmd(nc, maps, core_ids=list(range(NCORES)))
    outs = []
    for jb, key in zip(jobs, ("x_prompt", "x_sample")):
        n, Tq, per = jb["name"], jb["Tq"], jb["per_seq"]
        shp = np.asarray(inputs[key]).shape
        y = np.empty(shp, np.float32)
        for c in range(NCORES):
            b = c // per
            q0 = (c % per) * Tq
            y[b, q0:q0 + Tq] = res.results[c][f"y_{n}"]
        outs.append(y)
    return tuple(outs)
```
